# Optimizing a Trainium2 kernel written in Bass

```python
import jax, jax.numpy as jnp
from jax import lax
import numpy as np

D_MODEL = 1024
BATCH = 8
SEQ = 4096
DEPTH = 1

MEM_LEN = 256
NORM_EPS = 1e-6
HALF_STEP = 0.5
D_FF = 2816

RWKV_HEADS = 8
RWKV_HEAD_DIM = 64
RWKV_WIDTH = RWKV_HEADS * RWKV_HEAD_DIM
DECAY_LORA = 64
AAA_LORA = 64
GATE_LORA = 128
RWKV_GN_EPS = 64e-5
RWKV_SIZES = (RWKV_WIDTH, RWKV_WIDTH, RWKV_WIDTH, DECAY_LORA, AAA_LORA, GATE_LORA)
RWKV_COLS = 3 * RWKV_WIDTH + DECAY_LORA + AAA_LORA + GATE_LORA

SWA_Q_HEADS = 8
SWA_KV_HEADS = 2
SWA_GROUP = SWA_Q_HEADS // SWA_KV_HEADS
SWA_HEAD_DIM = 64
SWA_SIZES = (SWA_Q_HEADS * SWA_HEAD_DIM, SWA_KV_HEADS * SWA_HEAD_DIM, SWA_KV_HEADS * SWA_HEAD_DIM)
SWA_COLS = SWA_Q_HEADS * SWA_HEAD_DIM + 2 * SWA_KV_HEADS * SWA_HEAD_DIM
WINDOW = 128
BLOCK = 128
ROPE_THETA = 500000.0
ROPE_DIM = SWA_HEAD_DIM // 4

MEM_HEADS = 4
MEM_HEAD_DIM = 128
MEM_WIDTH = MEM_HEADS * MEM_HEAD_DIM

N_BRANCH = 3
BRANCH_WIDTH = 512
IN_SIZES = (RWKV_COLS, SWA_COLS, MEM_WIDTH, N_BRANCH * D_MODEL)
IN_COLS = RWKV_COLS + SWA_COLS + MEM_WIDTH + N_BRANCH * D_MODEL
NEG_INF = -1e30

kernel_name = 'hybrid_rwkv7_swa_sink_memxattn_macaron'


def _split(t, sizes):
    out, start = [], 0
    for s in sizes:
        out.append(t[..., start:start + s])
        start += s
    return out


def rms_norm(x, g):
    xf = x.astype(jnp.float32)
    y = xf * lax.rsqrt(jnp.mean(xf * xf, axis=-1, keepdims=True) + NORM_EPS)
    return (y * g.astype(jnp.float32)).astype(x.dtype)


def swiglu(u, w_in, w_out):
    gate, up = jnp.split(u @ w_in, 2, axis=-1)
    return (jax.nn.silu(gate) * up) @ w_out


def rope_tables(positions):
    inv_freq = ROPE_THETA ** (-jnp.arange(0, ROPE_DIM, 2, dtype=jnp.float32) / ROPE_DIM)
    ang = positions.astype(jnp.float32)[..., None] * inv_freq
    return jnp.cos(ang)[:, :, None, :], jnp.sin(ang)[:, :, None, :]


def partial_rope(t, cos, sin):
    half = ROPE_DIM // 2
    tf = t[..., :ROPE_DIM].astype(jnp.float32)
    x1, x2 = tf[..., :half], tf[..., half:]
    rot = jnp.concatenate([x1 * cos - x2 * sin, x2 * cos + x1 * sin], axis=-1)
    return jnp.concatenate([rot.astype(t.dtype), t[..., ROPE_DIM:]], axis=-1)


def rwkv7_recurrence(r, decay, k, v, a, b):
    def step(state, inp):
        r_t, w_t, k_t, v_t, a_t, b_t = inp
        sa = jnp.einsum('bhvk,bhk->bhv', state, a_t)
        state = (state * w_t[:, :, None, :] + sa[..., None] * b_t[:, :, None, :]
                 + v_t[..., None] * k_t[:, :, None, :])
        return state, jnp.einsum('bhvk,bhk->bhv', state, r_t)

    xs = (jnp.swapaxes(r, 0, 1), jnp.swapaxes(decay, 0, 1), jnp.swapaxes(k, 0, 1),
          jnp.swapaxes(v, 0, 1), jnp.swapaxes(a, 0, 1), jnp.swapaxes(b, 0, 1))
    B, _, H, N = r.shape
    s0 = jnp.zeros((B, H, N, N), jnp.float32)
    _, y = lax.scan(step, s0, xs)
    return jnp.swapaxes(y, 0, 1)


def rwkv7_mix(p, mu, w0, w_up, a0, a_up, g_up, k_k, k_a, r_k, gn_g, gn_b):
    f32 = jnp.float32
    B, S, _ = p.shape
    H, N = RWKV_HEADS, RWKV_HEAD_DIM
    p_prev = jnp.pad(p, ((0, 0), (1, 0), (0, 0)))[:, :-1]
    p = p + (p_prev - p) * mu
    r, k, v, wd, ad, gd = _split(p, RWKV_SIZES)
    w_log = -jax.nn.softplus(-(w0 + jnp.tanh(wd) @ w_up).astype(f32)) - 0.5
    decay = jnp.exp(-jnp.exp(w_log))
    a = jax.nn.sigmoid((a0 + ad @ a_up).astype(f32))
    g = (jax.nn.sigmoid(gd) @ g_up).astype(f32)
    r = r.astype(f32)
    k = k.astype(f32)
    v = v.astype(f32)
    kk = (k * k_k.astype(f32)).reshape(B, S, H, N)
    kk = kk / jnp.maximum(jnp.sqrt(jnp.sum(kk * kk, axis=-1, keepdims=True)), 1e-12)
    k = k * (1.0 + (a - 1.0) * k_a.astype(f32))
    r = r.reshape(B, S, H, N)
    k = k.reshape(B, S, H, N)
    v = v.reshape(B, S, H, N)
    a = a.reshape(B, S, H, N)
    decay = decay.reshape(B, S, H, N)
    y = rwkv7_recurrence(r, decay, k, v, -kk, kk * a)
    mean = jnp.mean(y, axis=-1, keepdims=True)
    var = jnp.mean(jnp.square(y - mean), axis=-1, keepdims=True)
    y = ((y - mean) * lax.rsqrt(var + RWKV_GN_EPS) * gn_g.astype(f32).reshape(H, N)
         + gn_b.astype(f32).reshape(H, N))
    y = y + jnp.sum(r * k * r_k.astype(f32), axis=-1, keepdims=True) * v
    return (y.reshape(B, S, RWKV_WIDTH) * g).astype(p.dtype)


def sliding_window_gqa_sinks(q, k, v, sinks):
    f32 = jnp.float32
    B, S = q.shape[0], q.shape[1]
    nb = S // BLOCK
    scale = SWA_HEAD_DIM ** -0.5
    qb = q.astype(f32).reshape(B, nb, BLOCK, SWA_KV_HEADS, SWA_GROUP, SWA_HEAD_DIM) * scale

    def banded(t):
        tb = t.astype(f32).reshape(B, nb, BLOCK, SWA_KV_HEADS, SWA_HEAD_DIM)
        prev = jnp.pad(tb, ((0, 0), (1, 0), (0, 0), (0, 0), (0, 0)))[:, :-1]
        return jnp.concatenate([prev, tb], axis=2)

    kw, vw = banded(k), banded(v)
    s = jnp.einsum('bnqhgd,bnkhd->bnhgqk', qb, kw)
    qi = jnp.arange(BLOCK)
    kj = jnp.arange(2 * BLOCK)
    blk = jnp.arange(nb)
    diff = qi[:, None] + BLOCK - kj[None, :]
    valid = (diff >= 0) & (diff < WINDOW)
    key_pos = blk[:, None] * BLOCK - BLOCK + kj[None, :]
    valid = valid[None, :, :] & (key_pos >= 0)[:, None, :]
    s = jnp.where(valid[None, :, None, None], s, NEG_INF)
    sink = sinks.astype(f32).reshape(SWA_KV_HEADS, SWA_GROUP)[None, None, :, :, None, None]
    sink = jnp.broadcast_to(sink, s.shape[:-1] + (1,))
    pr = jax.nn.softmax(jnp.concatenate([s, sink], axis=-1), axis=-1)[..., :-1]
    o = jnp.einsum('bnhgqk,bnkhd->bnqhgd', pr, vw)
    return o.reshape(B, S, SWA_Q_HEADS * SWA_HEAD_DIM).astype(q.dtype)


def memory_cross_attention(q, mem_n, w_kv):
    f32 = jnp.float32
    B, S = q.shape[0], q.shape[1]
    km, vm = jnp.split(mem_n @ w_kv, 2, axis=-1)
    km = km.reshape(B, -1, MEM_HEADS, MEM_HEAD_DIM).astype(f32)
    vm = vm.reshape(B, -1, MEM_HEADS, MEM_HEAD_DIM).astype(f32)
    s = jnp.einsum('bshd,bmhd->bhsm', q.astype(f32) * MEM_HEAD_DIM ** -0.5, km)
    pr = jax.nn.softmax(s, axis=-1)
    o = jnp.einsum('bhsm,bmhd->bshd', pr, vm)
    return o.reshape(B, S, MEM_WIDTH).astype(q.dtype)


def setup_inputs(seed: int = 0) -> dict:
    key = jax.random.key(seed)
    ks = iter(jax.random.split(key, 40))
    f32 = jnp.float32
    L, D = DEPTH, D_MODEL

    def nrm(shape, scale):
        return scale * jax.random.normal(next(ks), shape, f32)

    def gain(n):
        return 1.0 + 0.05 * jax.random.normal(next(ks), (L, n), f32)

    x = jax.random.normal(next(ks), (BATCH, SEQ, D), f32)
    mem = jax.random.normal(next(ks), (BATCH, MEM_LEN, D), f32)
    start = jax.random.randint(next(ks), (BATCH, 1), 0, 1024, dtype=jnp.int32)
    positions = start + jnp.arange(SEQ, dtype=jnp.int32)[None, :]
    return {
        'x': x,
        'mem': mem,
        'positions': positions,
        'ffn1_pre_g': gain(D),
        'ffn1_w_in': nrm((L, D, 2 * D_FF), D ** -0.5),
        'ffn1_w_out': nrm((L, D_FF, D), D_FF ** -0.5),
        'ffn1_post_g': gain(D),
        'mix_pre_g': gain(D),
        'w_in': nrm((L, D, IN_COLS), D ** -0.5),
        'gate_b': nrm((L, N_BRANCH * D), 0.1),
        'rwkv_mu': jax.random.uniform(next(ks), (L, RWKV_COLS), f32),
        'rwkv_w0': jax.random.uniform(next(ks), (L, RWKV_WIDTH), f32, -3.0, 3.0),
        'rwkv_w_up': nrm((L, DECAY_LORA, RWKV_WIDTH), 0.1),
        'rwkv_a0': nrm((L, RWKV_WIDTH), 0.5),
        'rwkv_a_up': nrm((L, AAA_LORA, RWKV_WIDTH), 0.1),
        'rwkv_g_up': nrm((L, GATE_LORA, RWKV_WIDTH), GATE_LORA ** -0.5),
        'rwkv_k_k': 0.85 + nrm((L, RWKV_WIDTH), 0.05),
        'rwkv_k_a': 1.0 + nrm((L, RWKV_WIDTH), 0.05),
        'rwkv_r_k': nrm((L, RWKV_HEADS, RWKV_HEAD_DIM), 0.1),
        'rwkv_gn_g': gain(RWKV_WIDTH),
        'rwkv_gn_b': nrm((L, RWKV_WIDTH), 0.01),
        'swa_sinks': nrm((L, SWA_Q_HEADS), 1.0),
        'mem_norm_g': gain(D),
        'mem_w_kv': nrm((L, D, 2 * MEM_WIDTH), D ** -0.5),
        'w_branch': nrm((L, N_BRANCH, BRANCH_WIDTH, D), BRANCH_WIDTH ** -0.5),
        'w_out': nrm((L, D, D), D ** -0.5),
        'mix_post_g': gain(D),
        'ffn2_pre_g': gain(D),
        'ffn2_w_in': nrm((L, D, 2 * D_FF), D ** -0.5),
        'ffn2_w_out': nrm((L, D_FF, D), D_FF ** -0.5),
        'ffn2_post_g': gain(D),
    }


def reference(x, mem, positions, ffn1_pre_g, ffn1_w_in, ffn1_w_out, ffn1_post_g,
              mix_pre_g, w_in, gate_b, rwkv_mu, rwkv_w0, rwkv_w_up, rwkv_a0, rwkv_a_up,
              rwkv_g_up, rwkv_k_k, rwkv_k_a, rwkv_r_k, rwkv_gn_g, rwkv_gn_b, swa_sinks,
              mem_norm_g, mem_w_kv, w_branch, w_out, mix_post_g,
              ffn2_pre_g, ffn2_w_in, ffn2_w_out, ffn2_post_g):
    B, S, D = x.shape
    cos, sin = rope_tables(positions)
    h = x
    for l in range(DEPTH):
        f1 = swiglu(rms_norm(h, ffn1_pre_g[l]), ffn1_w_in[l], ffn1_w_out[l])
        h = h + HALF_STEP * rms_norm(f1, ffn1_post_g[l])

        u = rms_norm(h, mix_pre_g[l])
        p_rwkv, p_swa, q_mem, gate_logits = _split(u @ w_in[l], IN_SIZES)

        y_a = rwkv7_mix(p_rwkv, rwkv_mu[l], rwkv_w0[l], rwkv_w_up[l], rwkv_a0[l],
                        rwkv_a_up[l], rwkv_g_up[l], rwkv_k_k[l], rwkv_k_a[l], rwkv_r_k[l],
                        rwkv_gn_g[l], rwkv_gn_b[l])

        q_s, k_s, v_s = _split(p_swa, SWA_SIZES)
        q_s = partial_rope(q_s.reshape(B, S, SWA_Q_HEADS, SWA_HEAD_DIM), cos, sin)
        k_s = partial_rope(k_s.reshape(B, S, SWA_KV_HEADS, SWA_HEAD_DIM), cos, sin)
        v_s = v_s.reshape(B, S, SWA_KV_HEADS, SWA_HEAD_DIM)
        y_b = sliding_window_gqa_sinks(q_s, k_s, v_s, swa_sinks[l])

        mem_n = rms_norm(mem, mem_norm_g[l])
        y_c = memory_cross_attention(q_mem.reshape(B, S, MEM_HEADS, MEM_HEAD_DIM),
                                     mem_n, mem_w_kv[l])

        g_a, g_b, g_c = _split(jax.nn.sigmoid(gate_logits + gate_b[l]), (D, D, D))
        merged = (g_a * (y_a @ w_branch[l, 0]) + g_b * (y_b @ w_branch[l, 1])
                  + g_c * (y_c @ w_branch[l, 2]))
        h = h + rms_norm(merged @ w_out[l], mix_post_g[l])

        f2 = swiglu(rms_norm(h, ffn2_pre_g[l]), ffn2_w_in[l], ffn2_w_out[l])
        h = h + HALF_STEP * rms_norm(f2, ffn2_post_g[l])
    return h
```

```python
import numpy as np
from contextlib import ExitStack
import concourse.bass as bass
import concourse.mybir as mybir
from concourse.bass_utils import run_bass_kernel_spmd

F32 = mybir.dt.float32
BF16 = mybir.dt.bfloat16
I32 = mybir.dt.int32
ALU = mybir.AluOpType
AF = mybir.ActivationFunctionType
AX = mybir.AxisListType

S = 4096
D = 1024
DFF = 2816
NFF = DFF // 128
EPS = 1e-6


class Buf:
    __slots__ = ("name", "ws", "readers", "sem", "excl", "prev")

    def __init__(self, name, excl=False):
        self.name = name
        self.excl = excl
        self.ws = set()
        self.readers = set()
        self.prev = set()
        self.sem = None


class Prog:
    ENGS = ("pe", "act", "dve", "pool", "sp")
    CE = ("pe", "act", "dve", "pool")
    DUR = {"pe": 0.11, "act": 0.6, "dve": 0.65, "pool": 1.3, "sp": 0.4}

    def __init__(self, nc, es, n_dma_sems=48):
        self.nc = nc
        self.ops = []
        self.batch_start = 0
        self.sems = {}
        for e in self.CE:
            self.sems[e] = es.enter_context(nc.semaphore("prog_" + e))
        self.free_dma = {"sw": [], "hw": []}
        self.semcnt = {}
        self.assigned = []
        for i in range(n_dma_sems):
            k = f"dma{i}"
            self.sems[k] = es.enter_context(nc.semaphore(k))
            self.free_dma["sw" if i < n_dma_sems // 2 else "hw"].append(k)
            self.semcnt[k] = 0
        self.nbuf = 0
        self.seen = {e: {} for e in self.ENGS}
        self.count = {e: 0 for e in self.CE}
        import os
        self.reorder = os.environ.get("REORDER", "1") == "1"

    def release(self, keep=()):
        still = []
        for b, kind in self.assigned:
            if b in keep:
                still.append((b, kind))
            else:
                self.free_dma[kind].append(b.sem)
                b.sem = None
        self.assigned = still

    def buf(self, name=None, excl=False):
        self.nbuf += 1
        return Buf(name or f"b{self.nbuf}", excl)

    def bufs(self, n, name="b", excl=False):
        return [self.buf(f"{name}{i}", excl) for i in range(n)]

    def _buf_sem(self, b, kind):
        if b.sem is None:
            b.sem = self.free_dma[kind].pop(0)
            self.assigned.append((b, kind))
        return b.sem

    def _add(self, eng, fn, r, w, nowaw, dma_tok=None, cost=None):
        oid = len(self.ops)
        deps = set()
        for b in r:
            deps |= b.ws
            if b.excl:
                deps |= {x for x in b.readers if self.ops[x]["eng"] != eng}
        for b in w:
            if not nowaw:
                deps |= b.ws
            else:
                deps |= b.prev
            deps |= b.readers
        self.ops.append(dict(id=oid, eng=eng, fn=fn, deps=deps, tok=dma_tok,
                             dur=(cost if cost is not None else (3.0 if dma_tok else self.DUR[eng]))))
        for b in r:
            b.readers.add(oid)
        for b in w:
            if nowaw:
                b.ws.add(oid)
                b.prev |= b.readers
            else:
                b.prev = set(b.ws) | set(b.readers)
                b.ws = {oid}
            b.readers = set()
        return oid

    def op(self, eng, fn, r=(), w=(), nowaw=False, cost=None):
        return self._add(eng, fn, r, w, nowaw, None, cost)

    def dma(self, eng, fn, r=(), w=(), nowaw=False, semb=None):
        sb = semb or (w[0] if w else r[0])
        key = self._buf_sem(sb, "sw" if eng == "pool" else "hw")
        self.semcnt[key] += 16
        return self._add(eng, fn, r, w, nowaw, (key, self.semcnt[key]))

    def wait_all(self, eng, bufs):
        self._add(eng, None, bufs, bufs, False, None, 0.01)

    def _schedule(self, batch):
        ids = [o["id"] for o in batch]
        lo = ids[0]
        n = len(batch)
        preds = [[] for _ in range(n)]
        succs = [[] for _ in range(n)]
        last_dma = {}
        for i, o in enumerate(batch):
            ps = {p - lo for p in o["deps"] if p >= lo}
            if o["tok"] is not None or o["eng"] == "sp":
                q = o["eng"]
                if q in last_dma:
                    ps.add(last_dma[q])
                last_dma[q] = i
            preds[i] = sorted(ps)
            for p in ps:
                succs[p].append(i)
        if not self.reorder:
            order = {e: [] for e in self.ENGS}
            for o in batch:
                order[o["eng"]].append(o)
            return order
        HOP = 0.2
        prio = [0.0] * n
        for i in range(n - 1, -1, -1):
            m = 0.0
            for sc in succs[i]:
                if prio[sc] > m:
                    m = prio[sc]
            prio[i] = m + batch[i]["dur"] + HOP
        indeg = [len(p) for p in preds]
        import heapq, os
        ready = {e: [] for e in self.ENGS}
        for i in range(n):
            if indeg[i] == 0:
                heapq.heappush(ready[batch[i]["eng"]], i)
        tfree = {e: 0.0 for e in self.ENGS}
        finish = [0.0] * n
        order = {e: [] for e in self.ENGS}
        done = 0
        WINDOW = int(os.environ.get("WINDOW", "8"))
        while done < n:
            best = None
            for e in self.ENGS:
                h = ready[e]
                if not h:
                    continue
                cand = heapq.nsmallest(WINDOW, h)
                for i in cand:
                    st = tfree[e]
                    for p in preds[i]:
                        f = finish[p] + (HOP if batch[p]["eng"] != e else 0.05)
                        if f > st:
                            st = f
                    key = (st, -prio[i], i)
                    if best is None or key < best[0]:
                        best = (key, i, e, st)
            _, i, e, st = best
            ready[e].remove(i)
            heapq.heapify(ready[e])
            finish[i] = st + batch[i]["dur"]
            tfree[e] = finish[i]
            order[e].append(batch[i])
            done += 1
            for sc in succs[i]:
                indeg[sc] -= 1
                if indeg[sc] == 0:
                    heapq.heappush(ready[batch[sc]["eng"]], sc)
        self.est_makespan = max(tfree.values())
        return order

    def emit(self, block):
        import os
        batch = self.ops[self.batch_start:]
        self.batch_start = len(self.ops)
        if not batch:
            return
        lo = batch[0]["id"]
        order = self._schedule(batch)
        CE = self.CE
        if not hasattr(self, "emitted"):
            self.emitted = {e: 0 for e in CE}
            self.marks = {e: [] for e in CE}
        base = dict(self.emitted)
        pos = {}
        for e in CE:
            k = base[e]
            for o in order[e]:
                if o["fn"] is not None:
                    k += 1
                    pos[o["id"]] = k
                else:
                    pos[o["id"]] = k
            self.emitted[e] = k
        waits_of = {}
        need = {e: set() for e in CE}
        for e in self.ENGS:
            seen = self.seen[e]
            for o in order[e]:
                deps = {}
                for p in o["deps"]:
                    po = self.ops[p]
                    if po["tok"] is not None:
                        k, v = po["tok"]
                    elif p >= lo:
                        if e == "pe" and po["eng"] == "pe":
                            continue
                        k, v = po["eng"], pos[p]
                        need[k].add(v)
                    else:
                        continue
                    if deps.get(k, 0) < v:
                        deps[k] = v
                ws = []
                for k, v in deps.items():
                    if seen.get(k, 0) >= v:
                        continue
                    seen[k] = v
                    ws.append((k, v))
                    if k in CE:
                        need[k].add(v)
                waits_of[o["id"]] = ws
        for e in CE:
            lastc = [pos[o["id"]] for o in order[e] if o["fn"] is not None and o["tok"] is None]
            if lastc:
                need[e].add(lastc[-1])
            self.marks[e].extend(sorted(need[e]))
        import bisect
        marks = self.marks
        P = self

        def xlate(k, v):
            if k not in CE:
                return v
            return bisect.bisect_left(marks[k], v) + 1
        if os.environ.get("CHECK") == "1":
            ptr = {e: 0 for e in self.ENGS}
            cnt = {e: bisect.bisect_right(marks[e], base[e]) for e in CE}
            dma_done = dict(getattr(self, "_dma_done", {}))
            doneset = set()
            progress = True
            nviol = 0
            while progress:
                progress = False
                for e in self.ENGS:
                    while ptr[e] < len(order[e]):
                        o = order[e][ptr[e]]
                        ok = True
                        for k, v in waits_of[o["id"]]:
                            if k in CE:
                                if cnt[k] < xlate(k, v):
                                    ok = False
                            elif dma_done.get(k, 0) < v:
                                ok = False
                        if not ok:
                            break
                        for p in o["deps"]:
                            if p >= lo and p not in doneset:
                                nviol += 1
                                if nviol < 5:
                                    print("VIOLATION: op", o["id"], e, "runs before dep", p, self.ops[p]["eng"], pos.get(p), "waits", waits_of[o["id"]])
                        doneset.add(o["id"])
                        if o["tok"] is not None:
                            dma_done[o["tok"][0]] = dma_done.get(o["tok"][0], 0) + 16
                        elif e in CE and o["fn"] is not None and pos[o["id"]] in need[e]:
                            cnt[e] += 1
                        ptr[e] += 1
                        progress = True
            self._dma_done = dma_done
            print("CHECK batch", lo, "ops", len(batch), "violations", nviol, "stuck", {e: (ptr[e], len(order[e])) for e in self.ENGS if ptr[e] < len(order[e])})
        def run(engobj, lst, ename):
            for o in lst:
                for k, v in waits_of[o["id"]]:
                    engobj.wait_ge(P.sems[k], xlate(k, v))
                if o["fn"] is None:
                    continue
                ins = o["fn"](engobj)
                if o["tok"] is not None:
                    ins.then_inc(P.sems[o["tok"][0]], 16)
                elif ename in CE and pos[o["id"]] in need[ename]:
                    ins.then_inc(P.sems[ename], 1)

        if order["pe"]:
            @block.tensor
            def _(e):
                run(e, order["pe"], "pe")
        if order["act"]:
            @block.scalar
            def _(e):
                run(e, order["act"], "act")
        if order["dve"]:
            @block.vector
            def _(e):
                run(e, order["dve"], "dve")
        if order["pool"]:
            @block.gpsimd
            def _(e):
                run(e, order["pool"], "pool")
        if order["sp"]:
            @block.sync
            def _(e):
                run(e, order["sp"], "sp")


def ffn_phase(nc, P, C, src, dst, w_in, w_out, g_pre_pc, g_post_bc, srcb, dstb, tag, n_groups=16, stage=9):
    G = 256
    with ExitStack() as es:
        sb = lambda name, shape, dt: es.enter_context(nc.sbuf_tensor(f"{tag}_{name}", shape, dt))
        ps = lambda name, shape, dt: es.enter_context(nc.psum_tensor(f"{tag}_{name}", shape, dt))
        w1 = sb("w1", [128, 8, 2 * DFF], BF16)
        w2 = sb("w2", [128, NFF, D], BF16)
        gpost = sb("gpost", [128, D], F32)
        gpre = sb("gpre", [128, 8], F32)
        xt = [sb(f"xt{i}", [128, 2, D], F32) for i in range(2)]
        ub = sb("ub", [128, 2, D], BF16)
        uT = [sb(f"uT{i}", [128, 8, G], BF16) for i in range(2)]
        actT = sb("actT", [128, NFF, G], BF16)
        sg = [sb(f"sg{i}", [128, G], F32) for i in range(2)]
        tmpf = [sb(f"tmpf{i}", [128, D], F32) for i in range(2)]
        junk = [sb(f"junk{i}", [128, 512], BF16) for i in range(2)]
        st_ss = sb("ss", [128, 4], F32)
        st_r = sb("rstd", [128, 4], F32)
        st_ss2 = sb("ss2", [128, 4], F32)
        st_r2 = sb("rstd2", [128, 4], F32)
        pgu = [ps(f"pgu{i}", [128, 512], F32) for i in range(4)]
        pout = [ps(f"pout{i}", [128, 512], F32) for i in range(2)]
        ptr = [ps(f"ptr{i}", [128, 1024], BF16) for i in range(2)]

        b_w1 = P.bufs(8, "w1c")
        b_w2 = P.buf("w2")
        b_gpost = P.buf("gpost")
        b_gpre = P.buf("gpre")
        b_xt = [[P.buf(f"xt{i}{j}") for j in range(2)] for i in range(2)]
        b_ub = P.bufs(2, "ub")
        b_uT = P.bufs(2, "uT")
        b_actT = P.bufs(NFF, "actT")
        b_sg = P.bufs(2, "sg")
        b_tmpf = [[P.buf(f"tmpf{i}{k}") for k in range(2)] for i in range(2)]
        b_junk = P.bufs(2, "junk")
        b_ss = [P.bufs(2, f"ss{i}") for i in range(2)]
        b_r = P.bufs(2, "r")
        b_ss2 = [P.bufs(2, f"ss2{i}") for i in range(2)]
        b_r2 = P.bufs(2, "r2")
        b_pgu = P.bufs(4, "pgu", True)
        b_pout = P.bufs(2, "pout", True)
        b_ptr = P.bufs(2, "ptr", True)

        w_in_v = w_in.rearrange("(c p) n -> p c n", p=128)
        w_out_v = w_out.rearrange("(j p) n -> p j n", p=128)
        P.dma("sp", lambda e: e.dma_start(out=gpre[:, :], in_=g_pre_pc), w=[b_gpre])
        P.dma("sp", lambda e: e.dma_start(out=gpost[:, :], in_=g_post_bc), w=[b_gpost])
        for c in range(8):
            P.dma("pool", lambda e, c=c: e.dma_start(out=w1[:, c, :], in_=w_in_v[:, c, :]), w=[b_w1[c]])

        def load_x(g):
            s = g % 2
            for j in range(2):
                r0 = g * G + j * 128
                P.dma("sp", lambda e, s=s, j=j, r0=r0: e.dma_start(out=xt[s][:, j, :], in_=src[r0:r0 + 128, :]),
                      r=[srcb], w=[b_xt[s][j]])

        def prenorm(g):
            s = g % 2
            for j in range(2):
                P.op("act", lambda e, s=s, j=j: e.activation(out=ub[:, j, :], in_=xt[s][:, j, :], func=AF.Square,
                                                             accum_out=st_ss[:, 2 * s + j:2 * s + j + 1]),
                     r=[b_xt[s][j]], w=[b_ub[j], b_ss[s][j]])
            P.op("pool", lambda e, s=s: e.tensor_scalar(out=st_r[:, 2 * s:2 * s + 2], in0=st_ss[:, 2 * s:2 * s + 2],
                                                        scalar1=1.0 / D, scalar2=EPS, op0=ALU.mult, op1=ALU.add),
                 r=[b_ss[s][0], b_ss[s][1]], w=[b_r[s]])
            P.op("pool", lambda e, s=s: e.tensor_tensor(out=st_r[:, 2 * s:2 * s + 2], in0=st_r[:, 2 * s:2 * s + 2],
                                                        in1=C["neg_half"][:, 0:2], op=ALU.pow),
                 r=[b_r[s], C["b_neg"]], w=[b_r[s]])
            for j in range(2):
                P.op("act", lambda e, s=s, j=j: e.activation(out=ub[:, j, :], in_=xt[s][:, j, :], func=AF.Copy,
                                                             scale=st_r[:, 2 * s + j:2 * s + j + 1]),
                     r=[b_xt[s][j], b_r[s]], w=[b_ub[j]])

        def transposes(g):
            s = g % 2
            for half in range(2):
                for cc in range(4):
                    c = half * 4 + cc
                    for j in range(2):
                        P.op("pe", lambda e, half=half, cc=cc, c=c, j=j: e.transpose(
                            out=ptr[half][:, cc * 256 + j * 128: cc * 256 + (j + 1) * 128],
                            in_=ub[:, j, c * 128:(c + 1) * 128], identity=C["ident_bf"][:, :]),
                            r=[b_ub[j], C["b_ident_bf"]], w=[b_ptr[half]])
                for cc in range(4):
                    c = half * 4 + cc
                    P.op("act", lambda e, s=s, half=half, cc=cc, c=c: e.activation(
                        out=uT[s][:, c, :], in_=ptr[half][:, cc * 256:(cc + 1) * 256], func=AF.Copy,
                        scale=gpre[:, c:c + 1]),
                        r=[b_ptr[half], b_gpre], w=[b_uT[s]], nowaw=True)

        def first_mm(g, j0, j1):
            s = g % 2
            for j in range(j0, j1):
                pb = 2 * (j % 2)
                for which in range(2):
                    col0 = which * DFF + j * 128
                    for c in range(8):
                        P.op("pe", lambda e, pb=pb, which=which, col0=col0, c=c, s=s: e.matmul(
                            pgu[pb + which][:, 0:G], lhsT=w1[:, c, col0:col0 + 128], rhs=uT[s][:, c, :],
                            start=(c == 0), stop=(c == 7)),
                            r=[b_w1[c], b_uT[s]], w=[b_pgu[pb + which]], cost=0.13)
                P.op("act", lambda e, pb=pb, j=j: e.activation(out=sg[j % 2][:, :], in_=pgu[pb][:, 0:G], func=AF.Silu),
                     r=[b_pgu[pb]], w=[b_sg[j % 2]])
                P.op("dve", lambda e, pb=pb, j=j: e.tensor_tensor(out=actT[:, j, :], in0=sg[j % 2][:, :],
                                                                  in1=pgu[pb + 1][:, 0:G], op=ALU.mult),
                     r=[b_sg[j % 2], b_pgu[pb + 1]], w=[b_actT[j]])

        import os
        VAR = int(os.environ.get("VAR", "9"))

        def second_mm(g):
            s = g % 2
            for j in range(2):
                for hf in range(2):
                    pb = hf
                    for k in range(NFF):
                        P.op("pe", lambda e, pb=pb, j=j, hf=hf, k=k: e.matmul(
                            pout[pb][:, :], lhsT=actT[:, k, j * 128:(j + 1) * 128], rhs=w2[:, k, hf * 512:(hf + 1) * 512],
                            start=(k == 0), stop=(k == NFF - 1)),
                            r=[b_actT[k], b_w2], w=[b_pout[pb]], cost=0.24)
                    P.op("act", lambda e, pb=pb, j=j, hf=hf: e.activation(
                        out=junk[hf][:, :], in_=pout[pb][:, :], func=AF.Square,
                        accum_out=st_ss2[:, 2 * j + hf:2 * j + hf + 1]),
                        r=[b_pout[pb]], w=[b_junk[hf], b_ss2[j][hf]])
                    if VAR < 2:
                        continue
                    P.op("dve", lambda e, pb=pb, j=j, hf=hf: e.tensor_tensor(
                        out=tmpf[j][:, hf * 512:(hf + 1) * 512], in0=pout[pb][:, :],
                        in1=gpost[:, hf * 512:(hf + 1) * 512], op=ALU.mult),
                        r=[b_pout[pb], b_gpost], w=[b_tmpf[j][hf]])
                if VAR < 3:
                    continue
                P.op("pool", lambda e, j=j: e.tensor_tensor(out=st_r2[:, 2 * j:2 * j + 1], in0=st_ss2[:, 2 * j:2 * j + 1],
                                                            in1=st_ss2[:, 2 * j + 1:2 * j + 2], op=ALU.add),
                     r=[b_ss2[j][0], b_ss2[j][1]], w=[b_r2[j]])
                P.op("pool", lambda e, j=j: e.tensor_scalar(out=st_r2[:, 2 * j:2 * j + 1], in0=st_r2[:, 2 * j:2 * j + 1],
                                                            scalar1=4.0 / D, scalar2=4.0 * EPS, op0=ALU.mult, op1=ALU.add),
                     r=[b_r2[j]], w=[b_r2[j]])
                P.op("pool", lambda e, j=j: e.tensor_tensor(out=st_r2[:, 2 * j:2 * j + 1], in0=st_r2[:, 2 * j:2 * j + 1],
                                                            in1=C["neg_half"][:, 0:1], op=ALU.pow),
                     r=[b_r2[j], C["b_neg"]], w=[b_r2[j]])
                if VAR < 4:
                    continue
                P.op("dve", lambda e, s=s, j=j: e.scalar_tensor_tensor(
                    out=xt[s][:, j, :], in0=tmpf[j][:, :], scalar=st_r2[:, 2 * j:2 * j + 1], in1=xt[s][:, j, :],
                    op0=ALU.mult, op1=ALU.add),
                    r=[b_tmpf[j][0], b_tmpf[j][1], b_r2[j], b_xt[s][j]], w=[b_xt[s][j]])
                r0 = g * G + j * 128
                if VAR < 5:
                    continue
                P.dma("sp", lambda e, s=s, j=j, r0=r0: e.dma_start(out=dst[r0:r0 + 128, :], in_=xt[s][:, j, :]),
                      r=[b_xt[s][j]], w=[dstb], nowaw=True, semb=b_xt[s][j])

        load_x(0)
        if stage >= 1:
            prenorm(0)
        for k in range(NFF):
            P.dma("pool", lambda e, k=k: e.dma_start(out=w2[:, k, :], in_=w_out_v[:, k, :]), w=[b_w2], nowaw=True)
        if stage >= 2:
            transposes(0)
        for g in range(n_groups if stage >= 3 else 0):
            if g + 1 < n_groups:
                load_x(g + 1)
            first_mm(g, 0, 6)
            if g + 1 < n_groups:
                prenorm(g + 1)
            first_mm(g, 6, NFF)
            if g + 1 < n_groups:
                transposes(g + 1)
            if stage >= 4:
                second_mm(g)

        with nc.Block() as block:
            P.emit(block)


def setup_consts(nc, P, es, ident_d, ident_f_d):
    C = {}
    C["ident_bf"] = es.enter_context(nc.sbuf_tensor("ident_bf", [128, 128], BF16))
    C["ident_f"] = es.enter_context(nc.sbuf_tensor("ident_f", [128, 128], F32))
    C["neg_half"] = es.enter_context(nc.sbuf_tensor("neg_half", [128, 8], F32))
    C["b_ident_bf"] = P.buf("identbf")
    C["b_ident_f"] = P.buf("identf")
    C["b_neg"] = P.buf("neg")
    P.dma("pool", lambda e: e.dma_start(out=C["ident_bf"][:, :], in_=ident_d), w=[C["b_ident_bf"]])
    P.dma("sp", lambda e: e.dma_start(out=C["ident_f"][:, :], in_=ident_f_d), w=[C["b_ident_f"]])
    P.op("pool", lambda e: e.memset(C["neg_half"][:, :], -0.5), w=[C["b_neg"]])
    return C


def build(phases=("A", "B1", "B2", "C"), n_groups=16, n_tiles=32, stage=9):
    nc = bass.Bass("TRN2", target_bir_lowering=False)

    def dt(name, shape, dtype=F32, kind="ExternalInput"):
        return nc.dram_tensor(name, shape, dtype, kind=kind).ap()
    full = set(phases) == {"A", "B1", "B2", "C"}
    I = {}
    x = dt("x", [S, D])
    out = dt("out", [S, D], kind="ExternalOutput")
    ident_d = dt("ident", [128, 128])
    scratch_kind = "Internal" if full else "ExternalOutput"
    h1 = dt("h1", [S, D], kind=("ExternalInput" if "A" not in phases else scratch_kind))
    ys = dt("ys", [S, 1536], BF16, kind=scratch_kind if "B1" in phases else "ExternalInput")
    h2 = dt("h2", [S, D], kind=scratch_kind)
    I["psc"] = dt("psc", [S + 1, RW], kind="Internal")
    for nm, shp in (("gate_b", [128, 3072]), ("w_branch", [1536, D]), ("w_out", [D, D]), ("mix_post_g", [128, D])):
        I[nm] = dt(nm, shp)
    for nm, shp in (("ffn1_w_in", [D, 2 * DFF]), ("ffn1_w_out", [DFF, D]), ("ffn1_pre_g", [128, 8]), ("ffn1_post_g", [128, D]),
                    ("ffn2_w_in", [D, 2 * DFF]), ("ffn2_w_out", [DFF, D]), ("ffn2_pre_g", [128, 8]), ("ffn2_post_g", [128, D]),
                    ("w_in", [D, 6144]), ("mix_pre_g", [128, 8]), ("cmat", [128, 4, 128]), ("masks", [128, 3, 128]),
                    ("bcv", [128, 7, 512]), ("mubc", [128, RW]), ("lw", [128, 3, 512]), ("invfreq", [128, 8]),
                    ("sinks", [128, 8]), ("mem_norm_g", [128, 8]), ("mem_w_kv", [D, D]), ("mem", [256, D])):
        I[nm] = dt(nm, shp)
    I["pos_pc"] = dt("pos_pc", [128, 32], I32)
    with ExitStack() as es:
        P = Prog(nc, es)
        C = setup_consts(nc, P, es, ident_d, ident_d)
        b_x = P.buf("x_dram")
        b_out = P.buf("out_dram")
        b_h1 = P.buf("h1_dram")
        b_ys = P.buf("ys_dram")
        keep = (C["b_ident_bf"], C["b_ident_f"])
        if "A" in phases:
            ffn_phase(nc, P, C, x, h1, I["ffn1_w_in"], I["ffn1_w_out"], I["ffn1_pre_g"], I["ffn1_post_g"], b_x, b_h1, "f1",
                      n_groups=n_groups, stage=stage)
            P.release(keep)
        if "B1" in phases:
            mixer_b1(nc, P, C, h1, b_h1, ys, b_ys, I["mem"], I, n_tiles=n_tiles)
            P.release(keep)
        b_h2 = P.buf("h2_dram")
        if "B2" in phases:
            P.reorder = False
            mixer_b2(nc, P, C, h1, b_h1, ys, b_ys, h2, b_h2, I["mem"], I, n_tiles=n_tiles)
            P.release(keep)
        P.reorder = True
        if "C" in phases:
            ffn_phase(nc, P, C, h2, out, I["ffn2_w_in"], I["ffn2_w_out"], I["ffn2_pre_g"], I["ffn2_post_g"], b_h2, b_out, "f2",
                      n_groups=n_groups, stage=stage)
        P.wait_all("sp", [b_out, b_h1, b_ys, b_h2])
        with nc.Block() as block:
            P.emit(block)
    return nc


def pc(v, n):
    return np.ascontiguousarray(np.asarray(v, np.float32).reshape(n, 128).T)


def bc(v):
    return np.ascontiguousarray(np.broadcast_to(np.asarray(v, np.float32)[None, :], (128, v.shape[-1])))


RW = 1792
LOGW_SCALE = -0.6065306597126334
TWO_PI = 6.283185307179586


def bcast(ap, shape):
    return ap.broadcast_to(shape)


def _fs(ap):
    try:
        return int(ap.free_size())
    except Exception:
        return 512


def cost_pe(rhs, fp32=False):
    n = _fs(rhs)
    return (0.07 + n * 0.00047) * (4.0 if fp32 else 1.0)


def cost_ew(eng, out):
    n = _fs(out)
    if eng == "act":
        return 0.22 + n * 0.00075
    if eng == "dve":
        return 0.12 + n * 0.00105
    return 0.35 + n * 0.0019


def mixer_b1(nc, P, C, h1, h1b, ys, ysb, mem, I, n_tiles=32, dbg=None):
    with ExitStack() as es:
        cnt = [0]

        def sb(name, shape, dt):
            return es.enter_context(nc.sbuf_tensor(f"b1_{name}", shape, dt))
        banks = [es.enter_context(nc.psum_tensor(f"b1_ps{i}", [128, 512], F32)) for i in range(8)]
        b_banks = P.bufs(8, "b1ps", True)
        rr = [0]

        def nb():
            i = rr[0] % 8
            rr[0] += 1
            return banks[i], b_banks[i]
        tog = [0]

        def evac(out, in_, r, w, nowaw=False):
            tog[0] ^= 1
            if tog[0]:
                P.op("act", lambda e: e.activation(out=out, in_=in_, func=AF.Copy), r=r, w=w, nowaw=nowaw, cost=cost_ew("act", out))
            else:
                P.op("dve", lambda e: e.tensor_copy(out=out, in_=in_), r=r, w=w, nowaw=nowaw, cost=cost_ew("dve", out))

        def TT(eng, out, in0, in1, op, r, w, nowaw=False):
            P.op(eng, lambda e: e.tensor_tensor(out=out, in0=in0, in1=in1, op=op), r=r, w=w, nowaw=nowaw, cost=cost_ew(eng, out))

        def TS(eng, out, in0, s1, s2, op0, op1, r, w, nowaw=False):
            if op1 is None:
                P.op(eng, lambda e: e.tensor_scalar(out=out, in0=in0, scalar1=s1, scalar2=None, op0=op0), r=r, w=w, nowaw=nowaw, cost=cost_ew(eng, out))
            else:
                P.op(eng, lambda e: e.tensor_scalar(out=out, in0=in0, scalar1=s1, scalar2=s2, op0=op0, op1=op1), r=r, w=w, nowaw=nowaw, cost=cost_ew(eng, out))

        def STT(out, in0, scalar, in1, op0, op1, r, w):
            P.op("dve", lambda e: e.scalar_tensor_tensor(out=out, in0=in0, scalar=scalar, in1=in1, op0=op0, op1=op1), r=r, w=w, cost=cost_ew("dve", out) * 1.6)

        def ACT(out, in_, func, r, w, scale=None, bias=None, accum=None, nowaw=False):
            kw = {}
            if scale is not None:
                kw["scale"] = scale
            if bias is not None:
                kw["bias"] = bias
            if accum is not None:
                kw["accum_out"] = accum
            P.op("act", lambda e: e.activation(out=out, in_=in_, func=func, **kw), r=r, w=w, nowaw=nowaw, cost=cost_ew("act", out))

        def MM(out, lhsT, rhs, start, stop, r, w):
            P.op("pe", lambda e: e.matmul(out, lhsT=lhsT, rhs=rhs, start=start, stop=stop), r=r, w=w,
                 cost=cost_pe(rhs, fp32=(rhs.dtype == F32)))

        def TR(out, in_, ident, r, w):
            P.op("pe", lambda e: e.transpose(out=out, in_=in_, identity=ident), r=r, w=w)

        wA = sb("wA", [128, 8, 2560], BF16)
        b_wA = P.bufs(8, "wA")
        gpre = sb("gpre", [128, 8], F32); b_gpre = P.buf("gpre")
        cmat = sb("cmat", [128, 4, 128], F32); b_cmat = P.buf("cmat")
        masks = sb("masks", [128, 3, 128], BF16); b_masks = P.buf("masks")
        onec = sb("onec", [128, 2], F32); b_onec = P.buf("onec")
        bcv = sb("bcv", [128, 7, 512], F32); b_bcv = P.buf("bcv")
        mubc = sb("mubc", [128, RW], F32); b_mubc = P.buf("mubc")
        lw = sb("lw", [128, 3, 512], BF16); b_lw = P.buf("lw")
        sinkexp = sb("sinkexp", [128, 8], F32); b_sink = P.buf("sink")
        cs = sb("cs", [128, 2, 32, 8], F32); b_cs = P.buf("cs")
        Sf = sb("Sf", [64, 512], F32); b_Sf = P.buf("Sf")
        Sb_ = sb("Sb", [64, 512], BF16); b_Sb = P.buf("Sb")
        ident_bf = C["ident_bf"]; b_ibf = C["b_ident_bf"]
        ident_f = C["ident_f"]; b_if = C["b_ident_f"]

        P.dma("sp", lambda e: e.dma_start(out=gpre[:, :], in_=I["mix_pre_g"]), w=[b_gpre])
        P.dma("sp", lambda e: e.dma_start(out=cmat[:, :, :], in_=I["cmat"]), w=[b_cmat])
        P.dma("pool", lambda e: e.dma_start(out=masks[:, :, :], in_=I["masks"]), w=[b_masks])
        P.dma("sp", lambda e: e.dma_start(out=bcv[:, :, :], in_=I["bcv"]), w=[b_bcv])
        P.dma("sp", lambda e: e.dma_start(out=mubc[:, :], in_=I["mubc"]), w=[b_mubc])
        P.dma("pool", lambda e: e.dma_start(out=lw[:, :, :], in_=I["lw"]), w=[b_lw])
        P.op("dve", lambda e: e.memset(onec[:, :], 1.0), w=[b_onec])
        P.op("dve", lambda e: e.memset(Sf[:, :], 0.0), w=[b_Sf])
        P.op("dve", lambda e: e.memset(Sb_[:, :], 0.0), w=[b_Sb])
        w_in_v = I["w_in"].rearrange("(c p) n -> p c n", p=128)
        for c in range(8):
            P.dma("pool", lambda e, c=c: e.dma_start(out=wA[:, c, :], in_=w_in_v[:, c, 0:2560]), w=[b_wA[c]])

        with ExitStack() as es2:
            def sb2(name, shape, dt):
                return es2.enter_context(nc.sbuf_tensor(f"b1p_{name}", shape, dt))
            posi = sb2("posi", [128, 32], I32); b_posi = P.buf("posi")
            posf = sb2("posf", [128, 32], F32); b_posf = P.buf("posf")
            invf = sb2("invf", [128, 8], F32); b_invf = P.buf("invf")
            ang = sb2("ang", [128, 32, 8], F32); b_ang = P.buf("ang")
            kf = sb2("kf", [128, 32, 8], F32); b_kf = P.buf("kf")
            ki = sb2("ki", [128, 32, 8], I32); b_ki = P.buf("ki")
            yv = sb2("yv", [128, 32, 8], F32); b_yv = P.buf("yv")
            y2 = sb2("y2", [128, 32, 8], F32); b_y2 = P.buf("y2")
            mm_ = sb2("mm", [128, 32, 8], F32); b_mm = P.buf("mm")
            sinks = sb2("sinks", [128, 8], F32); b_sinks = P.buf("sinks")
            P.dma("sp", lambda e: e.dma_start(out=posi[:, :], in_=I["pos_pc"]), w=[b_posi])
            P.dma("sp", lambda e: e.dma_start(out=invf[:, :], in_=I["invfreq"]), w=[b_invf])
            P.dma("sp", lambda e: e.dma_start(out=sinks[:, :], in_=I["sinks"]), w=[b_sinks])
            ACT(sinkexp[:, :], sinks[:, :], AF.Exp, [b_sinks], [b_sink])
            P.op("dve", lambda e: e.tensor_copy(out=posf[:, :], in_=posi[:, :]), r=[b_posi], w=[b_posf])
            TT("dve", ang[:, :, :], bcast(posf[:, :].unsqueeze(2), [128, 32, 8]), bcast(invf[:, :].unsqueeze(1), [128, 32, 8]),
               ALU.mult, [b_posf, b_invf], [b_ang])
            TS("dve", kf[:, :, :], ang[:, :, :], 1.0 / TWO_PI, None, ALU.mult, None, [b_ang], [b_kf])
            P.op("dve", lambda e: e.tensor_copy(out=ki[:, :, :], in_=kf[:, :, :]), r=[b_kf], w=[b_ki])
            P.op("dve", lambda e: e.tensor_copy(out=kf[:, :, :], in_=ki[:, :, :]), r=[b_ki], w=[b_kf])
            C1 = 6.28125
            C2 = TWO_PI - C1
            STT(yv[:, :, :], kf[:, :, :], -C1, ang[:, :, :], ALU.mult, ALU.add, [b_kf, b_ang], [b_yv])
            STT(yv[:, :, :], kf[:, :, :], -C2, yv[:, :, :], ALU.mult, ALU.add, [b_kf, b_yv], [b_yv])

            def wrap(t, bt):
                TS("dve", mm_[:, :, :], t, np.pi, -TWO_PI, ALU.is_gt, ALU.mult, [bt], [b_mm])
                TT("dve", t, t, mm_[:, :, :], ALU.add, [bt, b_mm], [bt])
                TS("dve", mm_[:, :, :], t, -np.pi, TWO_PI, ALU.is_lt, ALU.mult, [bt], [b_mm])
                TT("dve", t, t, mm_[:, :, :], ALU.add, [bt, b_mm], [bt])
            wrap(yv[:, :, :], b_yv)
            ACT(cs[:, 1, :, :], yv[:, :, :], AF.Sin, [b_yv], [b_cs])
            TS("dve", y2[:, :, :], yv[:, :, :], np.pi / 2, None, ALU.add, None, [b_yv], [b_y2])
            wrap(y2[:, :, :], b_y2)
            ACT(cs[:, 0, :, :], y2[:, :, :], AF.Sin, [b_y2], [b_cs], nowaw=True)

            with nc.Block() as block:
                P.emit(block)

        ht0 = sb("ht0", [128, D], F32); ht = [ht0, ht0]; b_ht0 = P.buf("ht"); b_ht = [b_ht0, b_ht0]
        ub = sb("ub", [128, D], BF16); b_ub = P.buf("ub")
        ss = sb("ss", [128, 2], F32); b_ss = P.buf("ss"); b_rs = P.buf("rs")
        uT = sb("uT", [128, 8, 128], BF16); b_uT = P.buf("uT")
        pc0 = sb("pc0", [128, RW], F32); pc = [pc0, pc0]; b_pc0 = P.buf("pc"); b_pc = [b_pc0, b_pc0]
        psh = sb("psh", [128, RW], F32); b_psh = P.buf("psh")
        b_psc = P.bufs(n_tiles + 1, "psc"); b_wslot = P.bufs(2, "wslot")
        pm = sb("pm", [128, RW], F32); b_pm = P.buf("pm")
        lin = sb("lin", [128, 256], BF16); b_lin = P.buf("lin")
        linT = sb("linT", [128, 2, 128], BF16); b_linT = P.buf("linT")
        logw = sb("logw", [128, 512], F32); b_logw = P.buf("logw")
        at = sb("at", [128, 512], F32); b_at = P.buf("at")
        gt = [sb(f"gt{i}", [128, 512], F32) for i in range(2)]; b_gt = P.bufs(2, "gt")
        bons = [sb(f"bon{i}", [128, 512], F32) for i in range(2)]; b_bons = P.bufs(2, "bon")
        tz = sb("tz", [128, 512], F32); b_tz = P.buf("tz")
        st8b = sb("st8b", [128, 3, 8], F32); b_st8b = P.buf("st8b")
        Zb = sb("Zb", [128, 512], BF16); b_Zb = P.buf("Zb")
        kk = sb("kk", [128, 512], F32); b_kk = P.buf("kk")
        t1 = sb("t1", [128, 512], F32); b_t1 = P.buf("t1")
        t2 = sb("t2", [128, 512], F32); b_t2 = P.buf("t2")
        k2 = sb("k2", [128, 512], F32); b_k2 = P.buf("k2")
        bt = sb("bt", [128, 512], F32); b_bt = P.buf("bt")
        st8 = sb("st8", [128, 8, 8], F32); b_st8 = P.buf("st8")
        Wt = sb("Wt", [128, 4, 512], BF16); b_Wt = P.bufs(4, "Wt")
        tok_bf = sb("tokbf", [128, 4, 512], BF16); b_tok4 = P.bufs(4, "tokbf")
        tokEs = [sb(f"tokE{i}", [128, 3, 512], BF16) for i in range(2)]; b_tokEs = [P.bufs(3, f"tokE{i}") for i in range(2)]
        Wc = [sb(f"Wc{i}", [64, 16], F32) for i in range(2)]; b_Wc = P.bufs(2, "Wc")
        ARTs = [sb(f"ART{i}", [64, 8, 2, 128], BF16) for i in range(2)]; b_ARTs = P.bufs(2, "ART")
        BKT = sb("BKT", [64, 2, 8, 128], BF16); b_BKT = P.bufs(2, "BKT")
        X1s = [sb(f"X1{i}", [128, 8, 2, 128], BF16) for i in range(2)]; b_X1s = P.bufs(2, "X1")
        X2s = [sb(f"X2{i}", [128, 8, 2, 128], BF16) for i in range(2)]; b_X2s = P.bufs(2, "X2")
        Pm = [sb(f"Pm{i}", [128, 8, 128], BF16) for i in range(2)]; b_Pm = P.bufs(2, "Pm")
        PTm = [sb(f"PTm{i}", [128, 8, 128], BF16) for i in range(2)]; b_PTm = P.bufs(2, "PTm")
        TTm = [sb(f"TTm{i}", [128, 8, 128], BF16) for i in range(2)]; b_TTm = P.bufs(2, "TTm")
        LVs = [sb(f"LV{i}", [128, 512], BF16) for i in range(2)]; b_LVs = P.bufs(2, "LV")
        Ub = sb("Ub", [128, 512], BF16); b_Ub = P.buf("Ub")
        Ys = sb("Ys", [128, 512], F32); b_Ys = P.buf("Ys")
        zt = t1; b_zt = b_t1
        yo = [sb(f"yo{i}", [128, 1024], BF16) for i in range(2)]; b_yo = [P.bufs(2, f"yo{i}") for i in range(2)]
        qf = sb("qf", [128, 640], F32); b_qf = P.buf("qf")
        rt = sb("rt", [128, 4, 10, 8], F32); b_rt = P.buf("rt")
        qb = sb("qb", [128, 640], BF16); b_qb = P.buf("qb")
        QT = sb("QT", [64, 8, 128], BF16); b_QT = P.buf("QT")
        KTs = [sb(f"KTs{i}", [64, 2, 128], BF16) for i in range(2)]; b_KTs = P.bufs(2, "KTs")
        Va = [sb(f"Va{i}", [128, 2, 65], BF16) for i in range(2)]; b_Va = P.bufs(2, "Va")
        PT_ = sb("PTs", [128, 2, 2, 512], BF16); b_PTs = [P.bufs(2, f"PTs{i}") for i in range(2)]
        den = sb("den", [128, 16], F32); b_den = P.buf("den")

        for i in range(2):
            P.op("pool", lambda e, i=i: e.memset(Va[i][:, :, 64:65], 1.0), w=[b_Va[i]])
        P.op("pool", lambda e: e.memset(psh[:, :], 0.0), w=[b_psh])
        P.dma("sp", lambda e: e.dma_start(out=I["psc"][0:1, :], in_=psh[0:1, :]), r=[b_psh], w=[b_psc[0]], semb=b_psh)

        def bv(i):
            return bcv[:, i, :]

        def v3(ap):
            return ap.rearrange("p (h n) -> p h n", h=8)

        def make_tile(t):
            s = t % 2
            yot = yo[s]
            pcur, bpc = pc[s], b_pc[s]
            r_ = pm[:, 0:512]; k_ = pm[:, 512:1024]; v_ = pm[:, 1024:1536]
            tokE = tokEs[s]
            b_tok = list(b_tok4) + list(b_tokEs[s])

            class _TokView:
                def __getitem__(self, key):
                    p_, i_, c_ = key
                    if isinstance(i_, int) and i_ >= 4:
                        return tokE[p_, i_ - 4, c_]
                    return tok_bf[p_, i_, c_]
            tokv = _TokView()
            vb = tokE[:, 2, :]
            ART, b_ART = ARTs[s], b_ARTs[s]
            X1, b_X1 = X1s[s], b_X1s[s]
            X2, b_X2 = X2s[s], b_X2s[s]
            LV, b_LV = LVs[s], b_LVs[s]
            bon, b_bon = bons[s], b_bons[s]
            TTf, bTTf = TTm[0], b_TTm[0]

            def g_head():
                P.dma("sp", lambda e, s=s, t=t: e.dma_start(out=ht[s][:, :], in_=h1[t * 128:(t + 1) * 128, :]), r=[h1b], w=[b_ht[s]])
                ACT(ub[:, :], ht[s][:, :], AF.Square, [b_ht[s]], [b_ub, b_ss], accum=ss[:, 0:1])
                TS("pool", ss[:, 1:2], ss[:, 0:1], 1.0 / D, EPS, ALU.mult, ALU.add, [b_ss], [b_rs])
                TT("pool", ss[:, 1:2], ss[:, 1:2], C["neg_half"][:, 0:1], ALU.pow, [b_rs, C["b_neg"]], [b_rs])
                ACT(ub[:, :], ht[s][:, :], AF.Copy, [b_ht[s], b_rs], [b_ub], scale=ss[:, 1:2])
                pb, bb = nb()
                pbv = pb[:, :].bitcast(BF16)
                pb2, bb2 = nb()
                pbv2 = pb2[:, :].bitcast(BF16)
                for c in range(4):
                    TR(pbv[:, c * 128:(c + 1) * 128], ub[:, c * 128:(c + 1) * 128], ident_bf[:, :], [b_ub, b_ibf], [bb])
                for c in range(4, 8):
                    TR(pbv2[:, c * 128:(c + 1) * 128], ub[:, c * 128:(c + 1) * 128], ident_bf[:, :], [b_ub, b_ibf], [bb2])
                for c in range(4):
                    ACT(uT[:, c, :], pbv[:, c * 128:(c + 1) * 128], AF.Copy, [bb, b_gpre], [b_uT], scale=gpre[:, c:c + 1], nowaw=(c > 0))
                for c in range(4, 8):
                    TS("dve", uT[:, c, :], pbv2[:, c * 128:(c + 1) * 128], gpre[:, c:c + 1], None, ALU.mult, None, [bb2, b_gpre], [b_uT], nowaw=True)
                pblk = []
                for blk in range(5):
                    pb, bb = nb()
                    for c in range(8):
                        MM(pb[:, :], uT[:, c, :], wA[:, c, blk * 512:(blk + 1) * 512], c == 0, c == 7, [b_uT, b_wA[c]], [bb])
                    pblk.append((pb, bb))
                    if blk < 3:
                        evac(pcur[:, blk * 512:(blk + 1) * 512], pb[:, :], [bb], [bpc], nowaw=(blk > 0))
                    elif blk == 3:
                        evac(pcur[:, 1536:1792], pb[:, 0:256], [bb], [bpc], nowaw=True)
                        evac(qf[:, 0:256], pb[:, 256:512], [bb], [b_qf])
                    else:
                        evac(qf[:, 256:640], pb[:, 0:384], [bb], [b_qf], nowaw=True)
                        for g in range(2):
                            evac(Va[s][:, g, 0:64], pb[:, 384 + g * 64:448 + g * 64], [bb], [b_Va[s]], nowaw=(g > 0))
                P.dma("sp", lambda e: e.dma_start(out=I["psc"][t * 128 + 1:t * 128 + 129, :], in_=pcur[:, :]),
                      r=[bpc], w=[b_psc[t + 1], b_wslot[t % 2]], semb=b_wslot[t % 2])
                yield

            def g_X():
                P.dma("sp", lambda e: e.dma_start(out=psh[:, :], in_=I["psc"][t * 128:(t + 1) * 128, :]),
                      r=[b_psc[t], b_psc[t + 1]], w=[b_psh])
                yield
                for hf_ in range(2):
                    sl = slice(hf_ * 896, (hf_ + 1) * 896)
                    TT("dve", psh[:, sl], psh[:, sl], pcur[:, sl], ALU.subtract, [b_psh, bpc], [b_psh])
                    TT("pool", psh[:, sl], psh[:, sl], mubc[:, sl], ALU.mult, [b_psh, b_mubc], [b_psh])
                    TT("pool" if hf_ == 0 else "dve", pm[:, sl], psh[:, sl], pcur[:, sl], ALU.add, [b_psh, bpc], [b_pm], nowaw=(hf_ > 0))
                yield
                ACT(lin[:, 0:64], pm[:, 1536:1600], AF.Tanh, [b_pm], [b_lin])
                ACT(lin[:, 128:256], pm[:, 1664:1792], AF.Sigmoid, [b_pm], [b_lin], nowaw=True)
                P.op("dve", lambda e: e.tensor_copy(out=lin[:, 64:128], in_=pm[:, 1600:1664]), r=[b_pm], w=[b_lin], nowaw=True)
                yield
                yield
                pb, bb = nb()
                pbv = pb[:, :].bitcast(BF16)
                for j in range(2):
                    TR(pbv[:, j * 128:(j + 1) * 128], lin[:, j * 128:(j + 1) * 128], ident_bf[:, :], [b_lin, b_ibf], [bb])
                evac(linT[:, :, :], pbv[:, 0:256].rearrange("p (j t) -> p j t", j=2), [bb], [b_linT])
                yield
                pz, bz = nb()
                MM(pz[:, :], linT[:, 0, :], lw[:, 0, :], True, True, [b_linT, b_lw], [bz])
                pa, ba = nb()
                MM(pa[:, :], linT[:, 0, :], lw[:, 1, :], True, True, [b_linT, b_lw], [ba])
                pg, bg = nb()
                MM(pg[:, :], linT[:, 1, :], lw[:, 2, :], True, True, [b_linT, b_lw], [bg])
                TT("dve", zt[:, :], pz[:, :], bv(0), ALU.add, [bz, b_bcv], [b_zt])
                ACT(zt[:, :], zt[:, :], AF.Sigmoid, [b_zt], [b_zt])
                ACT(logw[:, :], zt[:, :], AF.Copy, [b_zt], [b_logw], scale=LOGW_SCALE)
                TT("dve", at[:, :], pa[:, :], bv(1), ALU.add, [ba, b_bcv], [b_at])
                ACT(at[:, :], at[:, :], AF.Sigmoid, [b_at], [b_at])
                ACT(gt[s][:, :], pg[:, :], AF.Copy, [bg], [b_gt[s]])
                yield
                TT("dve", kk[:, :], k_, bv(2), ALU.mult, [b_pm, b_bcv], [b_kk])
                TT("pool", t1[:, :], kk[:, :], kk[:, :], ALU.mult, [b_kk], [b_t1])
                P.op("dve", lambda e: e.tensor_reduce(out=st8[:, 0, :], in_=v3(t1[:, :]), axis=AX.X, op=ALU.add), r=[b_t1], w=[b_st8])
                TS("dve", st8[:, 0, :], st8[:, 0, :], 1e-24, None, ALU.max, None, [b_st8], [b_st8])
                TT("pool", st8[:, 0, :], st8[:, 0, :], C["neg_half"][:, 0:8], ALU.pow, [b_st8, C["b_neg"]], [b_st8])
                TT("dve", v3(kk[:, :]), v3(kk[:, :]), bcast(st8[:, 0, :].unsqueeze(2), [128, 8, 64]), ALU.mult, [b_kk, b_st8], [b_kk])
                STT(t1[:, :], at[:, :], -1.0, bv(3), ALU.add, ALU.mult, [b_at, b_bcv], [b_t1])
                STT(k2[:, :], t1[:, :], 1.0, k_, ALU.add, ALU.mult, [b_t1, b_pm], [b_k2])
                TT("pool", bt[:, :], kk[:, :], at[:, :], ALU.mult, [b_kk, b_at], [b_bt])
                TT("pool", t2[:, :], r_, k2[:, :], ALU.mult, [b_pm, b_k2], [b_t2])
                TT("pool", t2[:, :], t2[:, :], bv(4), ALU.mult, [b_t2, b_bcv], [b_t2])
                P.op("dve", lambda e: e.tensor_reduce(out=st8[:, 1, :], in_=v3(t2[:, :]), axis=AX.X, op=ALU.add), r=[b_t2], w=[b_st8])
                yield
                yield
                yield
                pcr, bcr = nb()
                MM(pcr[:, :], cmat[:, 0, :], logw[:, :], True, True, [b_cmat, b_logw], [bcr])
                pcg, bcg = nb()
                MM(pcg[:, :], cmat[:, 1, :], logw[:, :], True, True, [b_cmat, b_logw], [bcg])
                pwc, bwc = nb()
                for h in range(8):
                    MM(pwc[0:64, h:h + 1], logw[:, h * 64:(h + 1) * 64], onec[:, 0:1], True, True, [b_logw, b_onec], [bwc])
                for h in range(8):
                    MM(pwc[0:64, 8 + h:9 + h], logw[0:64, h * 64:(h + 1) * 64], onec[0:64, 0:1], True, True, [b_logw, b_onec], [bwc])
                ACT(Wt[:, 0, :], pcr[:, :], AF.Exp, [bcr], [b_Wt[0]])
                ACT(Wt[:, 1, :], pcr[:, :], AF.Exp, [bcr], [b_Wt[1]], scale=-1.0)
                TT("dve", t1[:, :], pcr[:, :], logw[:, :], ALU.subtract, [bcr, b_logw], [b_t1])
                ACT(Wc[s][:, :], pwc[0:64, 0:16], AF.Exp, [bwc], [b_Wc[s]])
                ACT(Wt[:, 2, :], t1[:, :], AF.Exp, [b_t1], [b_Wt[2]])
                ACT(Wt[:, 3, :], pcg[:, :], AF.Exp, [bcg], [b_Wt[3]])
                yield

            def g_Y():
                TT("pool", tokv[:, 1, :], r_, Wt[:, 0, :], ALU.mult, [b_pm, b_Wt[0]], [b_tok[1]])
                TT("dve", tokv[:, 2, :], bt[:, :], Wt[:, 1, :], ALU.mult, [b_bt, b_Wt[1]], [b_tok[2]])
                TT("pool", tokv[:, 3, :], k2[:, :], Wt[:, 1, :], ALU.mult, [b_k2, b_Wt[1]], [b_tok[3]])
                STT(tokv[:, 0, :], kk[:, :], -1.0, Wt[:, 2, :], ALU.mult, ALU.mult, [b_kk, b_Wt[2]], [b_tok[0]])
                TT("dve", tokv[:, 4, :], bt[:, :], Wt[:, 3, :], ALU.mult, [b_bt, b_Wt[3]], [b_tok[4]])
                TT("pool", tokv[:, 5, :], k2[:, :], Wt[:, 3, :], ALU.mult, [b_k2, b_Wt[3]], [b_tok[5]])
                ACT(tokv[:, 6, :], v_, AF.Copy, [b_pm], [b_tok[6]])
                yield
                for qi in range(4):
                    pb, bb = nb()
                    pbv = pb[:, :].bitcast(BF16)
                    for h in range(8):
                        TR(pbv[0:64, h * 128:(h + 1) * 128], tokv[:, qi, h * 64:(h + 1) * 64], ident_bf[:, :], [b_tok[qi], b_ibf], [bb])
                    src = pbv[0:64, :].rearrange("p (h t) -> p h t", h=8)
                    if qi < 2:
                        evac(ART[:, :, qi, :], src, [bb], [b_ART], nowaw=(qi > 0))
                    else:
                        evac(BKT[:, qi - 2, :, :], src, [bb], [b_BKT[qi - 2]])
                yield
                mU = masks[:, 0:2, :]
                for hp in range(4):
                    yield
                    for (Xd, bXd, srcq) in ((X1, b_X1, 0), (X2, b_X2, 1)):
                        pb, bb = nb()
                        for e_ in range(2):
                            h = hp * 2 + e_
                            MM(pb[:, e_ * 256:(e_ + 1) * 256], BKT[:, srcq, h, :], ART[:, h, :, :].rearrange("p a t -> p (a t)"),
                               True, True, [b_BKT[srcq], b_ART], [bb])
                        TT("dve", Xd[:, hp * 2:hp * 2 + 2, :, :], pb[:, :].rearrange("p (e a t) -> p e a t", e=2, a=2),
                           bcast(mU.unsqueeze(1), [128, 2, 2, 128]), ALU.mult, [bb, b_masks], [bXd], nowaw=(hp > 0))
                for hq in range(2):
                    pb, bb = nb()
                    for e_ in range(4):
                        h = hq * 4 + e_
                        MM(pb[:, e_ * 128:(e_ + 1) * 128], ART[:, h, 0, :], BKT[:, 0, h, :], True, True, [b_ART, b_BKT[0]], [bb])
                    TT("dve", Pm[0][:, hq * 4:hq * 4 + 4, :], pb[:, :].rearrange("p (e t) -> p e t", e=4),
                       bcast(masks[:, 2, :].unsqueeze(1), [128, 4, 128]), ALU.mult, [bb, b_masks], [b_Pm[0]], nowaw=(hq > 0))
                P.op("dve", lambda e: e.tensor_copy(out=PTm[0][:, :, :], in_=X1[:, :, 0, :]), r=[b_X1], w=[b_PTm[0]])
                TT("pool", TTm[0][:, :, :], X1[:, :, 0, :], bcast(ident_bf[:, :].unsqueeze(1), [128, 8, 128]), ALU.add,
                   [b_X1, b_ibf], [b_TTm[0]])
                yield
                pb, bb = nb()
                for h in range(8):
                    MM(pb[:, h * 64:(h + 1) * 64], X2[:, h, 0, :], vb[:, h * 64:(h + 1) * 64], True, True, [b_X2, b_tok[6]], [bb])
                evac(LV[:, :], pb[:, :], [bb], [b_LV])
                yield
                TT("dve", v3(bon[:, :]), v3(v_), bcast(st8[:, 1, :].unsqueeze(2), [128, 8, 64]), ALU.mult, [b_pm, b_st8], [b_bon])
                yield

            def g_D():
                cur = 0
                pend = None

                def tt_update(lvl_, pslot):
                    tcur = (lvl_ - 1) % 2
                    tnxt = 1 - tcur
                    for hq in range(2):
                        pb, bb = nb()
                        for e_ in range(4):
                            h = hq * 4 + e_
                            MM(pb[:, e_ * 128:(e_ + 1) * 128], ident_bf[:, :], TTm[tcur][:, h, :], True, False, [b_ibf, b_TTm[tcur]], [bb])
                            MM(pb[:, e_ * 128:(e_ + 1) * 128], Pm[pslot][:, h, :], TTm[tcur][:, h, :], False, True, [b_Pm[pslot], b_TTm[tcur]], [bb])
                        evac(TTm[tnxt][:, hq * 4:hq * 4 + 4, :], pb[:, :].rearrange("p (e t) -> p e t", e=4), [bb], [b_TTm[tnxt]], nowaw=(hq > 0))
                for lvl in range(1, 7):
                    yield
                    nxt = 1 - cur
                    last = (lvl == 6)
                    for hq in range(2):
                        pbP, bbP = nb()
                        for e_ in range(4):
                            h = hq * 4 + e_
                            MM(pbP[:, e_ * 128:(e_ + 1) * 128], PTm[cur][:, h, :], Pm[cur][:, h, :], True, True, [b_PTm[cur], b_Pm[cur]], [bbP])
                        if not last:
                            pbT, bbT = nb()
                            for e_ in range(4):
                                h = hq * 4 + e_
                                MM(pbT[:, e_ * 128:(e_ + 1) * 128], Pm[cur][:, h, :], PTm[cur][:, h, :], True, True, [b_PTm[cur], b_Pm[cur]], [bbT])
                        if hq == 1 and pend is not None:
                            tt_update(pend, cur)
                        evac(Pm[nxt][:, hq * 4:hq * 4 + 4, :], pbP[:, :].rearrange("p (e t) -> p e t", e=4), [bbP], [b_Pm[nxt]], nowaw=(hq > 0))
                        if not last:
                            evac(PTm[nxt][:, hq * 4:hq * 4 + 4, :], pbT[:, :].rearrange("p (e t) -> p e t", e=4), [bbT], [b_PTm[nxt]], nowaw=(hq > 0))
                    pend = lvl
                    cur = nxt
                yield
                tt_update(6, cur)
                yield

            def g_Z1():
                TT("dve", v3(Sb_[:, :]), v3(Sf[:, :]), bcast(Wc[s][:, 8:16].unsqueeze(2), [64, 8, 64]), ALU.mult, [b_Sf, b_Wc[s]], [b_Sb])
                pz_, bz_ = nb()
                for h in range(8):
                    MM(pz_[:, h * 64:(h + 1) * 64], ART[:, h, 0, :], Sb_[:, h * 64:(h + 1) * 64], True, True, [b_ART, b_Sb], [bz_])
                TT("dve", Zb[:, :], pz_[:, :], LV[:, :], ALU.add, [bz_, b_LV], [b_Zb])
                pb, bb = nb()
                for h in range(8):
                    MM(pb[:, h * 64:(h + 1) * 64], TTf[:, h, :], Zb[:, h * 64:(h + 1) * 64], True, True, [bTTf, b_Zb], [bb])
                evac(Ub[:, :], pb[:, :], [bb], [b_Ub])
                py, by = nb()
                for h in range(8):
                    hs = slice(h * 64, (h + 1) * 64)
                    MM(py[:, hs], ART[:, h, 1, :], Sb_[:, hs], True, False, [b_ART, b_Sb], [by])
                    MM(py[:, hs], X1[:, h, 1, :], Ub[:, hs], False, False, [b_X1, b_Ub], [by])
                    MM(py[:, hs], X2[:, h, 1, :], vb[:, hs], False, True, [b_X2, b_tok[6]], [by])
                pst, bst = nb()
                for h in range(8):
                    hs = slice(h * 64, (h + 1) * 64)
                    MM(pst[0:64, hs], tokv[:, 4, hs], Ub[:, hs], True, False, [b_tok[4], b_Ub], [bst])
                    MM(pst[0:64, hs], tokv[:, 5, hs], vb[:, hs], False, True, [b_tok[5], b_tok[6]], [bst])
                ACT(Ys[:, :], py[:, :], AF.Copy, [by], [b_Ys])
                TT("pool", v3(Sf[:, :]), v3(Sf[:, :]), bcast(Wc[s][:, 0:8].unsqueeze(2), [64, 8, 64]), ALU.mult, [b_Sf, b_Wc[s]], [b_Sf])
                TT("dve", Sf[:, :], Sf[:, :], pst[0:64, :], ALU.add, [b_Sf, bst], [b_Sf])
                yield

            def g_Z2():
                P.op("dve", lambda e: e.tensor_reduce(out=st8b[:, 0, :], in_=v3(Ys[:, :]), axis=AX.X, op=ALU.add), r=[b_Ys], w=[b_st8b])
                TT("pool", tz[:, :], Ys[:, :], Ys[:, :], ALU.mult, [b_Ys], [b_tz])
                P.op("dve", lambda e: e.tensor_reduce(out=st8b[:, 1, :], in_=v3(tz[:, :]), axis=AX.X, op=ALU.add), r=[b_tz], w=[b_st8b])
                TS("dve", st8b[:, 0, :], st8b[:, 0, :], 1.0 / 64, None, ALU.mult, None, [b_st8b], [b_st8b])
                TT("dve", st8b[:, 2, :], st8b[:, 0, :], st8b[:, 0, :], ALU.mult, [b_st8b], [b_st8b])
                STT(st8b[:, 1, :], st8b[:, 1, :], 1.0 / 64, st8b[:, 2, :], ALU.mult, ALU.subtract, [b_st8b], [b_st8b])
                TS("dve", st8b[:, 1, :], st8b[:, 1, :], 64e-5, None, ALU.add, None, [b_st8b], [b_st8b])
                TT("pool", st8b[:, 1, :], st8b[:, 1, :], C["neg_half"][:, 0:8], ALU.pow, [b_st8b, C["b_neg"]], [b_st8b])
                TT("dve", v3(Ys[:, :]), v3(Ys[:, :]), bcast(st8b[:, 0, :].unsqueeze(2), [128, 8, 64]), ALU.subtract, [b_Ys, b_st8b], [b_Ys])
                TT("dve", v3(Ys[:, :]), v3(Ys[:, :]), bcast(st8b[:, 1, :].unsqueeze(2), [128, 8, 64]), ALU.mult, [b_Ys, b_st8b], [b_Ys])
                TT("pool", Ys[:, :], Ys[:, :], bv(5), ALU.mult, [b_Ys, b_bcv], [b_Ys])
                TT("pool", Ys[:, :], Ys[:, :], bv(6), ALU.add, [b_Ys, b_bcv], [b_Ys])
                TT("pool", Ys[:, :], Ys[:, :], bon[:, :], ALU.add, [b_Ys, b_bon], [b_Ys])
                TT("dve", yot[:, 0:512], Ys[:, :], gt[s][:, :], ALU.mult, [b_Ys, b_gt[s]], [b_yo[s][0]])
                yield

            def g_swa():
                cosb = bcast(cs[:, 0, t, :].unsqueeze(1), [128, 10, 8])
                sinb = bcast(cs[:, 1, t, :].unsqueeze(1), [128, 10, 8])
                q4 = qf[:, :].rearrange("p (h n) -> p h n", h=10)
                x1 = q4[:, :, 0:8]; x2 = q4[:, :, 8:16]
                TT("dve", rt[:, 0, :, :], x1, cosb, ALU.mult, [b_qf, b_cs], [b_rt])
                TT("pool", rt[:, 1, :, :], x2, sinb, ALU.mult, [b_qf, b_cs], [b_rt], nowaw=True)
                TT("dve", rt[:, 2, :, :], x2, cosb, ALU.mult, [b_qf, b_cs], [b_rt], nowaw=True)
                TT("pool", rt[:, 3, :, :], x1, sinb, ALU.mult, [b_qf, b_cs], [b_rt], nowaw=True)
                TT("dve", x1, rt[:, 0, :, :], rt[:, 1, :, :], ALU.subtract, [b_rt], [b_qf])
                TT("dve", x2, rt[:, 2, :, :], rt[:, 3, :, :], ALU.add, [b_rt], [b_qf], nowaw=True)
                ACT(qb[:, :], qf[:, :], AF.Copy, [b_qf], [b_qb])
                pb, bb = nb()
                pbv = pb[:, :].bitcast(BF16)
                for h in range(8):
                    TR(pbv[0:64, h * 128:(h + 1) * 128], qb[:, h * 64:(h + 1) * 64], ident_bf[:, :], [b_qb, b_ibf], [bb])
                evac(QT[:, :, :], pbv[0:64, :].rearrange("p (h t) -> p h t", h=8), [bb], [b_QT])
                pb, bb = nb()
                pbv = pb[:, :].bitcast(BF16)
                for g in range(2):
                    TR(pbv[0:64, g * 128:(g + 1) * 128], qb[:, 512 + g * 64:512 + (g + 1) * 64], ident_bf[:, :], [b_qb, b_ibf], [bb])
                evac(KTs[s][:, :, :], pbv[0:64, 0:256].rearrange("p (g t) -> p g t", g=2), [bb], [b_KTs[s]])
                kbs = [1] if t == 0 else [0, 1]
                yield
                for g in range(2):
                    for kb in kbs:
                        ksl = s if kb == 1 else 1 - s
                        pb, bb = nb()
                        MM(pb[:, :], KTs[ksl][:, g, :], QT[:, 4 * g:4 * g + 4, :].rearrange("p h t -> p (h t)"), True, True,
                           [b_KTs[ksl], b_QT], [bb])
                        ACT(PT_[:, kb, g, :], pb[:, :], AF.Exp, [bb], [b_PTs[kb][g]], scale=0.125)
                        mk = masks[:, 1, :] if kb == 1 else masks[:, 2, :]
                        TT("dve", PT_[:, kb, g, :].rearrange("p (h t) -> p h t", h=4), PT_[:, kb, g, :].rearrange("p (h t) -> p h t", h=4),
                           bcast(mk.unsqueeze(1), [128, 4, 128]), ALU.mult, [b_PTs[kb][g], b_masks], [b_PTs[kb][g]])
                po = []
                yield
                for g in range(2):
                    pb, bb = nb()
                    po.append((pb, bb))
                    for hh in range(4):
                        for i_, kb in enumerate(kbs):
                            vsl = s if kb == 1 else 1 - s
                            MM(pb[:, hh * 65:(hh + 1) * 65], PT_[:, kb, g, hh * 128:(hh + 1) * 128], Va[vsl][:, g, :],
                               i_ == 0, i_ == len(kbs) - 1, [b_PTs[kb][g], b_Va[vsl]], [bb])
                for g in range(2):
                    pb, bb = po[g]
                    o3 = pb[:, 0:260].rearrange("p (h n) -> p h n", h=4)
                    TT("dve", den[:, 4 * g:4 * g + 4], o3[:, :, 64], sinkexp[:, 4 * g:4 * g + 4], ALU.add, [bb, b_sink], [b_den], nowaw=(g > 0))
                P.op("dve", lambda e: e.reciprocal(out=den[:, 8:16], in_=den[:, 0:8]), r=[b_den], w=[b_den])
                for g in range(2):
                    pb, bb = po[g]
                    o3 = pb[:, 0:260].rearrange("p (h n) -> p h n", h=4)
                    TT("dve", yot[:, 512 + g * 256:512 + (g + 1) * 256].rearrange("p (h n) -> p h n", h=4), o3[:, :, 0:64],
                       bcast(den[:, 8 + 4 * g:12 + 4 * g].unsqueeze(2), [128, 4, 64]), ALU.mult, [bb, b_den], [b_yo[s][1]], nowaw=(g > 0))

                yield

            def g_store():
                P.dma("sp", lambda e, s=s, t=t: e.dma_start(out=ys[t * 128:(t + 1) * 128, 0:1024], in_=yo[s][:, :]),
                      r=[b_yo[s][0], b_yo[s][1]], w=[ysb], nowaw=True, semb=b_yo[s][0])

                yield
            return dict(head=g_head, X=g_X, Y=g_Y, D=g_D, Z1=g_Z1, Z2=g_Z2, swa=g_swa, store=g_store)

        def drain(gens):
            gens = [g for g in gens]
            while gens:
                nxt = []
                for g in gens:
                    try:
                        next(g)
                        nxt.append(g)
                    except StopIteration:
                        pass
                gens = nxt

        def chain(*gs):
            for g_ in gs:
                yield from g_

        T_ = [make_tile(t) for t in range(n_tiles)]
        drain([T_[0]["head"]()])
        drain([T_[0]["X"](), T_[0]["swa"]()])
        if n_tiles > 1:
            drain([T_[1]["head"]()])
        drain([T_[0]["Y"]()])
        for t in range(n_tiles):
            g = [chain(T_[t]["D"](), T_[t]["Z1"](), T_[t]["Z2"](), T_[t]["store"]())]
            if t + 1 < n_tiles:
                g.append(chain(T_[t + 1]["X"](), T_[t + 1]["Y"]()))
                s3 = [T_[t + 1]["swa"]()]
                if t + 2 < n_tiles:
                    s3.append(T_[t + 2]["head"]())
                g.append(chain(*s3))
            drain(g)

        with nc.Block() as block:
            P.emit(block)


def host_consts():
    s = np.arange(128)[:, None]
    t = np.arange(128)[None, :]
    one = np.ones((128, 128), np.float32)
    M_rel = (s <= t).astype(np.float32) - (s <= 63).astype(np.float32) * one
    M_gt = (s > t).astype(np.float32)
    Shift = (s == t - 1).astype(np.float32)
    Sel = ((s == 127) & (t == 0)).astype(np.float32)
    cmat = np.ascontiguousarray(np.stack([M_rel, M_gt, Shift, Sel], 1).astype(np.float32))
    masks = np.ascontiguousarray(np.stack([(s < t) * one, (s <= t) * one, (t < s) * one], 1).astype(np.float32))
    invf = (500000.0 ** (-np.arange(0, 16, 2, dtype=np.float32) / 16)).astype(np.float32)
    return cmat, masks, np.ascontiguousarray(np.broadcast_to(invf[None], (128, 8)))


def host_maps(inp, cores=range(8)):
    g = lambda k: np.asarray(inp[k][0], np.float32)
    cmat, masks, invf = host_consts()
    bcv = np.stack([bc(g(k).reshape(-1)) for k in ("rwkv_w0", "rwkv_a0", "rwkv_k_k", "rwkv_k_a", "rwkv_r_k", "rwkv_gn_g", "rwkv_gn_b")], 1)
    lw = np.zeros((128, 3, 512), np.float32)
    lw[0:64, 0, :] = g("rwkv_w_up")
    lw[64:128, 1, :] = g("rwkv_a_up")
    lw[:, 2, :] = g("rwkv_g_up")
    shared = {
        "ident": np.eye(128, dtype=np.float32),
        "ffn1_w_in": np.ascontiguousarray(g("ffn1_w_in")), "ffn1_w_out": np.ascontiguousarray(g("ffn1_w_out")),
        "ffn1_pre_g": pc(g("ffn1_pre_g"), 8), "ffn1_post_g": bc(g("ffn1_post_g")),
        "ffn2_w_in": np.ascontiguousarray(g("ffn2_w_in")), "ffn2_w_out": np.ascontiguousarray(g("ffn2_w_out")),
        "ffn2_pre_g": pc(g("ffn2_pre_g"), 8), "ffn2_post_g": bc(g("ffn2_post_g")),
        "w_in": np.ascontiguousarray(g("w_in")), "mix_pre_g": pc(g("mix_pre_g"), 8),
        "cmat": cmat, "masks": masks, "bcv": np.ascontiguousarray(bcv), "mubc": bc(g("rwkv_mu")), "lw": lw,
        "invfreq": invf, "sinks": bc(g("swa_sinks")), "mem_norm_g": pc(g("mem_norm_g"), 8),
        "mem_w_kv": np.ascontiguousarray(g("mem_w_kv")),
        "gate_b": bc(g("gate_b")), "w_branch": np.ascontiguousarray(g("w_branch").reshape(1536, 1024)),
        "w_out": np.ascontiguousarray(g("w_out")), "mix_post_g": bc(g("mix_post_g")),
    }
    maps = []
    for b in cores:
        m = dict(shared)
        m["x"] = np.ascontiguousarray(np.asarray(inp["x"][b], np.float32))
        m["mem"] = np.ascontiguousarray(np.asarray(inp["mem"][b], np.float32))
        m["pos_pc"] = np.ascontiguousarray(np.asarray(inp["positions"][b], np.int32).reshape(32, 128).T)
        maps.append(m)
    return maps


def mixer_b2(nc, P, C, h1, h1b, ys, ysb, h2, h2b, mem, I, n_tiles=32):
    with ExitStack() as es:
        def sb(name, shape, dt):
            return es.enter_context(nc.sbuf_tensor(f"b2_{name}", shape, dt))
        banks = [es.enter_context(nc.psum_tensor(f"b2_ps{i}", [128, 512], F32)) for i in range(8)]
        b_banks = P.bufs(8, "b2ps", True)
        rr = [0]

        def nb():
            i = rr[0] % 8
            rr[0] += 1
            return banks[i], b_banks[i]

        def TT(eng, out, in0, in1, op, r, w, nowaw=False):
            P.op(eng, lambda e: e.tensor_tensor(out=out, in0=in0, in1=in1, op=op), r=r, w=w, nowaw=nowaw, cost=cost_ew(eng, out))

        def TS(eng, out, in0, s1, s2, op0, op1, r, w, nowaw=False):
            if op1 is None:
                P.op(eng, lambda e: e.tensor_scalar(out=out, in0=in0, scalar1=s1, scalar2=None, op0=op0), r=r, w=w, nowaw=nowaw, cost=cost_ew(eng, out))
            else:
                P.op(eng, lambda e: e.tensor_scalar(out=out, in0=in0, scalar1=s1, scalar2=s2, op0=op0, op1=op1), r=r, w=w, nowaw=nowaw, cost=cost_ew(eng, out))

        def ACT(out, in_, func, r, w, scale=None, accum=None, nowaw=False):
            kw = {}
            if scale is not None:
                kw["scale"] = scale
            if accum is not None:
                kw["accum_out"] = accum
            P.op("act", lambda e: e.activation(out=out, in_=in_, func=func, **kw), r=r, w=w, nowaw=nowaw, cost=cost_ew("act", out))

        tog = [0]

        def evac(out, in_, r, w, nowaw=False):
            tog[0] ^= 1
            if tog[0]:
                P.op("act", lambda e: e.activation(out=out, in_=in_, func=AF.Copy), r=r, w=w, nowaw=nowaw, cost=cost_ew("act", out))
            else:
                P.op("dve", lambda e: e.tensor_copy(out=out, in_=in_), r=r, w=w, nowaw=nowaw, cost=cost_ew("dve", out))

        def MM(out, lhsT, rhs, start, stop, r, w):
            P.op("pe", lambda e: e.matmul(out, lhsT=lhsT, rhs=rhs, start=start, stop=stop), r=r, w=w,
                 cost=cost_pe(rhs, fp32=(rhs.dtype == F32)))

        def TR(out, in_, ident, r, w):
            P.op("pe", lambda e: e.transpose(out=out, in_=in_, identity=ident), r=r, w=w)

        ident_bf = C["ident_bf"]; b_ibf = C["b_ident_bf"]
        wQ = sb("wQ", [128, 8, 512], BF16); b_wQ = P.buf("wQ")
        KmT = sb("KmT", [128, 4, 256], BF16); b_KmT = P.buf("KmT")
        Vm = sb("Vm", [128, 2, 4, 129], BF16); b_Vm = P.buf("Vm")
        qmb = sb("qmb", [128, 512], BF16); b_qmb = P.buf("qmb")
        QmT = sb("QmT", [128, 4, 128], BF16); b_QmT = P.buf("QmT")
        PmT = sb("PmT", [128, 2, 4, 128], BF16); b_PmT = P.bufs(2, "PmT")
        den = sb("den", [128, 4], F32); b_den = P.buf("den")
        with ExitStack() as es2:
            def sb2(name, shape, dt):
                return es2.enter_context(nc.sbuf_tensor(f"b2p_{name}", shape, dt))
            mt = sb2("mt", [128, 2, D], F32); b_mt = P.bufs(2, "mt")
            mnb = sb2("mnb", [128, 2, D], BF16); b_mnb = P.bufs(2, "mnb")
            memT = sb2("memT", [128, 8, 256], BF16); b_memT = P.buf("memT")
            wkv = sb2("wkv", [128, 8, 1024], BF16); b_wkv = P.buf("wkv")
            gmem = sb2("gmem", [128, 8], F32); b_gmem = P.buf("gmem")
            mss = sb2("mss", [128, 2], F32); b_mss = P.bufs(2, "mss")
            mrs = sb2("mrs", [128, 2], F32); b_mrs = P.buf("mrs")
            P.dma("sp", lambda e: e.dma_start(out=gmem[:, :], in_=I["mem_norm_g"]), w=[b_gmem])
            wkv_v = I["mem_w_kv"].rearrange("(c p) n -> p c n", p=128)
            for c in range(8):
                P.dma("pool", lambda e, c=c: e.dma_start(out=wkv[:, c, :], in_=wkv_v[:, c, :]), w=[b_wkv], nowaw=True)
            for j in range(2):
                P.dma("sp", lambda e, j=j: e.dma_start(out=mt[:, j, :], in_=mem[j * 128:(j + 1) * 128, :]), w=[b_mt[j]])
                ACT(mnb[:, j, :], mt[:, j, :], AF.Square, [b_mt[j]], [b_mnb[j], b_mss[j]], accum=mss[:, j:j + 1])
            TS("pool", mrs[:, :], mss[:, :], 1.0 / D, EPS, ALU.mult, ALU.add, [b_mss[0], b_mss[1]], [b_mrs])
            TT("pool", mrs[:, :], mrs[:, :], C["neg_half"][:, 0:2], ALU.pow, [b_mrs, C["b_neg"]], [b_mrs])
            for j in range(2):
                ACT(mnb[:, j, :], mt[:, j, :], AF.Copy, [b_mt[j], b_mrs], [b_mnb[j]], scale=mrs[:, j:j + 1])
            for half in range(2):
                pb, bb = nb()
                pbv = pb[:, :].bitcast(BF16)
                for cc in range(4):
                    c = half * 4 + cc
                    for j in range(2):
                        TR(pbv[:, cc * 256 + j * 128: cc * 256 + (j + 1) * 128], mnb[:, j, c * 128:(c + 1) * 128],
                           ident_bf[:, :], [b_mnb[j], b_ibf], [bb])
                for cc in range(4):
                    c = half * 4 + cc
                    ACT(memT[:, c, :], pbv[:, cc * 256:(cc + 1) * 256], AF.Copy, [bb, b_gmem], [b_memT],
                        scale=gmem[:, c:c + 1], nowaw=True)
            for h in range(4):
                pb, bb = nb()
                for c in range(8):
                    MM(pb[:, 0:256], wkv[:, c, h * 128:(h + 1) * 128], memT[:, c, :], c == 0, c == 7, [b_wkv, b_memT], [bb])
                evac(KmT[:, h, :], pb[:, 0:256], [bb], [b_KmT], nowaw=True)
            P.op("dve", lambda e: e.memset(Vm[:, :, :, 128:129], 1.0), w=[b_Vm])
            for mb in range(2):
                pb, bb = nb()
                for c in range(8):
                    MM(pb[:, :], memT[:, c, mb * 128:(mb + 1) * 128], wkv[:, c, 512:1024], c == 0, c == 7, [b_wkv, b_memT], [bb])
                evac(Vm[:, mb, :, 0:128], pb[:, :].rearrange("p (h d) -> p h d", h=4), [bb], [b_Vm], nowaw=True)
            with nc.Block() as block:
                P.emit(block)
        wG = sb("wG", [128, 8, 3072], BF16); b_wG = P.bufs(8, "wG")
        wB = sb("wB", [128, 12, D], BF16); b_wB = P.buf("wB")
        wO = sb("wO", [128, 8, D], BF16); b_wO = P.buf("wO")
        gbb = sb("gbb", [128, 3072], F32); b_gbb = P.buf("gbb")
        gpost = sb("gpost", [128, D], F32); b_gpost = P.buf("gpost")
        gpre = sb("gpre", [128, 8], F32); b_gpre = P.buf("gpre")
        ht = [sb(f"ht{i}", [128, D], F32) for i in range(2)]; b_ht = P.bufs(2, "ht")
        yt = [sb(f"yt{i}", [128, 1536], BF16) for i in range(2)]; b_yt = P.bufs(2, "yt"); b_yc = P.bufs(2, "yc")
        ub = sb("ub", [128, D], BF16); b_ub = P.buf("ub")
        ss = sb("ss", [128, 8], F32); b_ss = P.buf("ss"); b_rs = P.buf("rs"); b_ss2 = P.bufs(2, "ss2"); b_rs2 = P.buf("rs2")
        uT = [sb(f"uT{i}", [128, 8, 128], BF16) for i in range(2)]; b_uT = P.bufs(2, "uT")
        gates = [sb(f"gates{i}", [128, 3072], F32) for i in range(2)]; b_gates = [P.bufs(6, f"gates{i}") for i in range(2)]
        yT = [sb(f"yT{i}", [128, 12, 128], BF16) for i in range(2)]; b_yT = [P.bufs(3, f"yT{i}") for i in range(2)]
        merged = sb("merged", [128, D], F32); b_mg = P.bufs(2, "mg")
        tmpm = sb("tmpm", [128, 512], F32); b_tmpm = P.buf("tmpm")
        mbf = sb("mbf", [128, D], BF16); b_mbf = P.bufs(2, "mbf")
        mT = sb("mT", [128, 8, 128], BF16); b_mT = P.buf("mT")
        tmpf = sb("tmpf", [128, D], F32); b_tmpf = P.bufs(2, "tmpf")
        junk = sb("junk", [128, 512], BF16); b_junk = P.buf("junk")

        w_in_v = I["w_in"].rearrange("(c p) n -> p c n", p=128)
        for c in range(8):
            P.dma("pool", lambda e, c=c: e.dma_start(out=wQ[:, c, :], in_=w_in_v[:, c, 2560:3072]), w=[b_wQ], nowaw=True)
        wb_v = I["w_branch"].rearrange("(c p) n -> p c n", p=128)
        wo_v = I["w_out"].rearrange("(c p) n -> p c n", p=128)
        P.dma("sp", lambda e: e.dma_start(out=gpre[:, :], in_=I["mix_pre_g"]), w=[b_gpre])
        P.dma("sp", lambda e: e.dma_start(out=gbb[:, :], in_=I["gate_b"]), w=[b_gbb])
        P.dma("sp", lambda e: e.dma_start(out=gpost[:, :], in_=I["mix_post_g"]), w=[b_gpost])
        for c in range(8):
            P.dma("pool", lambda e, c=c: e.dma_start(out=wG[:, c, :], in_=w_in_v[:, c, 3072:6144]), w=[b_wG[c]])
        for c in range(12):
            P.dma("pool", lambda e, c=c: e.dma_start(out=wB[:, c, :], in_=wb_v[:, c, :]), w=[b_wB], nowaw=True)
        for c in range(8):
            P.dma("pool", lambda e, c=c: e.dma_start(out=wO[:, c, :], in_=wo_v[:, c, :]), w=[b_wO], nowaw=True)

        def load(t):
            s = t % 2
            P.dma("sp", lambda e: e.dma_start(out=ht[s][:, :], in_=h1[t * 128:(t + 1) * 128, :]), r=[h1b], w=[b_ht[s]])
            P.dma("sp", lambda e: e.dma_start(out=yt[s][:, 0:1024], in_=ys[t * 128:(t + 1) * 128, 0:1024]), r=[ysb], w=[b_yt[s]])

        def pro(t):
            s = t % 2
            ACT(ub[:, :], ht[s][:, :], AF.Square, [b_ht[s]], [b_ub, b_ss], accum=ss[:, 0:1])
            TS("pool", ss[:, 1:2], ss[:, 0:1], 1.0 / D, EPS, ALU.mult, ALU.add, [b_ss], [b_rs])
            TT("pool", ss[:, 1:2], ss[:, 1:2], C["neg_half"][:, 0:1], ALU.pow, [b_rs, C["b_neg"]], [b_rs])
            ACT(ub[:, :], ht[s][:, :], AF.Copy, [b_ht[s], b_rs], [b_ub], scale=ss[:, 1:2])
            yield
            yield
            yield
            pb, bb = nb()
            pbv = pb[:, :].bitcast(BF16)
            for c in range(8):
                TR(pbv[:, c * 128:(c + 1) * 128], ub[:, c * 128:(c + 1) * 128], ident_bf[:, :], [b_ub, b_ibf], [bb])
            for c in range(8):
                ACT(uT[s][:, c, :], pbv[:, c * 128:(c + 1) * 128], AF.Copy, [bb, b_gpre], [b_uT[s]], scale=gpre[:, c:c + 1], nowaw=(c > 0))
            yield
            for i in range(2):
                pb, bb = nb()
                pbv = pb[:, :].bitcast(BF16)
                for c in range(4):
                    TR(pbv[:, c * 128:(c + 1) * 128], yt[s][:, i * 512 + c * 128:i * 512 + (c + 1) * 128], ident_bf[:, :], [b_yt[s], b_ibf], [bb])
                if i % 2 == 0:
                    ACT(yT[s][:, 4 * i:4 * i + 4, :], pbv[:, 0:512].rearrange("p (c t) -> p c t", c=4), AF.Copy, [bb], [b_yT[s][i]])
                else:
                    P.op("dve", lambda e, i=i, pbv=pbv: e.tensor_copy(out=yT[s][:, 4 * i:4 * i + 4, :], in_=pbv[:, 0:512].rearrange("p (c t) -> p c t", c=4)),
                         r=[bb], w=[b_yT[s][i]])

        def memgen(t):
            s = t % 2
            yot = None
            pb, bb = nb()
            for c in range(8):
                MM(pb[:, :], uT[s][:, c, :], wQ[:, c, :], c == 0, c == 7, [b_uT[s], b_wQ], [bb])
            evac(qmb[:, :], pb[:, :], [bb], [b_qmb])
            yield
            pb, bb = nb()
            pbv = pb[:, :].bitcast(BF16)
            for h in range(4):
                TR(pbv[:, h * 128:(h + 1) * 128], qmb[:, h * 128:(h + 1) * 128], ident_bf[:, :], [b_qmb, b_ibf], [bb])
            evac(QmT[:, :, :], pbv[:, 0:512].rearrange("p (h t) -> p h t", h=4), [bb], [b_QmT])
            for mb in range(2):
                pb, bb = nb()
                for h in range(4):
                    MM(pb[:, h * 128:(h + 1) * 128], KmT[:, h, mb * 128:(mb + 1) * 128], QmT[:, h, :], True, True, [b_KmT, b_QmT], [bb])
                ACT(PmT[:, mb, :, :], pb[:, :].rearrange("p (h t) -> p h t", h=4), AF.Exp, [bb], [b_PmT[mb]], scale=float(128 ** -0.5))
            pom = []
            yield
            for hp in range(2):
                pb, bb = nb()
                pom.append((pb, bb))
                for e_ in range(2):
                    h = hp * 2 + e_
                    for mb in range(2):
                        MM(pb[:, e_ * 129:(e_ + 1) * 129], PmT[:, mb, h, :], Vm[:, mb, h, :], mb == 0, mb == 1, [b_PmT[mb], b_Vm], [bb])
            for hp in range(2):
                pb, bb = pom[hp]
                o3 = pb[:, 0:258].rearrange("p (h n) -> p h n", h=2)
                P.op("dve", lambda e, o3=o3, hp=hp: e.reciprocal(out=den[:, 2 * hp:2 * hp + 2], in_=o3[:, :, 128]), r=[bb], w=[b_den])
                TT("dve", yt[s][:, 1024 + hp * 256:1024 + (hp + 1) * 256].rearrange("p (h n) -> p h n", h=2), o3[:, :, 0:128],
                   bcast(den[:, 2 * hp:2 * hp + 2].unsqueeze(2), [128, 2, 128]), ALU.mult, [bb, b_den], [b_yc[s]], nowaw=(hp > 0))
            yield


            pb, bb = nb()
            pbv = pb[:, :].bitcast(BF16)
            for c in range(4):
                TR(pbv[:, c * 128:(c + 1) * 128], yt[s][:, 1024 + c * 128:1024 + (c + 1) * 128], ident_bf[:, :], [b_yc[s], b_ibf], [bb])
            ACT(yT[s][:, 8:12, :], pbv[:, 0:512].rearrange("p (c t) -> p c t", c=4), AF.Copy, [bb], [b_yT[s][2]])
            yield

        def G(t):
            s = t % 2
            yield from pro(t)
            yield
            yield from memgen(t)
            for blk in range(6):
                pb, bb = nb()
                for c in range(8):
                    MM(pb[:, :], uT[s][:, c, :], wG[:, c, blk * 512:(blk + 1) * 512], c == 0, c == 7, [b_uT[s], b_wG[c]], [bb])
                gsl = gates[s][:, blk * 512:(blk + 1) * 512]
                TT("dve", gsl, pb[:, :], gbb[:, blk * 512:(blk + 1) * 512], ALU.add, [bb, b_gbb], [b_gates[s][blk]])
                ACT(gsl, gsl, AF.Sigmoid, [b_gates[s][blk]], [b_gates[s][blk]])
                yield

        def M(t):
            s = t % 2
            for hf in range(2):
                msl = merged[:, hf * 512:(hf + 1) * 512]
                for i in range(3):
                    pb, bb = nb()
                    for c in range(4):
                        MM(pb[:, :], yT[s][:, 4 * i + c, :], wB[:, 4 * i + c, hf * 512:(hf + 1) * 512], c == 0, c == 3, [b_yT[s][i], b_wB], [bb])
                    gsl = gates[s][:, i * 1024 + hf * 512:i * 1024 + (hf + 1) * 512]
                    gb_ = b_gates[s][2 * i + hf]
                    if i == 0:
                        TT("dve", msl, pb[:, :], gsl, ALU.mult, [bb, gb_], [b_mg[hf]])
                    else:
                        TT("dve", tmpm[:, :], pb[:, :], gsl, ALU.mult, [bb, gb_], [b_tmpm])
                        if i == 1:
                            TT("pool", msl, msl, tmpm[:, :], ALU.add, [b_mg[hf], b_tmpm], [b_mg[hf]])
                        else:
                            TT("pool", mbf[:, hf * 512:(hf + 1) * 512], msl, tmpm[:, :], ALU.add, [b_mg[hf], b_tmpm], [b_mbf[hf]])
                    yield
            pb, bb = nb()
            pbv = pb[:, :].bitcast(BF16)
            for c in range(8):
                TR(pbv[:, c * 128:(c + 1) * 128], mbf[:, c * 128:(c + 1) * 128], ident_bf[:, :], [b_mbf[c // 4], b_ibf], [bb])
            ACT(mT[:, :, :], pbv[:, :].rearrange("p (c t) -> p c t", c=8), AF.Copy, [bb], [b_mT])
            yield
            for hf in range(2):
                pb, bb = nb()
                for c in range(8):
                    MM(pb[:, :], mT[:, c, :], wO[:, c, hf * 512:(hf + 1) * 512], c == 0, c == 7, [b_mT, b_wO], [bb])
                ACT(junk[:, :], pb[:, :], AF.Square, [bb], [b_junk, b_ss2[hf]], accum=ss[:, 2 + hf:3 + hf])
                TT("dve", tmpf[:, hf * 512:(hf + 1) * 512], pb[:, :], gpost[:, hf * 512:(hf + 1) * 512], ALU.mult, [bb, b_gpost], [b_tmpf[hf]])
                yield
            TT("pool", ss[:, 5:6], ss[:, 2:3], ss[:, 3:4], ALU.add, [b_ss2[0], b_ss2[1]], [b_rs2])
            TS("pool", ss[:, 5:6], ss[:, 5:6], 1.0 / D, EPS, ALU.mult, ALU.add, [b_rs2], [b_rs2])
            TT("pool", ss[:, 5:6], ss[:, 5:6], C["neg_half"][:, 0:1], ALU.pow, [b_rs2, C["b_neg"]], [b_rs2])
            P.op("dve", lambda e: e.scalar_tensor_tensor(out=ht[s][:, :], in0=tmpf[:, :], scalar=ss[:, 5:6], in1=ht[s][:, :],
                                                         op0=ALU.mult, op1=ALU.add),
                 r=[b_tmpf[0], b_tmpf[1], b_rs2, b_ht[s]], w=[b_ht[s]])
            P.dma("sp", lambda e: e.dma_start(out=h2[t * 128:(t + 1) * 128, :], in_=ht[s][:, :]),
                  r=[b_ht[s]], w=[h2b], nowaw=True, semb=b_ht[s])
            yield

        def drain(gens):
            gens = [g for g in gens]
            while gens:
                nxt = []
                for g in gens:
                    try:
                        next(g)
                        nxt.append(g)
                    except StopIteration:
                        pass
                gens = nxt

        load(0)
        drain([G(0)])
        for t in range(n_tiles):
            if t + 1 < n_tiles:
                load(t + 1)
            drain([M(t)] + ([G(t + 1)] if t + 1 < n_tiles else []))

        with nc.Block() as block:
            P.emit(block)


_NC_CACHE = {}


def kernel(**inputs):
    if "nc" not in _NC_CACHE:
        _NC_CACHE["nc"] = build()
    nc = _NC_CACHE["nc"]
    maps = host_maps(inputs)
    res = run_bass_kernel_spmd(nc, maps, core_ids=list(range(8)))
    out = np.stack([np.asarray(r["out"], np.float32) for r in res.results], 0)
    return out
```

```python
import numpy as np
from contextlib import ExitStack
import concourse.bass as bass
import concourse.mybir as mybir
from concourse.bass_utils import run_bass_kernel_spmd

F32 = mybir.dt.float32
BF16 = mybir.dt.bfloat16
I32 = mybir.dt.int32
ALU = mybir.AluOpType
AF = mybir.ActivationFunctionType
AX = mybir.AxisListType

S = 4096
D = 1024
DFF = 2816
NFF = DFF // 128
EPS = 1e-6


class Buf:
    __slots__ = ("name", "ws", "readers", "sem", "excl", "prev")

    def __init__(self, name, excl=False):
        self.name = name
        self.excl = excl
        self.ws = set()
        self.readers = set()
        self.prev = set()
        self.sem = None


class Prog:
    ENGS = ("pe", "act", "dve", "pool", "sp")
    CE = ("pe", "act", "dve", "pool")
    DUR = {"pe": 0.11, "act": 0.6, "dve": 0.65, "pool": 1.3, "sp": 0.4}

    def __init__(self, nc, es, n_dma_sems=48):
        self.nc = nc
        self.ops = []
        self.batch_start = 0
        self.sems = {}
        for e in self.CE:
            self.sems[e] = es.enter_context(nc.semaphore("prog_" + e))
        self.free_dma = {"sw": [], "hw": []}
        self.semcnt = {}
        self.assigned = []
        for i in range(n_dma_sems):
            k = f"dma{i}"
            self.sems[k] = es.enter_context(nc.semaphore(k))
            self.free_dma["sw" if i < n_dma_sems // 2 else "hw"].append(k)
            self.semcnt[k] = 0
        self.nbuf = 0
        self.seen = {e: {} for e in self.ENGS}
        self.count = {e: 0 for e in self.CE}
        import os
        self.reorder = os.environ.get("REORDER", "1") == "1"

    def release(self, keep=()):
        still = []
        for b, kind in self.assigned:
            if b in keep:
                still.append((b, kind))
            else:
                self.free_dma[kind].append(b.sem)
                b.sem = None
        self.assigned = still

    def buf(self, name=None, excl=False):
        self.nbuf += 1
        return Buf(name or f"b{self.nbuf}", excl)

    def bufs(self, n, name="b", excl=False):
        return [self.buf(f"{name}{i}", excl) for i in range(n)]

    def _buf_sem(self, b, kind):
        if b.sem is None:
            b.sem = self.free_dma[kind].pop(0)
            self.assigned.append((b, kind))
        return b.sem

    def _add(self, eng, fn, r, w, nowaw, dma_tok=None, cost=None):
        oid = len(self.ops)
        deps = set()
        for b in r:
            deps |= b.ws
            if b.excl:
                deps |= {x for x in b.readers if self.ops[x]["eng"] != eng}
        for b in w:
            if not nowaw:
                deps |= b.ws
            else:
                deps |= b.prev
            deps |= b.readers
        self.ops.append(dict(id=oid, eng=eng, fn=fn, deps=deps, tok=dma_tok,
                             dur=(cost if cost is not None else (3.0 if dma_tok else self.DUR[eng]))))
        for b in r:
            b.readers.add(oid)
        for b in w:
            if nowaw:
                b.ws.add(oid)
                b.prev |= b.readers
            else:
                b.prev = set(b.ws) | set(b.readers)
                b.ws = {oid}
            b.readers = set()
        return oid

    def op(self, eng, fn, r=(), w=(), nowaw=False, cost=None):
        return self._add(eng, fn, r, w, nowaw, None, cost)

    def dma(self, eng, fn, r=(), w=(), nowaw=False, semb=None):
        sb = semb or (w[0] if w else r[0])
        key = self._buf_sem(sb, "sw" if eng == "pool" else "hw")
        self.semcnt[key] += 16
        return self._add(eng, fn, r, w, nowaw, (key, self.semcnt[key]))

    def wait_all(self, eng, bufs):
        self._add(eng, None, bufs, bufs, False, None, 0.01)

    def _schedule(self, batch):
        ids = [o["id"] for o in batch]
        lo = ids[0]
        n = len(batch)
        preds = [[] for _ in range(n)]
        succs = [[] for _ in range(n)]
        last_dma = {}
        for i, o in enumerate(batch):
            ps = {p - lo for p in o["deps"] if p >= lo}
            if o["tok"] is not None or o["eng"] == "sp":
                q = o["eng"]
                if q in last_dma:
                    ps.add(last_dma[q])
                last_dma[q] = i
            preds[i] = sorted(ps)
            for p in ps:
                succs[p].append(i)
        if not self.reorder:
            order = {e: [] for e in self.ENGS}
            for o in batch:
                order[o["eng"]].append(o)
            return order
        HOP = 0.3
        prio = [0.0] * n
        for i in range(n - 1, -1, -1):
            m = 0.0
            for sc in succs[i]:
                if prio[sc] > m:
                    m = prio[sc]
            prio[i] = m + batch[i]["dur"] + HOP
        indeg = [len(p) for p in preds]
        import heapq, os
        ready = {e: [] for e in self.ENGS}
        for i in range(n):
            if indeg[i] == 0:
                heapq.heappush(ready[batch[i]["eng"]], i)
        tfree = {e: 0.0 for e in self.ENGS}
        finish = [0.0] * n
        order = {e: [] for e in self.ENGS}
        done = 0
        WINDOW = int(os.environ.get("WINDOW", "12"))
        while done < n:
            best = None
            for e in self.ENGS:
                h = ready[e]
                if not h:
                    continue
                cand = heapq.nsmallest(WINDOW, h)
                for i in cand:
                    st = tfree[e]
                    for p in preds[i]:
                        f = finish[p] + (HOP if batch[p]["eng"] != e else 0.05)
                        if f > st:
                            st = f
                    key = (st, -prio[i], i)
                    if best is None or key < best[0]:
                        best = (key, i, e, st)
            _, i, e, st = best
            ready[e].remove(i)
            heapq.heapify(ready[e])
            finish[i] = st + batch[i]["dur"]
            tfree[e] = finish[i]
            order[e].append(batch[i])
            done += 1
            for sc in succs[i]:
                indeg[sc] -= 1
                if indeg[sc] == 0:
                    heapq.heappush(ready[batch[sc]["eng"]], sc)
        self.est_makespan = max(tfree.values())
        return order

    def emit(self, block):
        import os
        batch = self.ops[self.batch_start:]
        self.batch_start = len(self.ops)
        if not batch:
            return
        lo = batch[0]["id"]
        order = self._schedule(batch)
        CE = self.CE
        if not hasattr(self, "emitted"):
            self.emitted = {e: 0 for e in CE}
            self.marks = {e: [] for e in CE}
        base = dict(self.emitted)
        pos = {}
        for e in CE:
            k = base[e]
            for o in order[e]:
                if o["fn"] is not None:
                    k += 1
                    pos[o["id"]] = k
                else:
                    pos[o["id"]] = k
            self.emitted[e] = k
        waits_of = {}
        need = {e: set() for e in CE}
        for e in self.ENGS:
            seen = self.seen[e]
            for o in order[e]:
                deps = {}
                for p in o["deps"]:
                    po = self.ops[p]
                    if po["tok"] is not None:
                        k, v = po["tok"]
                    elif p >= lo:
                        if e == "pe" and po["eng"] == "pe":
                            continue
                        k, v = po["eng"], pos[p]
                        need[k].add(v)
                    else:
                        continue
                    if deps.get(k, 0) < v:
                        deps[k] = v
                ws = []
                for k, v in deps.items():
                    if seen.get(k, 0) >= v:
                        continue
                    seen[k] = v
                    ws.append((k, v))
                    if k in CE:
                        need[k].add(v)
                waits_of[o["id"]] = ws
        for e in CE:
            lastc = [pos[o["id"]] for o in order[e] if o["fn"] is not None and o["tok"] is None]
            if lastc:
                need[e].add(lastc[-1])
            self.marks[e].extend(sorted(need[e]))
        import bisect
        marks = self.marks
        P = self

        def xlate(k, v):
            if k not in CE:
                return v
            return bisect.bisect_left(marks[k], v) + 1
        if os.environ.get("CHECK") == "1":
            ptr = {e: 0 for e in self.ENGS}
            cnt = {e: bisect.bisect_right(marks[e], base[e]) for e in CE}
            dma_done = dict(getattr(self, "_dma_done", {}))
            doneset = set()
            progress = True
            nviol = 0
            while progress:
                progress = False
                for e in self.ENGS:
                    while ptr[e] < len(order[e]):
                        o = order[e][ptr[e]]
                        ok = True
                        for k, v in waits_of[o["id"]]:
                            if k in CE:
                                if cnt[k] < xlate(k, v):
                                    ok = False
                            elif dma_done.get(k, 0) < v:
                                ok = False
                        if not ok:
                            break
                        for p in o["deps"]:
                            if p >= lo and p not in doneset:
                                nviol += 1
                                if nviol < 5:
                                    print("VIOLATION: op", o["id"], e, "runs before dep", p, self.ops[p]["eng"], pos.get(p), "waits", waits_of[o["id"]])
                        doneset.add(o["id"])
                        if o["tok"] is not None:
                            dma_done[o["tok"][0]] = dma_done.get(o["tok"][0], 0) + 16
                        elif e in CE and o["fn"] is not None and pos[o["id"]] in need[e]:
                            cnt[e] += 1
                        ptr[e] += 1
                        progress = True
            self._dma_done = dma_done
            print("CHECK batch", lo, "ops", len(batch), "violations", nviol, "stuck", {e: (ptr[e], len(order[e])) for e in self.ENGS if ptr[e] < len(order[e])})
        def run(engobj, lst, ename):
            for o in lst:
                for k, v in waits_of[o["id"]]:
                    engobj.wait_ge(P.sems[k], xlate(k, v))
                if o["fn"] is None:
                    continue
                ins = o["fn"](engobj)
                if o["tok"] is not None:
                    ins.then_inc(P.sems[o["tok"][0]], 16)
                elif ename in CE and pos[o["id"]] in need[ename]:
                    ins.then_inc(P.sems[ename], 1)

        if order["pe"]:
            @block.tensor
            def _(e):
                run(e, order["pe"], "pe")
        if order["act"]:
            @block.scalar
            def _(e):
                run(e, order["act"], "act")
        if order["dve"]:
            @block.vector
            def _(e):
                run(e, order["dve"], "dve")
        if order["pool"]:
            @block.gpsimd
            def _(e):
                run(e, order["pool"], "pool")
        if order["sp"]:
            @block.sync
            def _(e):
                run(e, order["sp"], "sp")


def ffn_phase(nc, P, C, src, dst, w_in, w_out, g_pre_pc, g_post_bc, srcb, dstb, tag, n_groups=16, stage=9):
    G = 256
    with ExitStack() as es:
        sb = lambda name, shape, dt: es.enter_context(nc.sbuf_tensor(f"{tag}_{name}", shape, dt))
        ps = lambda name, shape, dt: es.enter_context(nc.psum_tensor(f"{tag}_{name}", shape, dt))
        w1 = sb("w1", [128, 8, 2 * DFF], BF16)
        w2 = sb("w2", [128, NFF, D], BF16)
        gpost = sb("gpost", [128, D], F32)
        gpre = sb("gpre", [128, 8], F32)
        xt = [sb(f"xt{i}", [128, 2, D], F32) for i in range(2)]
        ub = sb("ub", [128, 2, D], BF16)
        uT = [sb(f"uT{i}", [128, 8, G], BF16) for i in range(2)]
        actT = sb("actT", [128, NFF, G], BF16)
        sg = [sb(f"sg{i}", [128, G], F32) for i in range(2)]
        tmpf = [sb(f"tmpf{i}", [128, D], F32) for i in range(2)]
        junk = [sb(f"junk{i}", [128, 512], BF16) for i in range(2)]
        st_ss = sb("ss", [128, 4], F32)
        st_r = sb("rstd", [128, 4], F32)
        st_ss2 = sb("ss2", [128, 4], F32)
        st_r2 = sb("rstd2", [128, 4], F32)
        pgu = [ps(f"pgu{i}", [128, 512], F32) for i in range(4)]
        pout = [ps(f"pout{i}", [128, 512], F32) for i in range(2)]
        ptr = [ps(f"ptr{i}", [128, 1024], BF16) for i in range(2)]

        b_w1 = P.bufs(8, "w1c")
        b_w2 = P.buf("w2")
        b_gpost = P.buf("gpost")
        b_gpre = P.buf("gpre")
        b_xt = [[P.buf(f"xt{i}{j}") for j in range(2)] for i in range(2)]
        b_ub = P.bufs(2, "ub")
        b_uT = P.bufs(2, "uT")
        b_actT = P.bufs(NFF, "actT")
        b_sg = P.bufs(2, "sg")
        b_tmpf = [[P.buf(f"tmpf{i}{k}") for k in range(2)] for i in range(2)]
        b_junk = P.bufs(2, "junk")
        b_ss = [P.bufs(2, f"ss{i}") for i in range(2)]
        b_r = P.bufs(2, "r")
        b_ss2 = [P.bufs(2, f"ss2{i}") for i in range(2)]
        b_r2 = P.bufs(2, "r2")
        b_pgu = P.bufs(4, "pgu", True)
        b_pout = P.bufs(2, "pout", True)
        b_ptr = P.bufs(2, "ptr", True)

        w_in_v = w_in.rearrange("(c p) n -> p c n", p=128)
        w_out_v = w_out.rearrange("(j p) n -> p j n", p=128)
        P.dma("sp", lambda e: e.dma_start(out=gpre[:, :], in_=g_pre_pc), w=[b_gpre])
        P.dma("sp", lambda e: e.dma_start(out=gpost[:, :], in_=g_post_bc), w=[b_gpost])
        for c in range(8):
            P.dma("pool", lambda e, c=c: e.dma_start(out=w1[:, c, :], in_=w_in_v[:, c, :]), w=[b_w1[c]])

        def load_x(g):
            s = g % 2
            for j in range(2):
                r0 = g * G + j * 128
                P.dma("sp", lambda e, s=s, j=j, r0=r0: e.dma_start(out=xt[s][:, j, :], in_=src[r0:r0 + 128, :]),
                      r=[srcb], w=[b_xt[s][j]])

        def prenorm(g):
            s = g % 2
            for j in range(2):
                P.op("act", lambda e, s=s, j=j: e.activation(out=ub[:, j, :], in_=xt[s][:, j, :], func=AF.Square,
                                                             accum_out=st_ss[:, 2 * s + j:2 * s + j + 1]),
                     r=[b_xt[s][j]], w=[b_ub[j], b_ss[s][j]])
            P.op("pool", lambda e, s=s: e.tensor_scalar(out=st_r[:, 2 * s:2 * s + 2], in0=st_ss[:, 2 * s:2 * s + 2],
                                                        scalar1=1.0 / D, scalar2=EPS, op0=ALU.mult, op1=ALU.add),
                 r=[b_ss[s][0], b_ss[s][1]], w=[b_r[s]])
            P.op("pool", lambda e, s=s: e.tensor_tensor(out=st_r[:, 2 * s:2 * s + 2], in0=st_r[:, 2 * s:2 * s + 2],
                                                        in1=C["neg_half"][:, 0:2], op=ALU.pow),
                 r=[b_r[s], C["b_neg"]], w=[b_r[s]])
            for j in range(2):
                P.op("act", lambda e, s=s, j=j: e.activation(out=ub[:, j, :], in_=xt[s][:, j, :], func=AF.Copy,
                                                             scale=st_r[:, 2 * s + j:2 * s + j + 1]),
                     r=[b_xt[s][j], b_r[s]], w=[b_ub[j]])

        def transposes(g):
            s = g % 2
            for half in range(2):
                for cc in range(4):
                    c = half * 4 + cc
                    for j in range(2):
                        P.op("pe", lambda e, half=half, cc=cc, c=c, j=j: e.transpose(
                            out=ptr[half][:, cc * 256 + j * 128: cc * 256 + (j + 1) * 128],
                            in_=ub[:, j, c * 128:(c + 1) * 128], identity=C["ident_bf"][:, :]),
                            r=[b_ub[j], C["b_ident_bf"]], w=[b_ptr[half]])
                for cc in range(4):
                    c = half * 4 + cc
                    P.op("act", lambda e, s=s, half=half, cc=cc, c=c: e.activation(
                        out=uT[s][:, c, :], in_=ptr[half][:, cc * 256:(cc + 1) * 256], func=AF.Copy,
                        scale=gpre[:, c:c + 1]),
                        r=[b_ptr[half], b_gpre], w=[b_uT[s]], nowaw=True)

        def first_mm(g, j0, j1):
            s = g % 2
            for j in range(j0, j1):
                pb = 2 * (j % 2)
                for which in range(2):
                    col0 = which * DFF + j * 128
                    for c in range(8):
                        P.op("pe", lambda e, pb=pb, which=which, col0=col0, c=c, s=s: e.matmul(
                            pgu[pb + which][:, 0:G], lhsT=w1[:, c, col0:col0 + 128], rhs=uT[s][:, c, :],
                            start=(c == 0), stop=(c == 7)),
                            r=[b_w1[c], b_uT[s]], w=[b_pgu[pb + which]], cost=0.13)
                P.op("act", lambda e, pb=pb, j=j: e.activation(out=sg[j % 2][:, :], in_=pgu[pb][:, 0:G], func=AF.Silu),
                     r=[b_pgu[pb]], w=[b_sg[j % 2]])
                P.op("dve", lambda e, pb=pb, j=j: e.tensor_tensor(out=actT[:, j, :], in0=sg[j % 2][:, :],
                                                                  in1=pgu[pb + 1][:, 0:G], op=ALU.mult),
                     r=[b_sg[j % 2], b_pgu[pb + 1]], w=[b_actT[j]])

        import os
        VAR = int(os.environ.get("VAR", "9"))

        def second_mm(g):
            s = g % 2
            for j in range(2):
                for hf in range(2):
                    pb = hf
                    for k in range(NFF):
                        P.op("pe", lambda e, pb=pb, j=j, hf=hf, k=k: e.matmul(
                            pout[pb][:, :], lhsT=actT[:, k, j * 128:(j + 1) * 128], rhs=w2[:, k, hf * 512:(hf + 1) * 512],
                            start=(k == 0), stop=(k == NFF - 1)),
                            r=[b_actT[k], b_w2], w=[b_pout[pb]], cost=0.24)
                    P.op("act", lambda e, pb=pb, j=j, hf=hf: e.activation(
                        out=junk[hf][:, :], in_=pout[pb][:, :], func=AF.Square,
                        accum_out=st_ss2[:, 2 * j + hf:2 * j + hf + 1]),
                        r=[b_pout[pb]], w=[b_junk[hf], b_ss2[j][hf]])
                    if VAR < 2:
                        continue
                    P.op("dve", lambda e, pb=pb, j=j, hf=hf: e.tensor_tensor(
                        out=tmpf[j][:, hf * 512:(hf + 1) * 512], in0=pout[pb][:, :],
                        in1=gpost[:, hf * 512:(hf + 1) * 512], op=ALU.mult),
                        r=[b_pout[pb], b_gpost], w=[b_tmpf[j][hf]])
                if VAR < 3:
                    continue
                P.op("pool", lambda e, j=j: e.tensor_tensor(out=st_r2[:, 2 * j:2 * j + 1], in0=st_ss2[:, 2 * j:2 * j + 1],
                                                            in1=st_ss2[:, 2 * j + 1:2 * j + 2], op=ALU.add),
                     r=[b_ss2[j][0], b_ss2[j][1]], w=[b_r2[j]])
                P.op("pool", lambda e, j=j: e.tensor_scalar(out=st_r2[:, 2 * j:2 * j + 1], in0=st_r2[:, 2 * j:2 * j + 1],
                                                            scalar1=4.0 / D, scalar2=4.0 * EPS, op0=ALU.mult, op1=ALU.add),
                     r=[b_r2[j]], w=[b_r2[j]])
                P.op("pool", lambda e, j=j: e.tensor_tensor(out=st_r2[:, 2 * j:2 * j + 1], in0=st_r2[:, 2 * j:2 * j + 1],
                                                            in1=C["neg_half"][:, 0:1], op=ALU.pow),
                     r=[b_r2[j], C["b_neg"]], w=[b_r2[j]])
                if VAR < 4:
                    continue
                P.op("dve", lambda e, s=s, j=j: e.scalar_tensor_tensor(
                    out=xt[s][:, j, :], in0=tmpf[j][:, :], scalar=st_r2[:, 2 * j:2 * j + 1], in1=xt[s][:, j, :],
                    op0=ALU.mult, op1=ALU.add),
                    r=[b_tmpf[j][0], b_tmpf[j][1], b_r2[j], b_xt[s][j]], w=[b_xt[s][j]])
                r0 = g * G + j * 128
                if VAR < 5:
                    continue
                P.dma("sp", lambda e, s=s, j=j, r0=r0: e.dma_start(out=dst[r0:r0 + 128, :], in_=xt[s][:, j, :]),
                      r=[b_xt[s][j]], w=[dstb], nowaw=True, semb=b_xt[s][j])

        load_x(0)
        if stage >= 1:
            prenorm(0)
        for k in range(NFF):
            P.dma("pool", lambda e, k=k: e.dma_start(out=w2[:, k, :], in_=w_out_v[:, k, :]), w=[b_w2], nowaw=True)
        if stage >= 2:
            transposes(0)
        for g in range(n_groups if stage >= 3 else 0):
            if g + 1 < n_groups:
                load_x(g + 1)
            first_mm(g, 0, 6)
            if g + 1 < n_groups:
                prenorm(g + 1)
            first_mm(g, 6, NFF)
            if g + 1 < n_groups:
                transposes(g + 1)
            if stage >= 4:
                second_mm(g)

        with nc.Block() as block:
            P.emit(block)


def setup_consts(nc, P, es, ident_d, ident_f_d):
    C = {}
    C["ident_bf"] = es.enter_context(nc.sbuf_tensor("ident_bf", [128, 128], BF16))
    C["ident_f"] = es.enter_context(nc.sbuf_tensor("ident_f", [128, 128], F32))
    C["neg_half"] = es.enter_context(nc.sbuf_tensor("neg_half", [128, 8], F32))
    C["b_ident_bf"] = P.buf("identbf")
    C["b_ident_f"] = P.buf("identf")
    C["b_neg"] = P.buf("neg")
    P.dma("pool", lambda e: e.dma_start(out=C["ident_bf"][:, :], in_=ident_d), w=[C["b_ident_bf"]])
    P.dma("sp", lambda e: e.dma_start(out=C["ident_f"][:, :], in_=ident_f_d), w=[C["b_ident_f"]])
    P.op("pool", lambda e: e.memset(C["neg_half"][:, :], -0.5), w=[C["b_neg"]])
    return C


def build(phases=("A", "B1", "B2", "C"), n_groups=16, n_tiles=32, stage=9):
    nc = bass.Bass("TRN2", target_bir_lowering=False)

    def dt(name, shape, dtype=F32, kind="ExternalInput"):
        return nc.dram_tensor(name, shape, dtype, kind=kind).ap()
    full = set(phases) == {"A", "B1", "B2", "C"}
    I = {}
    x = dt("x", [S, D])
    out = dt("out", [S, D], kind="ExternalOutput")
    ident_d = dt("ident", [128, 128])
    scratch_kind = "Internal" if full else "ExternalOutput"
    h1 = dt("h1", [S, D], kind=("ExternalInput" if "A" not in phases else scratch_kind))
    ys = dt("ys", [S, 1536], BF16, kind=scratch_kind if "B1" in phases else "ExternalInput")
    h2 = dt("h2", [S, D], kind=scratch_kind)
    I["psc"] = dt("psc", [S + 1, RW], kind="Internal")
    for nm, shp in (("gate_b", [128, 3072]), ("w_branch", [1536, D]), ("w_out", [D, D]), ("mix_post_g", [128, D])):
        I[nm] = dt(nm, shp)
    for nm, shp in (("ffn1_w_in", [D, 2 * DFF]), ("ffn1_w_out", [DFF, D]), ("ffn1_pre_g", [128, 8]), ("ffn1_post_g", [128, D]),
                    ("ffn2_w_in", [D, 2 * DFF]), ("ffn2_w_out", [DFF, D]), ("ffn2_pre_g", [128, 8]), ("ffn2_post_g", [128, D]),
                    ("w_in", [D, 6144]), ("mix_pre_g", [128, 8]), ("cmat", [128, 4, 128]), ("masks", [128, 3, 128]),
                    ("bcv", [128, 7, 512]), ("mubc", [128, RW]), ("lw", [128, 3, 512]), ("invfreq", [128, 8]),
                    ("sinks", [128, 8]), ("mem_norm_g", [128, 8]), ("mem_w_kv", [D, D]), ("mem", [256, D])):
        I[nm] = dt(nm, shp)
    I["pos_pc"] = dt("pos_pc", [128, 32], I32)
    with ExitStack() as es:
        P = Prog(nc, es)
        C = setup_consts(nc, P, es, ident_d, ident_d)
        b_x = P.buf("x_dram")
        b_out = P.buf("out_dram")
        b_h1 = P.buf("h1_dram")
        b_ys = P.buf("ys_dram")
        keep = (C["b_ident_bf"], C["b_ident_f"])
        if "A" in phases:
            ffn_phase(nc, P, C, x, h1, I["ffn1_w_in"], I["ffn1_w_out"], I["ffn1_pre_g"], I["ffn1_post_g"], b_x, b_h1, "f1",
                      n_groups=n_groups, stage=stage)
            P.release(keep)
        if "B1" in phases:
            mixer_b1(nc, P, C, h1, b_h1, ys, b_ys, I["mem"], I, n_tiles=n_tiles)
            P.release(keep)
        b_h2 = P.buf("h2_dram")
        if "B2" in phases:
            P.reorder = False
            mixer_b2(nc, P, C, h1, b_h1, ys, b_ys, h2, b_h2, I["mem"], I, n_tiles=n_tiles)
            P.release(keep)
        P.reorder = True
        if "C" in phases:
            ffn_phase(nc, P, C, h2, out, I["ffn2_w_in"], I["ffn2_w_out"], I["ffn2_pre_g"], I["ffn2_post_g"], b_h2, b_out, "f2",
                      n_groups=n_groups, stage=stage)
        P.wait_all("sp", [b_out, b_h1, b_ys, b_h2])
        with nc.Block() as block:
            P.emit(block)
    return nc


def pc(v, n):
    return np.ascontiguousarray(np.asarray(v, np.float32).reshape(n, 128).T)


def bc(v):
    return np.ascontiguousarray(np.broadcast_to(np.asarray(v, np.float32)[None, :], (128, v.shape[-1])))


RW = 1792
LOGW_SCALE = -0.6065306597126334
TWO_PI = 6.283185307179586


def bcast(ap, shape):
    return ap.broadcast_to(shape)


def _fs(ap):
    try:
        return int(ap.free_size())
    except Exception:
        return 512


def cost_pe(rhs, fp32=False):
    n = _fs(rhs)
    return (0.07 + n * 0.00047) * (4.0 if fp32 else 1.0)


def cost_ew(eng, out):
    n = _fs(out)
    if eng == "act":
        return 0.22 + n * 0.00075
    if eng == "dve":
        return 0.12 + n * 0.00105
    return 0.35 + n * 0.0019


def mixer_b1(nc, P, C, h1, h1b, ys, ysb, mem, I, n_tiles=32, dbg=None):
    with ExitStack() as es:
        cnt = [0]

        def sb(name, shape, dt):
            return es.enter_context(nc.sbuf_tensor(f"b1_{name}", shape, dt))
        banks = [es.enter_context(nc.psum_tensor(f"b1_ps{i}", [128, 512], F32)) for i in range(8)]
        b_banks = P.bufs(8, "b1ps", True)
        rr = [0]

        def nb():
            i = rr[0] % 6
            rr[0] += 1
            return banks[i], b_banks[i]
        tog = [0]

        def evac(out, in_, r, w, nowaw=False):
            tog[0] ^= 1
            if tog[0]:
                P.op("act", lambda e: e.activation(out=out, in_=in_, func=AF.Copy), r=r, w=w, nowaw=nowaw, cost=cost_ew("act", out))
            else:
                P.op("dve", lambda e: e.tensor_copy(out=out, in_=in_), r=r, w=w, nowaw=nowaw, cost=cost_ew("dve", out))

        def TT(eng, out, in0, in1, op, r, w, nowaw=False):
            P.op(eng, lambda e: e.tensor_tensor(out=out, in0=in0, in1=in1, op=op), r=r, w=w, nowaw=nowaw, cost=cost_ew(eng, out))

        def TS(eng, out, in0, s1, s2, op0, op1, r, w, nowaw=False):
            if op1 is None:
                P.op(eng, lambda e: e.tensor_scalar(out=out, in0=in0, scalar1=s1, scalar2=None, op0=op0), r=r, w=w, nowaw=nowaw, cost=cost_ew(eng, out))
            else:
                P.op(eng, lambda e: e.tensor_scalar(out=out, in0=in0, scalar1=s1, scalar2=s2, op0=op0, op1=op1), r=r, w=w, nowaw=nowaw, cost=cost_ew(eng, out))

        def STT(out, in0, scalar, in1, op0, op1, r, w):
            P.op("dve", lambda e: e.scalar_tensor_tensor(out=out, in0=in0, scalar=scalar, in1=in1, op0=op0, op1=op1), r=r, w=w, cost=cost_ew("dve", out) * 1.6)

        def ACT(out, in_, func, r, w, scale=None, bias=None, accum=None, nowaw=False):
            kw = {}
            if scale is not None:
                kw["scale"] = scale
            if bias is not None:
                kw["bias"] = bias
            if accum is not None:
                kw["accum_out"] = accum
            P.op("act", lambda e: e.activation(out=out, in_=in_, func=func, **kw), r=r, w=w, nowaw=nowaw, cost=cost_ew("act", out))

        def MM(out, lhsT, rhs, start, stop, r, w):
            P.op("pe", lambda e: e.matmul(out, lhsT=lhsT, rhs=rhs, start=start, stop=stop), r=r, w=w,
                 cost=cost_pe(rhs, fp32=(rhs.dtype == F32)))

        def TR(out, in_, ident, r, w):
            P.op("pe", lambda e: e.transpose(out=out, in_=in_, identity=ident), r=r, w=w)

        wA = sb("wA", [128, 8, 2560], BF16)
        b_wA = P.bufs(8, "wA")
        gpre = sb("gpre", [128, 8], F32); b_gpre = P.buf("gpre")
        cmat = sb("cmat", [128, 4, 128], F32); b_cmat = P.buf("cmat")
        masks = sb("masks", [128, 3, 128], BF16); b_masks = P.buf("masks")
        onec = sb("onec", [128, 2], F32); b_onec = P.buf("onec")
        bcv = sb("bcv", [128, 7, 512], F32); b_bcv = P.buf("bcv")
        mubc = sb("mubc", [128, RW], F32); b_mubc = P.buf("mubc")
        lw = sb("lw", [128, 3, 512], BF16); b_lw = P.buf("lw")
        sinkexp = sb("sinkexp", [128, 8], F32); b_sink = P.buf("sink")
        cs = sb("cs", [128, 2, 32, 8], F32); b_cs = P.buf("cs")
        Sf = sb("Sf", [64, 512], F32); b_Sf = P.buf("Sf")
        Sb_ = sb("Sb", [64, 512], BF16); b_Sb = P.buf("Sb")
        ident_bf = C["ident_bf"]; b_ibf = C["b_ident_bf"]
        ident_f = C["ident_f"]; b_if = C["b_ident_f"]

        P.dma("sp", lambda e: e.dma_start(out=gpre[:, :], in_=I["mix_pre_g"]), w=[b_gpre])
        P.dma("sp", lambda e: e.dma_start(out=cmat[:, :, :], in_=I["cmat"]), w=[b_cmat])
        P.dma("pool", lambda e: e.dma_start(out=masks[:, :, :], in_=I["masks"]), w=[b_masks])
        P.dma("sp", lambda e: e.dma_start(out=bcv[:, :, :], in_=I["bcv"]), w=[b_bcv])
        P.dma("sp", lambda e: e.dma_start(out=mubc[:, :], in_=I["mubc"]), w=[b_mubc])
        P.dma("pool", lambda e: e.dma_start(out=lw[:, :, :], in_=I["lw"]), w=[b_lw])
        P.op("dve", lambda e: e.memset(onec[:, :], 1.0), w=[b_onec])
        P.op("dve", lambda e: e.memset(Sf[:, :], 0.0), w=[b_Sf])
        P.op("dve", lambda e: e.memset(Sb_[:, :], 0.0), w=[b_Sb])
        w_in_v = I["w_in"].rearrange("(c p) n -> p c n", p=128)
        for c in range(8):
            P.dma("pool", lambda e, c=c: e.dma_start(out=wA[:, c, :], in_=w_in_v[:, c, 0:2560]), w=[b_wA[c]])

        with ExitStack() as es2:
            def sb2(name, shape, dt):
                return es2.enter_context(nc.sbuf_tensor(f"b1p_{name}", shape, dt))
            posi = sb2("posi", [128, 32], I32); b_posi = P.buf("posi")
            posf = sb2("posf", [128, 32], F32); b_posf = P.buf("posf")
            invf = sb2("invf", [128, 8], F32); b_invf = P.buf("invf")
            ang = sb2("ang", [128, 32, 8], F32); b_ang = P.buf("ang")
            kf = sb2("kf", [128, 32, 8], F32); b_kf = P.buf("kf")
            ki = sb2("ki", [128, 32, 8], I32); b_ki = P.buf("ki")
            yv = sb2("yv", [128, 32, 8], F32); b_yv = P.buf("yv")
            y2 = sb2("y2", [128, 32, 8], F32); b_y2 = P.buf("y2")
            mm_ = sb2("mm", [128, 32, 8], F32); b_mm = P.buf("mm")
            sinks = sb2("sinks", [128, 8], F32); b_sinks = P.buf("sinks")
            P.dma("sp", lambda e: e.dma_start(out=posi[:, :], in_=I["pos_pc"]), w=[b_posi])
            P.dma("sp", lambda e: e.dma_start(out=invf[:, :], in_=I["invfreq"]), w=[b_invf])
            P.dma("sp", lambda e: e.dma_start(out=sinks[:, :], in_=I["sinks"]), w=[b_sinks])
            ACT(sinkexp[:, :], sinks[:, :], AF.Exp, [b_sinks], [b_sink])
            P.op("dve", lambda e: e.tensor_copy(out=posf[:, :], in_=posi[:, :]), r=[b_posi], w=[b_posf])
            TT("dve", ang[:, :, :], bcast(posf[:, :].unsqueeze(2), [128, 32, 8]), bcast(invf[:, :].unsqueeze(1), [128, 32, 8]),
               ALU.mult, [b_posf, b_invf], [b_ang])
            TS("dve", kf[:, :, :], ang[:, :, :], 1.0 / TWO_PI, None, ALU.mult, None, [b_ang], [b_kf])
            P.op("dve", lambda e: e.tensor_copy(out=ki[:, :, :], in_=kf[:, :, :]), r=[b_kf], w=[b_ki])
            P.op("dve", lambda e: e.tensor_copy(out=kf[:, :, :], in_=ki[:, :, :]), r=[b_ki], w=[b_kf])
            C1 = 6.28125
            C2 = TWO_PI - C1
            STT(yv[:, :, :], kf[:, :, :], -C1, ang[:, :, :], ALU.mult, ALU.add, [b_kf, b_ang], [b_yv])
            STT(yv[:, :, :], kf[:, :, :], -C2, yv[:, :, :], ALU.mult, ALU.add, [b_kf, b_yv], [b_yv])

            def wrap(t, bt):
                TS("dve", mm_[:, :, :], t, np.pi, -TWO_PI, ALU.is_gt, ALU.mult, [bt], [b_mm])
                TT("dve", t, t, mm_[:, :, :], ALU.add, [bt, b_mm], [bt])
                TS("dve", mm_[:, :, :], t, -np.pi, TWO_PI, ALU.is_lt, ALU.mult, [bt], [b_mm])
                TT("dve", t, t, mm_[:, :, :], ALU.add, [bt, b_mm], [bt])
            wrap(yv[:, :, :], b_yv)
            ACT(cs[:, 1, :, :], yv[:, :, :], AF.Sin, [b_yv], [b_cs])
            TS("dve", y2[:, :, :], yv[:, :, :], np.pi / 2, None, ALU.add, None, [b_yv], [b_y2])
            wrap(y2[:, :, :], b_y2)
            ACT(cs[:, 0, :, :], y2[:, :, :], AF.Sin, [b_y2], [b_cs], nowaw=True)

            with nc.Block() as block:
                P.emit(block)

        ht0 = sb("ht0", [128, D], F32); ht = [ht0, ht0]; b_ht0 = P.buf("ht"); b_ht = [b_ht0, b_ht0]
        ub = sb("ub", [128, D], BF16); b_ub = P.buf("ub")
        ss = sb("ss", [128, 2], F32); b_ss = P.buf("ss"); b_rs = P.buf("rs")
        uT = sb("uT", [128, 8, 128], BF16); b_uT = P.buf("uT")
        pc0 = sb("pc0", [128, RW], F32); pc = [pc0, pc0]; b_pc0 = P.buf("pc"); b_pc = [b_pc0, b_pc0]
        psh = sb("psh", [128, RW], F32); b_psh = P.buf("psh")
        b_psc = P.bufs(n_tiles + 1, "psc"); b_wslot = P.bufs(2, "wslot")
        pm = sb("pm", [128, RW], F32); b_pm = P.buf("pm")
        lin = sb("lin", [128, 256], BF16); b_lin = P.buf("lin")
        linT = sb("linT", [128, 2, 128], BF16); b_linT = P.buf("linT")
        logw = sb("logw", [128, 512], F32); b_logw = P.buf("logw")
        at = sb("at", [128, 512], F32); b_at = P.buf("at")
        gt = [sb(f"gt{i}", [128, 512], F32) for i in range(2)]; b_gt = P.bufs(2, "gt")
        bons = [sb(f"bon{i}", [128, 512], F32) for i in range(2)]; b_bons = P.bufs(2, "bon")
        tz = sb("tz", [128, 512], F32); b_tz = P.buf("tz")
        st8b = sb("st8b", [128, 3, 8], F32); b_st8b = P.buf("st8b")
        Zb = sb("Zb", [128, 512], BF16); b_Zb = P.buf("Zb")
        kk = sb("kk", [128, 512], F32); b_kk = P.buf("kk")
        t1 = sb("t1", [128, 512], F32); b_t1 = P.buf("t1")
        t2 = sb("t2", [128, 512], F32); b_t2 = P.buf("t2")
        k2 = sb("k2", [128, 512], F32); b_k2 = P.buf("k2")
        bt = sb("bt", [128, 512], F32); b_bt = P.buf("bt")
        st8 = sb("st8", [128, 8, 8], F32); b_st8 = P.buf("st8")
        Wt = sb("Wt", [128, 4, 512], BF16); b_Wt = P.bufs(4, "Wt")
        tok_bf = sb("tokbf", [128, 4, 512], BF16); b_tok4 = P.bufs(4, "tokbf")
        tokEs = [sb(f"tokE{i}", [128, 3, 512], BF16) for i in range(2)]; b_tokEs = [P.bufs(3, f"tokE{i}") for i in range(2)]
        Wc = [sb(f"Wc{i}", [64, 16], F32) for i in range(2)]; b_Wc = P.bufs(2, "Wc")
        ARTs = [sb(f"ART{i}", [64, 8, 2, 128], BF16) for i in range(2)]; b_ARTs = P.bufs(2, "ART")
        BKT = sb("BKT", [64, 2, 8, 128], BF16); b_BKT = P.bufs(2, "BKT")
        X1s = [sb(f"X1{i}", [128, 8, 2, 128], BF16) for i in range(2)]; b_X1s = P.bufs(2, "X1")
        X2s = [sb(f"X2{i}", [128, 8, 2, 128], BF16) for i in range(2)]; b_X2s = P.bufs(2, "X2")
        Pm = [sb(f"Pm{i}", [128, 8, 128], BF16) for i in range(2)]; b_Pm = P.bufs(2, "Pm")
        PTm = [sb(f"PTm{i}", [128, 8, 128], BF16) for i in range(2)]; b_PTm = P.bufs(2, "PTm")
        TTm = [sb(f"TTm{i}", [128, 8, 128], BF16) for i in range(2)]; b_TTm = P.bufs(2, "TTm")
        LVs = [sb(f"LV{i}", [128, 512], BF16) for i in range(2)]; b_LVs = P.bufs(2, "LV")
        Ub = sb("Ub", [128, 512], BF16); b_Ub = P.buf("Ub")
        Ys = sb("Ys", [128, 512], F32); b_Ys = P.buf("Ys")
        zt = t1; b_zt = b_t1
        yo = [sb(f"yo{i}", [128, 1024], BF16) for i in range(2)]; b_yo = [P.bufs(2, f"yo{i}") for i in range(2)]
        qf = sb("qf", [128, 640], F32); b_qf = P.buf("qf")
        rt = sb("rt", [128, 4, 10, 8], F32); b_rt = P.buf("rt")
        qb = sb("qb", [128, 640], BF16); b_qb = P.buf("qb")
        QT = sb("QT", [64, 8, 128], BF16); b_QT = P.buf("QT")
        KTs = [sb(f"KTs{i}", [64, 2, 128], BF16) for i in range(2)]; b_KTs = P.bufs(2, "KTs")
        Va = [sb(f"Va{i}", [128, 2, 65], BF16) for i in range(2)]; b_Va = P.bufs(2, "Va")
        PT_ = sb("PTs", [128, 2, 2, 512], BF16); b_PTs = [P.bufs(2, f"PTs{i}") for i in range(2)]
        den = sb("den", [128, 16], F32); b_den = P.buf("den")

        for i in range(2):
            P.op("pool", lambda e, i=i: e.memset(Va[i][:, :, 64:65], 1.0), w=[b_Va[i]])
        P.op("pool", lambda e: e.memset(psh[:, :], 0.0), w=[b_psh])
        P.dma("sp", lambda e: e.dma_start(out=I["psc"][0:1, :], in_=psh[0:1, :]), r=[b_psh], w=[b_psc[0]], semb=b_psh)

        def bv(i):
            return bcv[:, i, :]

        def v3(ap):
            return ap.rearrange("p (h n) -> p h n", h=8)

        def make_tile(t):
            s = t % 2
            yot = yo[s]
            pcur, bpc = pc[s], b_pc[s]
            r_ = pm[:, 0:512]; k_ = pm[:, 512:1024]; v_ = pm[:, 1024:1536]
            tokE = tokEs[s]
            b_tok = list(b_tok4) + list(b_tokEs[s])

            class _TokView:
                def __getitem__(self, key):
                    p_, i_, c_ = key
                    if isinstance(i_, int) and i_ >= 4:
                        return tokE[p_, i_ - 4, c_]
                    return tok_bf[p_, i_, c_]
            tokv = _TokView()
            vb = tokE[:, 2, :]
            ART, b_ART = ARTs[s], b_ARTs[s]
            X1, b_X1 = X1s[s], b_X1s[s]
            X2, b_X2 = X2s[s], b_X2s[s]
            LV, b_LV = LVs[s], b_LVs[s]
            bon, b_bon = bons[s], b_bons[s]
            TTf, bTTf = TTm[0], b_TTm[0]

            def g_head():
                P.dma("sp", lambda e, s=s, t=t: e.dma_start(out=ht[s][:, :], in_=h1[t * 128:(t + 1) * 128, :]), r=[h1b], w=[b_ht[s]])
                ACT(ub[:, :], ht[s][:, :], AF.Square, [b_ht[s]], [b_ub, b_ss], accum=ss[:, 0:1])
                TS("pool", ss[:, 1:2], ss[:, 0:1], 1.0 / D, EPS, ALU.mult, ALU.add, [b_ss], [b_rs])
                TT("pool", ss[:, 1:2], ss[:, 1:2], C["neg_half"][:, 0:1], ALU.pow, [b_rs, C["b_neg"]], [b_rs])
                ACT(ub[:, :], ht[s][:, :], AF.Copy, [b_ht[s], b_rs], [b_ub], scale=ss[:, 1:2])
                pb, bb = nb()
                pbv = pb[:, :].bitcast(BF16)
                pb2, bb2 = nb()
                pbv2 = pb2[:, :].bitcast(BF16)
                for c in range(4):
                    TR(pbv[:, c * 128:(c + 1) * 128], ub[:, c * 128:(c + 1) * 128], ident_bf[:, :], [b_ub, b_ibf], [bb])
                for c in range(4, 8):
                    TR(pbv2[:, c * 128:(c + 1) * 128], ub[:, c * 128:(c + 1) * 128], ident_bf[:, :], [b_ub, b_ibf], [bb2])
                for c in range(4):
                    ACT(uT[:, c, :], pbv[:, c * 128:(c + 1) * 128], AF.Copy, [bb, b_gpre], [b_uT], scale=gpre[:, c:c + 1], nowaw=(c > 0))
                for c in range(4, 8):
                    TS("dve", uT[:, c, :], pbv2[:, c * 128:(c + 1) * 128], gpre[:, c:c + 1], None, ALU.mult, None, [bb2, b_gpre], [b_uT], nowaw=True)
                pblk = []
                for blk in range(5):
                    pb, bb = nb()
                    for c in range(8):
                        MM(pb[:, :], uT[:, c, :], wA[:, c, blk * 512:(blk + 1) * 512], c == 0, c == 7, [b_uT, b_wA[c]], [bb])
                    pblk.append((pb, bb))
                    if blk < 3:
                        evac(pcur[:, blk * 512:(blk + 1) * 512], pb[:, :], [bb], [bpc], nowaw=(blk > 0))
                    elif blk == 3:
                        evac(pcur[:, 1536:1792], pb[:, 0:256], [bb], [bpc], nowaw=True)
                        evac(qf[:, 0:256], pb[:, 256:512], [bb], [b_qf])
                    else:
                        evac(qf[:, 256:640], pb[:, 0:384], [bb], [b_qf], nowaw=True)
                        for g in range(2):
                            evac(Va[s][:, g, 0:64], pb[:, 384 + g * 64:448 + g * 64], [bb], [b_Va[s]], nowaw=(g > 0))
                P.dma("sp", lambda e: e.dma_start(out=I["psc"][t * 128 + 1:t * 128 + 129, :], in_=pcur[:, :]),
                      r=[bpc], w=[b_psc[t + 1], b_wslot[t % 2]], semb=b_wslot[t % 2])
                yield

            def g_X():
                P.dma("sp", lambda e: e.dma_start(out=psh[:, :], in_=I["psc"][t * 128:(t + 1) * 128, :]),
                      r=[b_psc[t], b_psc[t + 1]], w=[b_psh])
                yield
                for hf_ in range(2):
                    sl = slice(hf_ * 896, (hf_ + 1) * 896)
                    TT("dve", psh[:, sl], psh[:, sl], pcur[:, sl], ALU.subtract, [b_psh, bpc], [b_psh])
                    TT("pool", psh[:, sl], psh[:, sl], mubc[:, sl], ALU.mult, [b_psh, b_mubc], [b_psh])
                    TT("pool" if hf_ == 0 else "dve", pm[:, sl], psh[:, sl], pcur[:, sl], ALU.add, [b_psh, bpc], [b_pm], nowaw=(hf_ > 0))
                yield
                ACT(lin[:, 0:64], pm[:, 1536:1600], AF.Tanh, [b_pm], [b_lin])
                ACT(lin[:, 128:256], pm[:, 1664:1792], AF.Sigmoid, [b_pm], [b_lin], nowaw=True)
                P.op("dve", lambda e: e.tensor_copy(out=lin[:, 64:128], in_=pm[:, 1600:1664]), r=[b_pm], w=[b_lin], nowaw=True)
                yield
                yield
                pb, bb = nb()
                pbv = pb[:, :].bitcast(BF16)
                for j in range(2):
                    TR(pbv[:, j * 128:(j + 1) * 128], lin[:, j * 128:(j + 1) * 128], ident_bf[:, :], [b_lin, b_ibf], [bb])
                evac(linT[:, :, :], pbv[:, 0:256].rearrange("p (j t) -> p j t", j=2), [bb], [b_linT])
                yield
                pz, bz = nb()
                MM(pz[:, :], linT[:, 0, :], lw[:, 0, :], True, True, [b_linT, b_lw], [bz])
                pa, ba = nb()
                MM(pa[:, :], linT[:, 0, :], lw[:, 1, :], True, True, [b_linT, b_lw], [ba])
                pg, bg = nb()
                MM(pg[:, :], linT[:, 1, :], lw[:, 2, :], True, True, [b_linT, b_lw], [bg])
                TT("dve", zt[:, :], pz[:, :], bv(0), ALU.add, [bz, b_bcv], [b_zt])
                ACT(zt[:, :], zt[:, :], AF.Sigmoid, [b_zt], [b_zt])
                ACT(logw[:, :], zt[:, :], AF.Copy, [b_zt], [b_logw], scale=LOGW_SCALE)
                TT("dve", at[:, :], pa[:, :], bv(1), ALU.add, [ba, b_bcv], [b_at])
                ACT(at[:, :], at[:, :], AF.Sigmoid, [b_at], [b_at])
                ACT(gt[s][:, :], pg[:, :], AF.Copy, [bg], [b_gt[s]])
                yield
                TT("dve", kk[:, :], k_, bv(2), ALU.mult, [b_pm, b_bcv], [b_kk])
                TT("pool", t1[:, :], kk[:, :], kk[:, :], ALU.mult, [b_kk], [b_t1])
                P.op("dve", lambda e: e.tensor_reduce(out=st8[:, 0, :], in_=v3(t1[:, :]), axis=AX.X, op=ALU.add), r=[b_t1], w=[b_st8])
                TS("dve", st8[:, 0, :], st8[:, 0, :], 1e-24, None, ALU.max, None, [b_st8], [b_st8])
                TT("pool", st8[:, 0, :], st8[:, 0, :], C["neg_half"][:, 0:8], ALU.pow, [b_st8, C["b_neg"]], [b_st8])
                TT("dve", v3(kk[:, :]), v3(kk[:, :]), bcast(st8[:, 0, :].unsqueeze(2), [128, 8, 64]), ALU.mult, [b_kk, b_st8], [b_kk])
                STT(t1[:, :], at[:, :], -1.0, bv(3), ALU.add, ALU.mult, [b_at, b_bcv], [b_t1])
                STT(k2[:, :], t1[:, :], 1.0, k_, ALU.add, ALU.mult, [b_t1, b_pm], [b_k2])
                TT("pool", bt[:, :], kk[:, :], at[:, :], ALU.mult, [b_kk, b_at], [b_bt])
                TT("pool", t2[:, :], r_, k2[:, :], ALU.mult, [b_pm, b_k2], [b_t2])
                TT("pool", t2[:, :], t2[:, :], bv(4), ALU.mult, [b_t2, b_bcv], [b_t2])
                P.op("dve", lambda e: e.tensor_reduce(out=st8[:, 1, :], in_=v3(t2[:, :]), axis=AX.X, op=ALU.add), r=[b_t2], w=[b_st8])
                yield
                yield
                yield
                pcr, bcr = nb()
                MM(pcr[:, :], cmat[:, 0, :], logw[:, :], True, True, [b_cmat, b_logw], [bcr])
                pcg, bcg = nb()
                MM(pcg[:, :], cmat[:, 1, :], logw[:, :], True, True, [b_cmat, b_logw], [bcg])
                pwc, bwc = nb()
                for h in range(8):
                    MM(pwc[0:64, h:h + 1], logw[:, h * 64:(h + 1) * 64], onec[:, 0:1], True, True, [b_logw, b_onec], [bwc])
                for h in range(8):
                    MM(pwc[0:64, 8 + h:9 + h], logw[0:64, h * 64:(h + 1) * 64], onec[0:64, 0:1], True, True, [b_logw, b_onec], [bwc])
                ACT(Wt[:, 0, :], pcr[:, :], AF.Exp, [bcr], [b_Wt[0]])
                ACT(Wt[:, 1, :], pcr[:, :], AF.Exp, [bcr], [b_Wt[1]], scale=-1.0)
                TT("dve", t1[:, :], pcr[:, :], logw[:, :], ALU.subtract, [bcr, b_logw], [b_t1])
                ACT(Wc[s][:, :], pwc[0:64, 0:16], AF.Exp, [bwc], [b_Wc[s]])
                ACT(Wt[:, 2, :], t1[:, :], AF.Exp, [b_t1], [b_Wt[2]])
                ACT(Wt[:, 3, :], pcg[:, :], AF.Exp, [bcg], [b_Wt[3]])
                yield

            def g_Y():
                TT("pool", tokv[:, 1, :], r_, Wt[:, 0, :], ALU.mult, [b_pm, b_Wt[0]], [b_tok[1]])
                TT("dve", tokv[:, 2, :], bt[:, :], Wt[:, 1, :], ALU.mult, [b_bt, b_Wt[1]], [b_tok[2]])
                TT("pool", tokv[:, 3, :], k2[:, :], Wt[:, 1, :], ALU.mult, [b_k2, b_Wt[1]], [b_tok[3]])
                STT(tokv[:, 0, :], kk[:, :], -1.0, Wt[:, 2, :], ALU.mult, ALU.mult, [b_kk, b_Wt[2]], [b_tok[0]])
                TT("dve", tokv[:, 4, :], bt[:, :], Wt[:, 3, :], ALU.mult, [b_bt, b_Wt[3]], [b_tok[4]])
                TT("pool", tokv[:, 5, :], k2[:, :], Wt[:, 3, :], ALU.mult, [b_k2, b_Wt[3]], [b_tok[5]])
                ACT(tokv[:, 6, :], v_, AF.Copy, [b_pm], [b_tok[6]])
                yield
                for qi in range(4):
                    pb, bb = nb()
                    pbv = pb[:, :].bitcast(BF16)
                    for h in range(8):
                        TR(pbv[0:64, h * 128:(h + 1) * 128], tokv[:, qi, h * 64:(h + 1) * 64], ident_bf[:, :], [b_tok[qi], b_ibf], [bb])
                    src = pbv[0:64, :].rearrange("p (h t) -> p h t", h=8)
                    if qi < 2:
                        evac(ART[:, :, qi, :], src, [bb], [b_ART], nowaw=(qi > 0))
                    else:
                        evac(BKT[:, qi - 2, :, :], src, [bb], [b_BKT[qi - 2]])
                yield
                mU = masks[:, 0:2, :]
                for hp in range(4):
                    yield
                    for (Xd, bXd, srcq) in ((X1, b_X1, 0), (X2, b_X2, 1)):
                        pb, bb = nb()
                        for e_ in range(2):
                            h = hp * 2 + e_
                            MM(pb[:, e_ * 256:(e_ + 1) * 256], BKT[:, srcq, h, :], ART[:, h, :, :].rearrange("p a t -> p (a t)"),
                               True, True, [b_BKT[srcq], b_ART], [bb])
                        TT("dve", Xd[:, hp * 2:hp * 2 + 2, :, :], pb[:, :].rearrange("p (e a t) -> p e a t", e=2, a=2),
                           bcast(mU.unsqueeze(1), [128, 2, 2, 128]), ALU.mult, [bb, b_masks], [bXd], nowaw=(hp > 0))
                for hq in range(2):
                    pb, bb = nb()
                    for e_ in range(4):
                        h = hq * 4 + e_
                        MM(pb[:, e_ * 128:(e_ + 1) * 128], ART[:, h, 0, :], BKT[:, 0, h, :], True, True, [b_ART, b_BKT[0]], [bb])
                    TT("dve", Pm[0][:, hq * 4:hq * 4 + 4, :], pb[:, :].rearrange("p (e t) -> p e t", e=4),
                       bcast(masks[:, 2, :].unsqueeze(1), [128, 4, 128]), ALU.mult, [bb, b_masks], [b_Pm[0]], nowaw=(hq > 0))
                P.op("dve", lambda e: e.tensor_copy(out=PTm[0][:, :, :], in_=X1[:, :, 0, :]), r=[b_X1], w=[b_PTm[0]])
                TT("pool", TTm[0][:, :, :], X1[:, :, 0, :], bcast(ident_bf[:, :].unsqueeze(1), [128, 8, 128]), ALU.add,
                   [b_X1, b_ibf], [b_TTm[0]])
                yield
                pb, bb = nb()
                for h in range(8):
                    MM(pb[:, h * 64:(h + 1) * 64], X2[:, h, 0, :], vb[:, h * 64:(h + 1) * 64], True, True, [b_X2, b_tok[6]], [bb])
                evac(LV[:, :], pb[:, :], [bb], [b_LV])
                yield
                TT("dve", v3(bon[:, :]), v3(v_), bcast(st8[:, 1, :].unsqueeze(2), [128, 8, 64]), ALU.mult, [b_pm, b_st8], [b_bon])
                yield

            def g_D():
                cur = 0
                pend = None

                def tt_update(lvl_, pslot):
                    tcur = (lvl_ - 1) % 2
                    tnxt = 1 - tcur
                    for hq in range(2):
                        pb, bb = banks[6 + hq], b_banks[6 + hq]
                        for e_ in range(4):
                            h = hq * 4 + e_
                            osl = pb[:, e_ * 128:(e_ + 1) * 128]
                            if lvl_ == 1:
                                P.op("pe", lambda e, osl=osl, h=h: e.matmul(osl, lhsT=ident_bf[:, :], rhs=TTm[tcur][:, h, :],
                                                                           start=(h % 4 == 0), stop=False, skip_group_check=True),
                                     r=[b_ibf, b_TTm[tcur]], w=[bb], cost=0.13)
                            P.op("pe", lambda e, osl=osl, h=h: e.matmul(osl, lhsT=Pm[pslot][:, h, :], rhs=TTm[tcur][:, h, :],
                                                                       start=False, stop=True, skip_group_check=True),
                                 r=[b_Pm[pslot], b_TTm[tcur]], w=[bb], cost=0.13)
                        evac(TTm[tnxt][:, hq * 4:hq * 4 + 4, :], pb[:, :].rearrange("p (e t) -> p e t", e=4), [bb], [b_TTm[tnxt]], nowaw=(hq > 0))
                for lvl in range(1, 7):
                    yield
                    nxt = 1 - cur
                    last = (lvl == 6)
                    for hq in range(2):
                        pbP, bbP = nb()
                        for e_ in range(4):
                            h = hq * 4 + e_
                            MM(pbP[:, e_ * 128:(e_ + 1) * 128], PTm[cur][:, h, :], Pm[cur][:, h, :], True, True, [b_PTm[cur], b_Pm[cur]], [bbP])
                        if not last:
                            pbT, bbT = nb()
                            for e_ in range(4):
                                h = hq * 4 + e_
                                MM(pbT[:, e_ * 128:(e_ + 1) * 128], Pm[cur][:, h, :], PTm[cur][:, h, :], True, True, [b_PTm[cur], b_Pm[cur]], [bbT])
                        if hq == 1 and pend is not None:
                            tt_update(pend, cur)
                        evac(Pm[nxt][:, hq * 4:hq * 4 + 4, :], pbP[:, :].rearrange("p (e t) -> p e t", e=4), [bbP], [b_Pm[nxt]], nowaw=(hq > 0))
                        if not last:
                            evac(PTm[nxt][:, hq * 4:hq * 4 + 4, :], pbT[:, :].rearrange("p (e t) -> p e t", e=4), [bbT], [b_PTm[nxt]], nowaw=(hq > 0))
                    pend = lvl
                    cur = nxt
                yield
                tt_update(6, cur)
                yield

            def g_Z1():
                TT("dve", v3(Sb_[:, :]), v3(Sf[:, :]), bcast(Wc[s][:, 8:16].unsqueeze(2), [64, 8, 64]), ALU.mult, [b_Sf, b_Wc[s]], [b_Sb])
                pz_, bz_ = nb()
                for h in range(8):
                    MM(pz_[:, h * 64:(h + 1) * 64], ART[:, h, 0, :], Sb_[:, h * 64:(h + 1) * 64], True, True, [b_ART, b_Sb], [bz_])
                TT("dve", Zb[:, :], pz_[:, :], LV[:, :], ALU.add, [bz_, b_LV], [b_Zb])
                pb, bb = nb()
                for h in range(8):
                    MM(pb[:, h * 64:(h + 1) * 64], TTf[:, h, :], Zb[:, h * 64:(h + 1) * 64], True, True, [bTTf, b_Zb], [bb])
                evac(Ub[:, :], pb[:, :], [bb], [b_Ub])
                py, by = nb()
                for h in range(8):
                    hs = slice(h * 64, (h + 1) * 64)
                    MM(py[:, hs], ART[:, h, 1, :], Sb_[:, hs], True, False, [b_ART, b_Sb], [by])
                    MM(py[:, hs], X1[:, h, 1, :], Ub[:, hs], False, False, [b_X1, b_Ub], [by])
                    MM(py[:, hs], X2[:, h, 1, :], vb[:, hs], False, True, [b_X2, b_tok[6]], [by])
                pst, bst = nb()
                for h in range(8):
                    hs = slice(h * 64, (h + 1) * 64)
                    MM(pst[0:64, hs], tokv[:, 4, hs], Ub[:, hs], True, False, [b_tok[4], b_Ub], [bst])
                    MM(pst[0:64, hs], tokv[:, 5, hs], vb[:, hs], False, True, [b_tok[5], b_tok[6]], [bst])
                ACT(Ys[:, :], py[:, :], AF.Copy, [by], [b_Ys])
                TT("pool", v3(Sf[:, :]), v3(Sf[:, :]), bcast(Wc[s][:, 0:8].unsqueeze(2), [64, 8, 64]), ALU.mult, [b_Sf, b_Wc[s]], [b_Sf])
                TT("dve", Sf[:, :], Sf[:, :], pst[0:64, :], ALU.add, [b_Sf, bst], [b_Sf])
                yield

            def g_Z2():
                P.op("dve", lambda e: e.tensor_reduce(out=st8b[:, 0, :], in_=v3(Ys[:, :]), axis=AX.X, op=ALU.add), r=[b_Ys], w=[b_st8b])
                TT("pool", tz[:, :], Ys[:, :], Ys[:, :], ALU.mult, [b_Ys], [b_tz])
                P.op("dve", lambda e: e.tensor_reduce(out=st8b[:, 1, :], in_=v3(tz[:, :]), axis=AX.X, op=ALU.add), r=[b_tz], w=[b_st8b])
                TS("dve", st8b[:, 0, :], st8b[:, 0, :], 1.0 / 64, None, ALU.mult, None, [b_st8b], [b_st8b])
                TT("dve", st8b[:, 2, :], st8b[:, 0, :], st8b[:, 0, :], ALU.mult, [b_st8b], [b_st8b])
                STT(st8b[:, 1, :], st8b[:, 1, :], 1.0 / 64, st8b[:, 2, :], ALU.mult, ALU.subtract, [b_st8b], [b_st8b])
                TS("dve", st8b[:, 1, :], st8b[:, 1, :], 64e-5, None, ALU.add, None, [b_st8b], [b_st8b])
                TT("pool", st8b[:, 1, :], st8b[:, 1, :], C["neg_half"][:, 0:8], ALU.pow, [b_st8b, C["b_neg"]], [b_st8b])
                TT("dve", v3(Ys[:, :]), v3(Ys[:, :]), bcast(st8b[:, 0, :].unsqueeze(2), [128, 8, 64]), ALU.subtract, [b_Ys, b_st8b], [b_Ys])
                TT("dve", v3(Ys[:, :]), v3(Ys[:, :]), bcast(st8b[:, 1, :].unsqueeze(2), [128, 8, 64]), ALU.mult, [b_Ys, b_st8b], [b_Ys])
                TT("pool", Ys[:, :], Ys[:, :], bv(5), ALU.mult, [b_Ys, b_bcv], [b_Ys])
                TT("pool", Ys[:, :], Ys[:, :], bv(6), ALU.add, [b_Ys, b_bcv], [b_Ys])
                TT("pool", Ys[:, :], Ys[:, :], bon[:, :], ALU.add, [b_Ys, b_bon], [b_Ys])
                TT("dve", yot[:, 0:512], Ys[:, :], gt[s][:, :], ALU.mult, [b_Ys, b_gt[s]], [b_yo[s][0]])
                yield

            def g_swa():
                cosb = bcast(cs[:, 0, t, :].unsqueeze(1), [128, 10, 8])
                sinb = bcast(cs[:, 1, t, :].unsqueeze(1), [128, 10, 8])
                q4 = qf[:, :].rearrange("p (h n) -> p h n", h=10)
                x1 = q4[:, :, 0:8]; x2 = q4[:, :, 8:16]
                TT("dve", rt[:, 0, :, :], x1, cosb, ALU.mult, [b_qf, b_cs], [b_rt])
                TT("pool", rt[:, 1, :, :], x2, sinb, ALU.mult, [b_qf, b_cs], [b_rt], nowaw=True)
                TT("dve", rt[:, 2, :, :], x2, cosb, ALU.mult, [b_qf, b_cs], [b_rt], nowaw=True)
                TT("pool", rt[:, 3, :, :], x1, sinb, ALU.mult, [b_qf, b_cs], [b_rt], nowaw=True)
                TT("dve", x1, rt[:, 0, :, :], rt[:, 1, :, :], ALU.subtract, [b_rt], [b_qf])
                TT("dve", x2, rt[:, 2, :, :], rt[:, 3, :, :], ALU.add, [b_rt], [b_qf], nowaw=True)
                ACT(qb[:, :], qf[:, :], AF.Copy, [b_qf], [b_qb])
                pb, bb = nb()
                pbv = pb[:, :].bitcast(BF16)
                for h in range(8):
                    TR(pbv[0:64, h * 128:(h + 1) * 128], qb[:, h * 64:(h + 1) * 64], ident_bf[:, :], [b_qb, b_ibf], [bb])
                evac(QT[:, :, :], pbv[0:64, :].rearrange("p (h t) -> p h t", h=8), [bb], [b_QT])
                pb, bb = nb()
                pbv = pb[:, :].bitcast(BF16)
                for g in range(2):
                    TR(pbv[0:64, g * 128:(g + 1) * 128], qb[:, 512 + g * 64:512 + (g + 1) * 64], ident_bf[:, :], [b_qb, b_ibf], [bb])
                evac(KTs[s][:, :, :], pbv[0:64, 0:256].rearrange("p (g t) -> p g t", g=2), [bb], [b_KTs[s]])
                kbs = [1] if t == 0 else [0, 1]
                yield
                for g in range(2):
                    for kb in kbs:
                        ksl = s if kb == 1 else 1 - s
                        pb, bb = nb()
                        MM(pb[:, :], KTs[ksl][:, g, :], QT[:, 4 * g:4 * g + 4, :].rearrange("p h t -> p (h t)"), True, True,
                           [b_KTs[ksl], b_QT], [bb])
                        ACT(PT_[:, kb, g, :], pb[:, :], AF.Exp, [bb], [b_PTs[kb][g]], scale=0.125)
                        mk = masks[:, 1, :] if kb == 1 else masks[:, 2, :]
                        TT("dve", PT_[:, kb, g, :].rearrange("p (h t) -> p h t", h=4), PT_[:, kb, g, :].rearrange("p (h t) -> p h t", h=4),
                           bcast(mk.unsqueeze(1), [128, 4, 128]), ALU.mult, [b_PTs[kb][g], b_masks], [b_PTs[kb][g]])
                po = []
                yield
                for g in range(2):
                    pb, bb = nb()
                    po.append((pb, bb))
                    for hh in range(4):
                        for i_, kb in enumerate(kbs):
                            vsl = s if kb == 1 else 1 - s
                            MM(pb[:, hh * 65:(hh + 1) * 65], PT_[:, kb, g, hh * 128:(hh + 1) * 128], Va[vsl][:, g, :],
                               i_ == 0, i_ == len(kbs) - 1, [b_PTs[kb][g], b_Va[vsl]], [bb])
                for g in range(2):
                    pb, bb = po[g]
                    o3 = pb[:, 0:260].rearrange("p (h n) -> p h n", h=4)
                    TT("dve", den[:, 4 * g:4 * g + 4], o3[:, :, 64], sinkexp[:, 4 * g:4 * g + 4], ALU.add, [bb, b_sink], [b_den], nowaw=(g > 0))
                P.op("dve", lambda e: e.reciprocal(out=den[:, 8:16], in_=den[:, 0:8]), r=[b_den], w=[b_den])
                for g in range(2):
                    pb, bb = po[g]
                    o3 = pb[:, 0:260].rearrange("p (h n) -> p h n", h=4)
                    TT("dve", yot[:, 512 + g * 256:512 + (g + 1) * 256].rearrange("p (h n) -> p h n", h=4), o3[:, :, 0:64],
                       bcast(den[:, 8 + 4 * g:12 + 4 * g].unsqueeze(2), [128, 4, 64]), ALU.mult, [bb, b_den], [b_yo[s][1]], nowaw=(g > 0))

                yield

            def g_store():
                P.dma("sp", lambda e, s=s, t=t: e.dma_start(out=ys[t * 128:(t + 1) * 128, 0:1024], in_=yo[s][:, :]),
                      r=[b_yo[s][0], b_yo[s][1]], w=[ysb], nowaw=True, semb=b_yo[s][0])

                yield
            return dict(head=g_head, X=g_X, Y=g_Y, D=g_D, Z1=g_Z1, Z2=g_Z2, swa=g_swa, store=g_store)

        def drain(gens):
            gens = [g for g in gens]
            while gens:
                nxt = []
                for g in gens:
                    try:
                        next(g)
                        nxt.append(g)
                    except StopIteration:
                        pass
                gens = nxt

        def chain(*gs):
            for g_ in gs:
                yield from g_

        T_ = [make_tile(t) for t in range(n_tiles)]
        drain([T_[0]["head"]()])
        drain([T_[0]["X"](), T_[0]["swa"]()])
        if n_tiles > 1:
            drain([T_[1]["head"]()])
        drain([T_[0]["Y"]()])
        for t in range(n_tiles):
            g = [chain(T_[t]["D"](), T_[t]["Z1"](), T_[t]["Z2"](), T_[t]["store"]())]
            if t + 1 < n_tiles:
                g.append(chain(T_[t + 1]["X"](), T_[t + 1]["Y"]()))
                s3 = [T_[t + 1]["swa"]()]
                if t + 2 < n_tiles:
                    s3.append(T_[t + 2]["head"]())
                g.append(chain(*s3))
            drain(g)

        with nc.Block() as block:
            P.emit(block)


def host_consts():
    s = np.arange(128)[:, None]
    t = np.arange(128)[None, :]
    one = np.ones((128, 128), np.float32)
    M_rel = (s <= t).astype(np.float32) - (s <= 63).astype(np.float32) * one
    M_gt = (s > t).astype(np.float32)
    Shift = (s == t - 1).astype(np.float32)
    Sel = ((s == 127) & (t == 0)).astype(np.float32)
    cmat = np.ascontiguousarray(np.stack([M_rel, M_gt, Shift, Sel], 1).astype(np.float32))
    masks = np.ascontiguousarray(np.stack([(s < t) * one, (s <= t) * one, (t < s) * one], 1).astype(np.float32))
    invf = (500000.0 ** (-np.arange(0, 16, 2, dtype=np.float32) / 16)).astype(np.float32)
    return cmat, masks, np.ascontiguousarray(np.broadcast_to(invf[None], (128, 8)))


def host_maps(inp, cores=range(8)):
    g = lambda k: np.asarray(inp[k][0], np.float32)
    cmat, masks, invf = host_consts()
    bcv = np.stack([bc(g(k).reshape(-1)) for k in ("rwkv_w0", "rwkv_a0", "rwkv_k_k", "rwkv_k_a", "rwkv_r_k", "rwkv_gn_g", "rwkv_gn_b")], 1)
    lw = np.zeros((128, 3, 512), np.float32)
    lw[0:64, 0, :] = g("rwkv_w_up")
    lw[64:128, 1, :] = g("rwkv_a_up")
    lw[:, 2, :] = g("rwkv_g_up")
    shared = {
        "ident": np.eye(128, dtype=np.float32),
        "ffn1_w_in": np.ascontiguousarray(g("ffn1_w_in")), "ffn1_w_out": np.ascontiguousarray(g("ffn1_w_out")),
        "ffn1_pre_g": pc(g("ffn1_pre_g"), 8), "ffn1_post_g": bc(g("ffn1_post_g")),
        "ffn2_w_in": np.ascontiguousarray(g("ffn2_w_in")), "ffn2_w_out": np.ascontiguousarray(g("ffn2_w_out")),
        "ffn2_pre_g": pc(g("ffn2_pre_g"), 8), "ffn2_post_g": bc(g("ffn2_post_g")),
        "w_in": np.ascontiguousarray(g("w_in")), "mix_pre_g": pc(g("mix_pre_g"), 8),
        "cmat": cmat, "masks": masks, "bcv": np.ascontiguousarray(bcv), "mubc": bc(g("rwkv_mu")), "lw": lw,
        "invfreq": invf, "sinks": bc(g("swa_sinks")), "mem_norm_g": pc(g("mem_norm_g"), 8),
        "mem_w_kv": np.ascontiguousarray(g("mem_w_kv")),
        "gate_b": bc(g("gate_b")), "w_branch": np.ascontiguousarray(g("w_branch").reshape(1536, 1024)),
        "w_out": np.ascontiguousarray(g("w_out")), "mix_post_g": bc(g("mix_post_g")),
    }
    maps = []
    for b in cores:
        m = dict(shared)
        m["x"] = np.ascontiguousarray(np.asarray(inp["x"][b], np.float32))
        m["mem"] = np.ascontiguousarray(np.asarray(inp["mem"][b], np.float32))
        m["pos_pc"] = np.ascontiguousarray(np.asarray(inp["positions"][b], np.int32).reshape(32, 128).T)
        maps.append(m)
    return maps


def mixer_b2(nc, P, C, h1, h1b, ys, ysb, h2, h2b, mem, I, n_tiles=32):
    with ExitStack() as es:
        def sb(name, shape, dt):
            return es.enter_context(nc.sbuf_tensor(f"b2_{name}", shape, dt))
        banks = [es.enter_context(nc.psum_tensor(f"b2_ps{i}", [128, 512], F32)) for i in range(8)]
        b_banks = P.bufs(8, "b2ps", True)
        rr = [0]

        def nb():
            i = rr[0] % 8
            rr[0] += 1
            return banks[i], b_banks[i]

        def TT(eng, out, in0, in1, op, r, w, nowaw=False):
            P.op(eng, lambda e: e.tensor_tensor(out=out, in0=in0, in1=in1, op=op), r=r, w=w, nowaw=nowaw, cost=cost_ew(eng, out))

        def TS(eng, out, in0, s1, s2, op0, op1, r, w, nowaw=False):
            if op1 is None:
                P.op(eng, lambda e: e.tensor_scalar(out=out, in0=in0, scalar1=s1, scalar2=None, op0=op0), r=r, w=w, nowaw=nowaw, cost=cost_ew(eng, out))
            else:
                P.op(eng, lambda e: e.tensor_scalar(out=out, in0=in0, scalar1=s1, scalar2=s2, op0=op0, op1=op1), r=r, w=w, nowaw=nowaw, cost=cost_ew(eng, out))

        def ACT(out, in_, func, r, w, scale=None, accum=None, nowaw=False):
            kw = {}
            if scale is not None:
                kw["scale"] = scale
            if accum is not None:
                kw["accum_out"] = accum
            P.op("act", lambda e: e.activation(out=out, in_=in_, func=func, **kw), r=r, w=w, nowaw=nowaw, cost=cost_ew("act", out))

        tog = [0]

        def evac(out, in_, r, w, nowaw=False):
            tog[0] ^= 1
            if tog[0]:
                P.op("act", lambda e: e.activation(out=out, in_=in_, func=AF.Copy), r=r, w=w, nowaw=nowaw, cost=cost_ew("act", out))
            else:
                P.op("dve", lambda e: e.tensor_copy(out=out, in_=in_), r=r, w=w, nowaw=nowaw, cost=cost_ew("dve", out))

        def MM(out, lhsT, rhs, start, stop, r, w):
            P.op("pe", lambda e: e.matmul(out, lhsT=lhsT, rhs=rhs, start=start, stop=stop), r=r, w=w,
                 cost=cost_pe(rhs, fp32=(rhs.dtype == F32)))

        def TR(out, in_, ident, r, w):
            P.op("pe", lambda e: e.transpose(out=out, in_=in_, identity=ident), r=r, w=w)

        ident_bf = C["ident_bf"]; b_ibf = C["b_ident_bf"]
        wQ = sb("wQ", [128, 8, 512], BF16); b_wQ = P.buf("wQ")
        KmT = sb("KmT", [128, 4, 256], BF16); b_KmT = P.buf("KmT")
        Vm = sb("Vm", [128, 2, 4, 129], BF16); b_Vm = P.buf("Vm")
        qmb = sb("qmb", [128, 512], BF16); b_qmb = P.buf("qmb")
        QmT = sb("QmT", [128, 4, 128], BF16); b_QmT = P.buf("QmT")
        PmT = sb("PmT", [128, 2, 4, 128], BF16); b_PmT = P.bufs(2, "PmT")
        den = sb("den", [128, 4], F32); b_den = P.buf("den")
        with ExitStack() as es2:
            def sb2(name, shape, dt):
                return es2.enter_context(nc.sbuf_tensor(f"b2p_{name}", shape, dt))
            mt = sb2("mt", [128, 2, D], F32); b_mt = P.bufs(2, "mt")
            mnb = sb2("mnb", [128, 2, D], BF16); b_mnb = P.bufs(2, "mnb")
            memT = sb2("memT", [128, 8, 256], BF16); b_memT = P.buf("memT")
            wkv = sb2("wkv", [128, 8, 1024], BF16); b_wkv = P.buf("wkv")
            gmem = sb2("gmem", [128, 8], F32); b_gmem = P.buf("gmem")
            mss = sb2("mss", [128, 2], F32); b_mss = P.bufs(2, "mss")
            mrs = sb2("mrs", [128, 2], F32); b_mrs = P.buf("mrs")
            P.dma("sp", lambda e: e.dma_start(out=gmem[:, :], in_=I["mem_norm_g"]), w=[b_gmem])
            wkv_v = I["mem_w_kv"].rearrange("(c p) n -> p c n", p=128)
            for c in range(8):
                P.dma("pool", lambda e, c=c: e.dma_start(out=wkv[:, c, :], in_=wkv_v[:, c, :]), w=[b_wkv], nowaw=True)
            for j in range(2):
                P.dma("sp", lambda e, j=j: e.dma_start(out=mt[:, j, :], in_=mem[j * 128:(j + 1) * 128, :]), w=[b_mt[j]])
                ACT(mnb[:, j, :], mt[:, j, :], AF.Square, [b_mt[j]], [b_mnb[j], b_mss[j]], accum=mss[:, j:j + 1])
            TS("pool", mrs[:, :], mss[:, :], 1.0 / D, EPS, ALU.mult, ALU.add, [b_mss[0], b_mss[1]], [b_mrs])
            TT("pool", mrs[:, :], mrs[:, :], C["neg_half"][:, 0:2], ALU.pow, [b_mrs, C["b_neg"]], [b_mrs])
            for j in range(2):
                ACT(mnb[:, j, :], mt[:, j, :], AF.Copy, [b_mt[j], b_mrs], [b_mnb[j]], scale=mrs[:, j:j + 1])
            for half in range(2):
                pb, bb = nb()
                pbv = pb[:, :].bitcast(BF16)
                for cc in range(4):
                    c = half * 4 + cc
                    for j in range(2):
                        TR(pbv[:, cc * 256 + j * 128: cc * 256 + (j + 1) * 128], mnb[:, j, c * 128:(c + 1) * 128],
                           ident_bf[:, :], [b_mnb[j], b_ibf], [bb])
                for cc in range(4):
                    c = half * 4 + cc
                    ACT(memT[:, c, :], pbv[:, cc * 256:(cc + 1) * 256], AF.Copy, [bb, b_gmem], [b_memT],
                        scale=gmem[:, c:c + 1], nowaw=True)
            for h in range(4):
                pb, bb = nb()
                for c in range(8):
                    MM(pb[:, 0:256], wkv[:, c, h * 128:(h + 1) * 128], memT[:, c, :], c == 0, c == 7, [b_wkv, b_memT], [bb])
                evac(KmT[:, h, :], pb[:, 0:256], [bb], [b_KmT], nowaw=True)
            P.op("dve", lambda e: e.memset(Vm[:, :, :, 128:129], 1.0), w=[b_Vm])
            for mb in range(2):
                pb, bb = nb()
                for c in range(8):
                    MM(pb[:, :], memT[:, c, mb * 128:(mb + 1) * 128], wkv[:, c, 512:1024], c == 0, c == 7, [b_wkv, b_memT], [bb])
                evac(Vm[:, mb, :, 0:128], pb[:, :].rearrange("p (h d) -> p h d", h=4), [bb], [b_Vm], nowaw=True)
            with nc.Block() as block:
                P.emit(block)
        wG = sb("wG", [128, 8, 3072], BF16); b_wG = P.bufs(8, "wG")
        wB = sb("wB", [128, 12, D], BF16); b_wB = P.buf("wB")
        wO = sb("wO", [128, 8, D], BF16); b_wO = P.buf("wO")
        gbb = sb("gbb", [128, 3072], F32); b_gbb = P.buf("gbb")
        gpost = sb("gpost", [128, D], F32); b_gpost = P.buf("gpost")
        gpre = sb("gpre", [128, 8], F32); b_gpre = P.buf("gpre")
        ht = [sb(f"ht{i}", [128, D], F32) for i in range(2)]; b_ht = P.bufs(2, "ht")
        yt = [sb(f"yt{i}", [128, 1536], BF16) for i in range(2)]; b_yt = P.bufs(2, "yt"); b_yc = P.bufs(2, "yc")
        ub = sb("ub", [128, D], BF16); b_ub = P.buf("ub")
        ss = sb("ss", [128, 8], F32); b_ss = P.buf("ss"); b_rs = P.buf("rs"); b_ss2 = P.bufs(2, "ss2"); b_rs2 = P.buf("rs2")
        uT = [sb(f"uT{i}", [128, 8, 128], BF16) for i in range(2)]; b_uT = P.bufs(2, "uT")
        gates = [sb(f"gates{i}", [128, 3072], F32) for i in range(2)]; b_gates = [P.bufs(6, f"gates{i}") for i in range(2)]
        yT = [sb(f"yT{i}", [128, 12, 128], BF16) for i in range(2)]; b_yT = [P.bufs(3, f"yT{i}") for i in range(2)]
        merged = sb("merged", [128, D], F32); b_mg = P.bufs(2, "mg")
        tmpm = sb("tmpm", [128, 512], F32); b_tmpm = P.buf("tmpm")
        mbf = sb("mbf", [128, D], BF16); b_mbf = P.bufs(2, "mbf")
        mT = sb("mT", [128, 8, 128], BF16); b_mT = P.buf("mT")
        tmpf = sb("tmpf", [128, D], F32); b_tmpf = P.bufs(2, "tmpf")
        junk = sb("junk", [128, 512], BF16); b_junk = P.buf("junk")

        w_in_v = I["w_in"].rearrange("(c p) n -> p c n", p=128)
        for c in range(8):
            P.dma("pool", lambda e, c=c: e.dma_start(out=wQ[:, c, :], in_=w_in_v[:, c, 2560:3072]), w=[b_wQ], nowaw=True)
        wb_v = I["w_branch"].rearrange("(c p) n -> p c n", p=128)
        wo_v = I["w_out"].rearrange("(c p) n -> p c n", p=128)
        P.dma("sp", lambda e: e.dma_start(out=gpre[:, :], in_=I["mix_pre_g"]), w=[b_gpre])
        P.dma("sp", lambda e: e.dma_start(out=gbb[:, :], in_=I["gate_b"]), w=[b_gbb])
        P.dma("sp", lambda e: e.dma_start(out=gpost[:, :], in_=I["mix_post_g"]), w=[b_gpost])
        for c in range(8):
            P.dma("pool", lambda e, c=c: e.dma_start(out=wG[:, c, :], in_=w_in_v[:, c, 3072:6144]), w=[b_wG[c]])
        for c in range(12):
            P.dma("pool", lambda e, c=c: e.dma_start(out=wB[:, c, :], in_=wb_v[:, c, :]), w=[b_wB], nowaw=True)
        for c in range(8):
            P.dma("pool", lambda e, c=c: e.dma_start(out=wO[:, c, :], in_=wo_v[:, c, :]), w=[b_wO], nowaw=True)

        def load(t):
            s = t % 2
            P.dma("sp", lambda e: e.dma_start(out=ht[s][:, :], in_=h1[t * 128:(t + 1) * 128, :]), r=[h1b], w=[b_ht[s]])
            P.dma("sp", lambda e: e.dma_start(out=yt[s][:, 0:1024], in_=ys[t * 128:(t + 1) * 128, 0:1024]), r=[ysb], w=[b_yt[s]])

        def pro(t):
            s = t % 2
            ACT(ub[:, :], ht[s][:, :], AF.Square, [b_ht[s]], [b_ub, b_ss], accum=ss[:, 0:1])
            TS("pool", ss[:, 1:2], ss[:, 0:1], 1.0 / D, EPS, ALU.mult, ALU.add, [b_ss], [b_rs])
            TT("pool", ss[:, 1:2], ss[:, 1:2], C["neg_half"][:, 0:1], ALU.pow, [b_rs, C["b_neg"]], [b_rs])
            ACT(ub[:, :], ht[s][:, :], AF.Copy, [b_ht[s], b_rs], [b_ub], scale=ss[:, 1:2])
            yield
            yield
            yield
            pb, bb = nb()
            pbv = pb[:, :].bitcast(BF16)
            for c in range(8):
                TR(pbv[:, c * 128:(c + 1) * 128], ub[:, c * 128:(c + 1) * 128], ident_bf[:, :], [b_ub, b_ibf], [bb])
            for c in range(8):
                ACT(uT[s][:, c, :], pbv[:, c * 128:(c + 1) * 128], AF.Copy, [bb, b_gpre], [b_uT[s]], scale=gpre[:, c:c + 1], nowaw=(c > 0))
            yield
            for i in range(2):
                pb, bb = nb()
                pbv = pb[:, :].bitcast(BF16)
                for c in range(4):
                    TR(pbv[:, c * 128:(c + 1) * 128], yt[s][:, i * 512 + c * 128:i * 512 + (c + 1) * 128], ident_bf[:, :], [b_yt[s], b_ibf], [bb])
                if i % 2 == 0:
                    ACT(yT[s][:, 4 * i:4 * i + 4, :], pbv[:, 0:512].rearrange("p (c t) -> p c t", c=4), AF.Copy, [bb], [b_yT[s][i]])
                else:
                    P.op("dve", lambda e, i=i, pbv=pbv: e.tensor_copy(out=yT[s][:, 4 * i:4 * i + 4, :], in_=pbv[:, 0:512].rearrange("p (c t) -> p c t", c=4)),
                         r=[bb], w=[b_yT[s][i]])

        def memgen(t):
            s = t % 2
            yot = None
            pb, bb = nb()
            for c in range(8):
                MM(pb[:, :], uT[s][:, c, :], wQ[:, c, :], c == 0, c == 7, [b_uT[s], b_wQ], [bb])
            evac(qmb[:, :], pb[:, :], [bb], [b_qmb])
            yield
            pb, bb = nb()
            pbv = pb[:, :].bitcast(BF16)
            for h in range(4):
                TR(pbv[:, h * 128:(h + 1) * 128], qmb[:, h * 128:(h + 1) * 128], ident_bf[:, :], [b_qmb, b_ibf], [bb])
            evac(QmT[:, :, :], pbv[:, 0:512].rearrange("p (h t) -> p h t", h=4), [bb], [b_QmT])
            for mb in range(2):
                pb, bb = nb()
                for h in range(4):
                    MM(pb[:, h * 128:(h + 1) * 128], KmT[:, h, mb * 128:(mb + 1) * 128], QmT[:, h, :], True, True, [b_KmT, b_QmT], [bb])
                ACT(PmT[:, mb, :, :], pb[:, :].rearrange("p (h t) -> p h t", h=4), AF.Exp, [bb], [b_PmT[mb]], scale=float(128 ** -0.5))
            pom = []
            yield
            for hp in range(2):
                pb, bb = nb()
                pom.append((pb, bb))
                for e_ in range(2):
                    h = hp * 2 + e_
                    for mb in range(2):
                        MM(pb[:, e_ * 129:(e_ + 1) * 129], PmT[:, mb, h, :], Vm[:, mb, h, :], mb == 0, mb == 1, [b_PmT[mb], b_Vm], [bb])
            for hp in range(2):
                pb, bb = pom[hp]
                o3 = pb[:, 0:258].rearrange("p (h n) -> p h n", h=2)
                P.op("dve", lambda e, o3=o3, hp=hp: e.reciprocal(out=den[:, 2 * hp:2 * hp + 2], in_=o3[:, :, 128]), r=[bb], w=[b_den])
                TT("dve", yt[s][:, 1024 + hp * 256:1024 + (hp + 1) * 256].rearrange("p (h n) -> p h n", h=2), o3[:, :, 0:128],
                   bcast(den[:, 2 * hp:2 * hp + 2].unsqueeze(2), [128, 2, 128]), ALU.mult, [bb, b_den], [b_yc[s]], nowaw=(hp > 0))
            yield


            pb, bb = nb()
            pbv = pb[:, :].bitcast(BF16)
            for c in range(4):
                TR(pbv[:, c * 128:(c + 1) * 128], yt[s][:, 1024 + c * 128:1024 + (c + 1) * 128], ident_bf[:, :], [b_yc[s], b_ibf], [bb])
            ACT(yT[s][:, 8:12, :], pbv[:, 0:512].rearrange("p (c t) -> p c t", c=4), AF.Copy, [bb], [b_yT[s][2]])
            yield

        def G(t):
            s = t % 2
            yield from pro(t)
            yield
            yield from memgen(t)
            for blk in range(6):
                pb, bb = nb()
                for c in range(8):
                    MM(pb[:, :], uT[s][:, c, :], wG[:, c, blk * 512:(blk + 1) * 512], c == 0, c == 7, [b_uT[s], b_wG[c]], [bb])
                gsl = gates[s][:, blk * 512:(blk + 1) * 512]
                TT("dve", gsl, pb[:, :], gbb[:, blk * 512:(blk + 1) * 512], ALU.add, [bb, b_gbb], [b_gates[s][blk]])
                ACT(gsl, gsl, AF.Sigmoid, [b_gates[s][blk]], [b_gates[s][blk]])
                yield

        def M(t):
            s = t % 2
            for hf in range(2):
                msl = merged[:, hf * 512:(hf + 1) * 512]
                for i in range(3):
                    pb, bb = nb()
                    for c in range(4):
                        MM(pb[:, :], yT[s][:, 4 * i + c, :], wB[:, 4 * i + c, hf * 512:(hf + 1) * 512], c == 0, c == 3, [b_yT[s][i], b_wB], [bb])
                    gsl = gates[s][:, i * 1024 + hf * 512:i * 1024 + (hf + 1) * 512]
                    gb_ = b_gates[s][2 * i + hf]
                    if i == 0:
                        TT("dve", msl, pb[:, :], gsl, ALU.mult, [bb, gb_], [b_mg[hf]])
                    else:
                        TT("dve", tmpm[:, :], pb[:, :], gsl, ALU.mult, [bb, gb_], [b_tmpm])
                        if i == 1:
                            TT("pool", msl, msl, tmpm[:, :], ALU.add, [b_mg[hf], b_tmpm], [b_mg[hf]])
                        else:
                            TT("pool", mbf[:, hf * 512:(hf + 1) * 512], msl, tmpm[:, :], ALU.add, [b_mg[hf], b_tmpm], [b_mbf[hf]])
                    yield
            pb, bb = nb()
            pbv = pb[:, :].bitcast(BF16)
            for c in range(8):
                TR(pbv[:, c * 128:(c + 1) * 128], mbf[:, c * 128:(c + 1) * 128], ident_bf[:, :], [b_mbf[c // 4], b_ibf], [bb])
            ACT(mT[:, :, :], pbv[:, :].rearrange("p (c t) -> p c t", c=8), AF.Copy, [bb], [b_mT])
            yield
            for hf in range(2):
                pb, bb = nb()
                for c in range(8):
                    MM(pb[:, :], mT[:, c, :], wO[:, c, hf * 512:(hf + 1) * 512], c == 0, c == 7, [b_mT, b_wO], [bb])
                ACT(junk[:, :], pb[:, :], AF.Square, [bb], [b_junk, b_ss2[hf]], accum=ss[:, 2 + hf:3 + hf])
                TT("dve", tmpf[:, hf * 512:(hf + 1) * 512], pb[:, :], gpost[:, hf * 512:(hf + 1) * 512], ALU.mult, [bb, b_gpost], [b_tmpf[hf]])
                yield
            TT("pool", ss[:, 5:6], ss[:, 2:3], ss[:, 3:4], ALU.add, [b_ss2[0], b_ss2[1]], [b_rs2])
            TS("pool", ss[:, 5:6], ss[:, 5:6], 1.0 / D, EPS, ALU.mult, ALU.add, [b_rs2], [b_rs2])
            TT("pool", ss[:, 5:6], ss[:, 5:6], C["neg_half"][:, 0:1], ALU.pow, [b_rs2, C["b_neg"]], [b_rs2])
            P.op("dve", lambda e: e.scalar_tensor_tensor(out=ht[s][:, :], in0=tmpf[:, :], scalar=ss[:, 5:6], in1=ht[s][:, :],
                                                         op0=ALU.mult, op1=ALU.add),
                 r=[b_tmpf[0], b_tmpf[1], b_rs2, b_ht[s]], w=[b_ht[s]])
            P.dma("sp", lambda e: e.dma_start(out=h2[t * 128:(t + 1) * 128, :], in_=ht[s][:, :]),
                  r=[b_ht[s]], w=[h2b], nowaw=True, semb=b_ht[s])
            yield

        def drain(gens):
            gens = [g for g in gens]
            while gens:
                nxt = []
                for g in gens:
                    try:
                        next(g)
                        nxt.append(g)
                    except StopIteration:
                        pass
                gens = nxt

        load(0)
        drain([G(0)])
        for t in range(n_tiles):
            if t + 1 < n_tiles:
                load(t + 1)
            drain([M(t)] + ([G(t + 1)] if t + 1 < n_tiles else []))

        with nc.Block() as block:
            P.emit(block)


_NC_CACHE = {}


def kernel(**inputs):
    if "nc" not in _NC_CACHE:
        _NC_CACHE["nc"] = build()
    nc = _NC_CACHE["nc"]
    maps = host_maps(inputs)
    res = run_bass_kernel_spmd(nc, maps, core_ids=list(range(8)))
    out = np.stack([np.asarray(r["out"], np.float32) for r in res.results], 0)
    return out
```
